# Optimizing a Trainium2 kernel written in Bass

```python
import functools
import jax, jax.numpy as jnp
from jax import lax
import numpy as np

D_MODEL = 1024
BATCH = 16
SEQ = 2048
DEPTH = 1
DEC_BATCH = 128
DEC_SEQ = 1
PAST_LEN = 8192
PAGE_SIZE = 128

MIX_WIDTH = D_MODEL
A_WIDTH = MIX_WIDTH // 2
A_GROUPS = 4
A_GROUP_DIM = A_WIDTH // A_GROUPS
CHUNK = 128
B_WIDTH = MIX_WIDTH - A_WIDTH
HEAD_DIM = 64
N_HEADS = B_WIDTH // HEAD_DIM
N_KV_HEADS = 2
GROUP_SIZE = N_HEADS // N_KV_HEADS
N_IDX_HEADS = 4
IDX_DIM = 64
IDX_W_SCALE = (N_IDX_HEADS * IDX_DIM) ** -0.5
TOPK_MAX = 256
Q_BLOCK = 128
ROPE_THETA = 10000.0
D_FF = 2816
LN_EPS = 1e-5
ALPHA = (2.0 * DEPTH) ** 0.25
BETA = (8.0 * DEPTH) ** -0.25
IN_SPLITS = (A_WIDTH, A_WIDTH, N_HEADS * HEAD_DIM, N_KV_HEADS * HEAD_DIM, N_KV_HEADS * HEAD_DIM, N_IDX_HEADS * IDX_DIM, IDX_DIM, N_IDX_HEADS)
IN_WIDTH = sum(IN_SPLITS)
SPLIT_POINTS = tuple(sum(IN_SPLITS[:i + 1]) for i in range(len(IN_SPLITS) - 1))

kernel_name = 'hymba_gmlp_dsa_macaron_deepnorm_step'


def _layernorm(x, g, b):
    xf = x.astype(jnp.float32)
    mu = jnp.mean(xf, axis=-1, keepdims=True)
    var = jnp.mean(jnp.square(xf - mu), axis=-1, keepdims=True)
    return ((xf - mu) * lax.rsqrt(var + LN_EPS) * g.astype(jnp.float32) + b.astype(jnp.float32)).astype(x.dtype)


def _swiglu(x, w_up, w_down):
    gate, up = jnp.split(x @ w_up, 2, axis=-1)
    return (jax.nn.silu(gate) * up) @ w_down


def _rope(x, pos):
    half = x.shape[-1] // 2
    inv = ROPE_THETA ** (-jnp.arange(half, dtype=jnp.float32) / half)
    ang = pos.astype(jnp.float32)[:, None] * inv[None, :]
    cos = jnp.cos(ang)[None, :, None, :]
    sin = jnp.sin(ang)[None, :, None, :]
    xf = x.astype(jnp.float32)
    x1, x2 = xf[..., :half], xf[..., half:]
    return jnp.concatenate([x1 * cos - x2 * sin, x2 * cos + x1 * sin], axis=-1).astype(x.dtype)


def _mixer_inputs(h, w_in, pos):
    B, T, _ = h.shape
    u, va, q, k, v, qi, ki, wi = jnp.split(h @ w_in, SPLIT_POINTS, axis=-1)
    q = _rope(q.reshape(B, T, N_HEADS, HEAD_DIM), pos)
    k = _rope(k.reshape(B, T, N_KV_HEADS, HEAD_DIM), pos)
    v = v.reshape(B, T, N_KV_HEADS, HEAD_DIM)
    qi = _rope(qi.reshape(B, T, N_IDX_HEADS, IDX_DIM), pos)
    ki = _rope(ki.reshape(B, T, 1, IDX_DIM), pos)[:, :, 0]
    return u, va, q, k, v, qi, ki, wi * IDX_W_SCALE


def _chunk_gmlp(u, v, a_ln_g, a_ln_b, a_ws, a_bs):
    B, T, _ = u.shape
    u = jax.nn.gelu(u)
    v = _layernorm(jax.nn.gelu(v), a_ln_g, a_ln_b)
    n_chunks = -(-T // CHUNK)
    pad = n_chunks * CHUNK - T
    vp = jnp.pad(v, ((0, 0), (0, pad), (0, 0))).reshape(B, n_chunks, CHUNK, A_GROUPS, A_GROUP_DIM)
    causal = jnp.tril(jnp.ones((CHUNK, CHUNK), dtype=bool))
    ws = jnp.where(causal[None], a_ws, 0)
    mixed = jnp.einsum('gts,bcsgd->bctgd', ws, vp) + a_bs.T[None, None, :, :, None]
    mixed = mixed.reshape(B, n_chunks * CHUNK, A_WIDTH)[:, :T]
    return u * mixed, v


def _indexer_scores(q_idx, w_idx, k_idx, q_pos):
    dots = jnp.einsum('bthd,bsd->bths', q_idx.astype(jnp.float32), k_idx.astype(jnp.float32))
    s = jnp.einsum('bth,bths->bts', w_idx.astype(jnp.float32), jax.nn.relu(dots))
    visible = jnp.arange(k_idx.shape[1])[None, :] <= q_pos[:, None]
    return jnp.where(visible[None], s, -jnp.inf)


def _select(scores, q_pos, topk):
    _, idx = lax.top_k(scores, topk)
    return idx, idx <= q_pos[None, :, None]


def _sparse_attend(q, k_sel, v_sel, valid):
    B, T = q.shape[:2]
    qg = q.reshape(B, T, N_KV_HEADS, GROUP_SIZE, HEAD_DIM).astype(jnp.float32)
    logits = jnp.einsum('btgrd,btkgd->btgrk', qg, k_sel.astype(jnp.float32)) * (HEAD_DIM ** -0.5)
    logits = jnp.where(valid[:, :, None, None, :], logits, -jnp.inf)
    p = jax.nn.softmax(logits, axis=-1)
    out = jnp.einsum('btgrk,btkgd->btgrd', p, v_sel.astype(jnp.float32))
    return out.reshape(B, T, N_HEADS * HEAD_DIM).astype(q.dtype)


def _prompt_attention(q, k, v, q_idx, k_idx, w_idx, pos):
    B, S = q.shape[:2]
    topk = min(TOPK_MAX, S // 4)
    nb = S // Q_BLOCK
    b_ix = jnp.arange(B)[:, None, None]

    def blockify(t):
        return jnp.moveaxis(t.reshape((B, nb, Q_BLOCK) + t.shape[2:]), 1, 0)

    def one_block(args):
        qb, qib, wb, pb = args
        idx, valid = _select(_indexer_scores(qib, wb, k_idx, pb), pb, topk)
        return _sparse_attend(qb, k[b_ix, idx], v[b_ix, idx], valid)

    out = lax.map(one_block, (blockify(q), blockify(q_idx), blockify(w_idx), pos.reshape(nb, Q_BLOCK)))
    return jnp.moveaxis(out, 0, 1).reshape(B, S, N_HEADS * HEAD_DIM)


def _sample_attention(q, k_new, v_new, q_idx, k_idx_new, w_idx, pos, cache_k, cache_v, cache_kidx, page_table):
    DB, T = q.shape[:2]
    past = page_table.shape[1] * PAGE_SIZE
    topk = min(TOPK_MAX, (past + T) // 4)
    kidx_past = cache_kidx[page_table].reshape(DB, past, IDX_DIM)
    kidx_all = jnp.concatenate([kidx_past, k_idx_new], axis=1)
    idx, valid = _select(_indexer_scores(q_idx, w_idx, kidx_all, pos), pos, topk)
    b_ix = jnp.arange(DB)[:, None, None]
    in_past = (idx < past)[..., None, None]
    p_pos = jnp.minimum(idx, past - 1)
    phys = page_table[b_ix, p_pos // PAGE_SIZE]
    slot = p_pos % PAGE_SIZE
    n_pos = jnp.clip(idx - past, 0, T - 1)
    k_sel = jnp.where(in_past, cache_k[phys, slot], k_new[b_ix, n_pos])
    v_sel = jnp.where(in_past, cache_v[phys, slot], v_new[b_ix, n_pos])
    return _sparse_attend(q, k_sel, v_sel, valid)


def _layer(x, pos, attend, ln1_g, ln1_b, ffn1_w_up, ffn1_w_down, ln2_g, ln2_b, w_in, a_ln_g, a_ln_b, a_ws, a_bs, w_out, ln3_g, ln3_b, ffn2_w_up, ffn2_w_down):
    x = _layernorm(ALPHA * x + 0.5 * _swiglu(x, ffn1_w_up, ffn1_w_down), ln1_g, ln1_b)
    u, va, q, k, v, qi, ki, wi = _mixer_inputs(x, w_in, pos)
    a_out, a_v = _chunk_gmlp(u, va, a_ln_g, a_ln_b, a_ws, a_bs)
    b_out = attend(q, k, v, qi, ki, wi, pos)
    mix = jnp.concatenate([a_out, b_out], axis=-1) @ w_out
    x = _layernorm(ALPHA * x + mix, ln2_g, ln2_b)
    x = _layernorm(ALPHA * x + 0.5 * _swiglu(x, ffn2_w_up, ffn2_w_down), ln3_g, ln3_b)
    return x, k, v, ki, a_v


def setup_inputs(seed: int = 0) -> dict:
    key = jax.random.key(seed)
    ks = jax.random.split(key, 26)
    n_pages = PAST_LEN // PAGE_SIZE
    n_pool = (5 * DEC_BATCH * n_pages) // 4
    f32 = jnp.float32

    def nrm(k, shape, scale=1.0):
        return jax.random.normal(k, shape, f32) * scale

    perm = jax.random.permutation(ks[5], n_pool)[:DEC_BATCH * n_pages]
    return {
        'x_prompt': nrm(ks[0], (BATCH, SEQ, D_MODEL)),
        'x_sample': nrm(ks[1], (DEC_BATCH, DEC_SEQ, D_MODEL)),
        'cache_k': nrm(ks[2], (DEPTH, n_pool, PAGE_SIZE, N_KV_HEADS, HEAD_DIM)),
        'cache_v': nrm(ks[3], (DEPTH, n_pool, PAGE_SIZE, N_KV_HEADS, HEAD_DIM)),
        'cache_kidx': nrm(ks[4], (DEPTH, n_pool, PAGE_SIZE, IDX_DIM)),
        'page_table': perm.reshape(DEC_BATCH, n_pages).astype(jnp.int32),
        'ln1_g': 1.0 + nrm(ks[6], (DEPTH, D_MODEL), 0.02),
        'ln1_b': nrm(ks[7], (DEPTH, D_MODEL), 0.02),
        'ffn1_w_up': nrm(ks[8], (DEPTH, D_MODEL, 2 * D_FF), D_MODEL ** -0.5),
        'ffn1_w_down': nrm(ks[9], (DEPTH, D_FF, D_MODEL), BETA * D_FF ** -0.5),
        'ln2_g': 1.0 + nrm(ks[10], (DEPTH, D_MODEL), 0.02),
        'ln2_b': nrm(ks[11], (DEPTH, D_MODEL), 0.02),
        'w_in': nrm(ks[12], (DEPTH, D_MODEL, IN_WIDTH), D_MODEL ** -0.5),
        'a_ln_g': 1.0 + nrm(ks[13], (DEPTH, A_WIDTH), 0.02),
        'a_ln_b': nrm(ks[14], (DEPTH, A_WIDTH), 0.02),
        'a_ws': nrm(ks[15], (DEPTH, A_GROUPS, CHUNK, CHUNK), CHUNK ** -0.5),
        'a_bs': 1.0 + nrm(ks[16], (DEPTH, A_GROUPS, CHUNK), 0.02),
        'w_out': nrm(ks[17], (DEPTH, MIX_WIDTH, D_MODEL), BETA * MIX_WIDTH ** -0.5),
        'ln3_g': 1.0 + nrm(ks[18], (DEPTH, D_MODEL), 0.02),
        'ln3_b': nrm(ks[19], (DEPTH, D_MODEL), 0.02),
        'ffn2_w_up': nrm(ks[20], (DEPTH, D_MODEL, 2 * D_FF), D_MODEL ** -0.5),
        'ffn2_w_down': nrm(ks[21], (DEPTH, D_FF, D_MODEL), BETA * D_FF ** -0.5),
    }


def reference(x_prompt, x_sample, cache_k, cache_v, cache_kidx, page_table, ln1_g, ln1_b, ffn1_w_up, ffn1_w_down, ln2_g, ln2_b, w_in, a_ln_g, a_ln_b, a_ws, a_bs, w_out, ln3_g, ln3_b, ffn2_w_up, ffn2_w_down):
    pos_p = jnp.arange(x_prompt.shape[1], dtype=jnp.int32)
    past = page_table.shape[1] * PAGE_SIZE
    pos_s = past + jnp.arange(x_sample.shape[1], dtype=jnp.int32)
    weights = (ln1_g, ln1_b, ffn1_w_up, ffn1_w_down, ln2_g, ln2_b, w_in, a_ln_g, a_ln_b, a_ws, a_bs, w_out, ln3_g, ln3_b, ffn2_w_up, ffn2_w_down)
    yp, ys = x_prompt, x_sample
    kp, vp, kip, ksm, vsm, kis, avs = [], [], [], [], [], [], []
    for l in range(DEPTH):
        lw = [w[l] for w in weights]
        yp, k1, v1, ki1, _ = _layer(yp, pos_p, _prompt_attention, *lw)
        attend_s = functools.partial(_sample_attention, cache_k=cache_k[l], cache_v=cache_v[l], cache_kidx=cache_kidx[l], page_table=page_table)
        ys, k2, v2, ki2, av2 = _layer(ys, pos_s, attend_s, *lw)
        kp.append(k1); vp.append(v1); kip.append(ki1)
        ksm.append(k2); vsm.append(v2); kis.append(ki2); avs.append(av2)
    return (yp, ys, jnp.stack(kp), jnp.stack(vp), jnp.stack(kip), jnp.stack(ksm), jnp.stack(vsm), jnp.stack(kis), jnp.stack(avs))
```

```python
import numpy as np
from contextlib import ExitStack
import concourse.bass as bass
import concourse.mybir as mybir
from concourse.bass_utils import run_bass_kernel_spmd

F32 = mybir.dt.float32
BF16 = mybir.dt.bfloat16
I32 = mybir.dt.int32
AF = mybir.ActivationFunctionType
ALU = mybir.AluOpType
AX = mybir.AxisListType

D = 1024
FF = 2816
NFC = 22
SLICES = [(0, 6), (6, 6), (12, 5), (17, 5)]
CPS = 6
WG = 3
INW = 2116
SEQ = 2048
NSEQ = 2
NSMP = 16
ALPHA = 2.0 ** 0.25
EPS = 1e-5
IDX_W_SCALE = 256.0 ** -0.5
TOPK = 256
NBIS = 22
BIG = 1.0e30
GT = 8
NPAGE = 64


class Op:
    __slots__ = ("eng", "fn", "reads", "writes", "dma", "deps", "signal", "sem", "val", "idx", "prev_val", "raw")


class Prog:
    ENGS = ("pe", "act", "dve", "pool", "sp")

    def __init__(self):
        self.ops = []
        self.last_w = {}
        self.readers = {}

    def op(self, eng, fn, reads=(), writes=(), dma=False):
        o = Op()
        o.eng, o.fn, o.dma = eng, fn, dma
        o.reads, o.writes = tuple(reads), tuple(writes)
        o.deps = set()
        o.raw = set()
        o.signal = dma
        o.idx = len(self.ops)
        for k in o.reads:
            w = self.last_w.get(k)
            if w is not None:
                o.deps.add(w)
                o.raw.add(w)
        for k in o.writes:
            w = self.last_w.get(k)
            if w is not None:
                o.deps.add(w)
            for r in self.readers.get(k, ()):
                o.deps.add(r)
        for k in o.reads:
            self.readers.setdefault(k, []).append(o.idx)
        for k in o.writes:
            self.last_w[k] = o.idx
            self.readers[k] = []
        o.deps.discard(o.idx)
        self.ops.append(o)
        return o

    def emit(self, nc, es, out_dma_ops):
        ops = self.ops
        for o in ops:
            nd = set()
            for d in o.deps:
                p = ops[d]
                if (not p.dma) and (not o.dma) and p.eng == o.eng:
                    if o.eng == "pe" or d not in o.raw:
                        continue
                nd.add(d)
                p.signal = True
            o.deps = nd
        esem = {e: es.enter_context(nc.semaphore("S_" + e)) for e in self.ENGS}
        NDS = 12
        dsem = {e: [es.enter_context(nc.semaphore("D_%s_%d" % (e, i))) for i in range(NDS)]
                for e in ("sp", "pool", "act")}
        ecount = {e: 0 for e in self.ENGS}
        dcount = {e: [0] * NDS for e in dsem}
        dnext = {e: 0 for e in dsem}
        for o in ops:
            if o.dma:
                j = dnext[o.eng]
                dnext[o.eng] = (j + 1) % NDS
                o.prev_val = dcount[o.eng][j]
                dcount[o.eng][j] += 16
                o.sem, o.val = dsem[o.eng][j], dcount[o.eng][j]
            elif o.signal:
                ecount[o.eng] += 1
                o.sem, o.val = esem[o.eng], ecount[o.eng]
        by_eng = {e: [o for o in ops if o.eng == e] for e in self.ENGS}
        final_waits = [(o.sem, o.val) for o in ops if o.dma]
        block = es.enter_context(nc.Block())

        def run(engname, eng):
            waited = {}

            def wait(sem, val):
                key = id(sem)
                if waited.get(key, 0) >= val:
                    return
                waited[key] = val
                eng.wait_ge(sem, val)

            for o in by_eng[engname]:
                for d in sorted(o.deps):
                    p = ops[d]
                    wait(p.sem, p.val)
                if o.dma and o.prev_val > 0:
                    wait(o.sem, o.prev_val)
                ins = o.fn(eng)
                if o.dma:
                    ins.then_inc(o.sem, 16)
                elif o.signal:
                    ins.then_inc(o.sem, 1)
            if engname == "sp":
                fw = {}
                for sem, val in final_waits:
                    fw[id(sem)] = (sem, max(val, fw.get(id(sem), (None, 0))[1]))
                for sem, val in fw.values():
                    eng.wait_ge(sem, val)

        @block.tensor
        def _(e):
            run("pe", e)

        @block.scalar
        def _(e):
            run("act", e)

        @block.vector
        def _(e):
            run("dve", e)

        @block.gpsimd
        def _(e):
            run("pool", e)

        @block.sync
        def _(e):
            run("sp", e)


def build_nc(debug=False, npool=10240, ngroups_run=None):
    nc = bass.Bass("TRN2", target_bir_lowering=False)
    P = Prog()
    es = ExitStack()

    def din(name, shape, dt=F32):
        return nc.dram_tensor(name, list(shape), dt, kind="ExternalInput").ap()

    def dout(name, shape, dt=F32):
        return nc.dram_tensor(name, list(shape), dt, kind="ExternalOutput").ap()

    NTOK = NSEQ * SEQ
    x_prompt = din("x_prompt", [NTOK, D])
    x_sample = din("x_sample", [NSMP, D])
    cache_k = din("cache_k", [npool * 8, 2048])
    cache_v = din("cache_v", [npool * 8, 2048])
    cache_kidx = din("cache_kidx", [npool * 4, 2048])
    page_table = din("page_table", [NSMP * NPAGE, 1], I32)
    lnp = {n: din(n, [1, D]) for n in ("ln1_g", "ln1_b", "ln2_g", "ln2_b", "ln3_g", "ln3_b")}
    a_ln_g = din("a_ln_g", [1, 512])
    a_ln_b = din("a_ln_b", [1, 512])
    w_up_d = [din("ffn1_w_up", [D, 2 * FF]), din("ffn2_w_up", [D, 2 * FF])]
    w_dn_d = [din("ffn1_w_down", [FF, D]), din("ffn2_w_down", [FF, D])]
    w_in_d = din("w_in", [D, INW])
    w_out_d = din("w_out", [D, D])
    a_ws = din("a_ws", [4, 128, 128])
    a_bs = din("a_bs", [4, 128])
    c_invf = din("c_invf", [128, 32])
    c_bo = din("c_bo", [128, 128])
    c_aall = din("c_aall", [16, 128])
    c_rep = din("c_rep", [16, 128])
    c_oh = din("c_oh", [16, 8])
    c_negfill = din("c_negfill", [128, 1])
    c_selp = din("c_selp", [16, 8 * 16])
    c_hm = din("c_hm", [16, 10])

    y_prompt = dout("y_prompt", [NTOK, D])
    y_sample = dout("y_sample", [NSMP, D])
    nk_p = dout("nk_p", [NTOK, 128])
    nv_p = dout("nv_p", [NTOK, 128])
    nki_p = dout("nki_p", [NTOK, 64])
    nk_s = dout("nk_s", [NSMP, 128])
    nv_s = dout("nv_s", [NSMP, 128])
    nki_s = dout("nki_s", [NSMP, 64])
    av_s = dout("av_s", [NSMP, 512])
    if debug:
        dbg_tab = dout("dbg_tab", [128, 4 * 512 + 128])
        dbg_ln1 = dout("dbg_ln1", [128, (GT + 1) * D])
        dbg_ln2 = dout("dbg_ln2", [128, (GT + 1) * D])
        dbg_mix = dout("dbg_mix", [128, (GT + 1) * D], BF16)
        dbg_sc = dout("dbg_sc", [128, (GT + 1) * SEQ])
        dbg_msk = dout("dbg_msk", [128, (GT + 1) * SEQ], BF16)

    def sb(name, shape, dt=F32):
        return es.enter_context(nc.sbuf_tensor(name, list(shape), dt))

    psf = [es.enter_context(nc.psum_tensor("ps%d" % i, [128, 512], F32)) for i in range(8)]

    def psb(i):
        return psf[i][:].bitcast(BF16)

    def pk(i):
        return "ps%d" % i

    NT = GT + 1
    NTK = GT * 128 + NSMP
    x_res = sb("x_res", [128, NT, D])
    xT = sb("xT", [128, 8, NTK], BF16)
    R_bytes = max(CPS * NTK * 2 + CPS * D * 2 + 2 * 8 * 2 * WG * 128 * 2, 8 * INW * 2 + 8 * D * 2)
    Rt = sb("Rregion", [128, R_bytes // 4], F32)
    Rb = Rt[:].bitcast(BF16)
    o0 = 0
    hT = Rb[:, o0:o0 + CPS * NTK].rearrange("p (c n) -> p c n", c=CPS)
    o0 += CPS * NTK
    wd = Rb[:, o0:o0 + CPS * D].rearrange("p (c n) -> p c n", c=CPS)
    o0 += CPS * D
    WUE = 8 * 2 * WG * 128
    wu = [Rb[:, o0 + i * WUE:o0 + (i + 1) * WUE].rearrange("p (k t n) -> p k t n", k=8, t=2) for i in range(2)]
    w_in = Rb[:, 0:8 * INW].rearrange("p (k n) -> p k n", k=8)
    w_out = Rb[:, 8 * INW:8 * INW + 8 * D].rearrange("p (k n) -> p k n", k=8)
    RKEYS = ["hT", "wd", "wu0", "wu1"]

    gcur = sb("gcur", [128, D])
    bcur = sb("bcur", [128, D])
    agB = sb("agB", [128, 512])
    abB = sb("abB", [128, 512])
    ident = sb("ident", [128, 128], BF16)
    identf = sb("identf", [128, 128])
    wsT = sb("wsT", [128, 4, 128], BF16)
    wsf = sb("wsf", [128, 4, 128])
    wsb = sb("wsb", [128, 4, 128], BF16)
    bsT = sb("bsT", [128, 4])
    ws00 = sb("ws00", [16, 4])
    bs0 = sb("bs0", [16, 4])
    cosT = sb("cosT", [128, 16, 32])
    sinT = sb("sinT", [128, 16, 32])
    cosS = sb("cosS", [128, 1, 32])
    sinS = sb("sinS", [128, 1, 32])
    invf = sb("invf", [128, 32])
    kT = sb("kT", [128, SEQ], BF16)
    kidxT = sb("kidxT", [64, SEQ], BF16)
    vaug = sb("vaug", [128, 16, 2, 65], BF16)
    ybf = sb("ybf", [128, D], BF16)
    stats_ = [sb("stats%d" % i, [128, 12]) for i in range(4)]
    mv_ = [sb("mv%d" % i, [128, 2]) for i in range(4)]
    rstd_ = [sb("rstd%d" % i, [128, 1]) for i in range(4)]
    nmr_ = [sb("nmr%d" % i, [128, 1]) for i in range(4)]
    ln_ctr = [0]
    u_sb = sb("u_sb", [128, 512])
    vg = sb("vg", [128, 512])
    sg = [u_sb, vg]
    vln = sb("vln", [128, 512])
    vbf = sb("vbf", [128, 512], BF16)
    mix = sb("mix", [128, D], BF16)
    mixT = ybf[:, :].rearrange("p (k n) -> p k n", k=8)
    rq = sb("rq", [128, 10, 64], BF16)
    rqf = sb("rqf", [128, 10, 64])
    ri = sb("ri", [128, 5, 64], BF16)
    rif = sb("rif", [128, 5, 64])
    rt1 = sb("rt1", [128, 10, 32])
    rt2 = sb("rt2", [128, 10, 32])
    vf = sb("vf", [128, 128])
    wq = sb("wq", [128, 4])
    qT = sb("qT", [128, 4, 128], BF16)
    qiT = sb("qiT", [64, 4, 128], BF16)
    sc = sb("sc", [128, SEQ])
    rl = sb("rl", [128, 512])
    msk = sb("msk", [128, SEQ], BF16)
    junk = msk
    maskT = sb("maskT", [128, 16, 128], BF16)
    pT = sb("pT", [128, 16, 256], BF16)
    PTK = ["pT"] + ["pT%d" % i for i in range(16)]
    ebuf = [sb("ebuf%d" % i, [128, 256], BF16) for i in range(2)]
    lo = sb("lo", [128, 8])
    Wt = sb("Wt", [128, NBIS + 2])
    wdt = sb("wdt", [128, 8])
    w0 = sb("w0", [128, 8])
    mid = sb("mid", [128, 8])
    cnt = sb("cnt", [128, 8])
    tsel = sb("tsel", [128, 8])
    rinv = sb("rinv", [128, 8])
    bo = sb("bo", [128, 128])
    aall = sb("aall", [16, 128])
    rep = sb("rep", [16, 128])
    wqm = sb("wqm", [16, 4])
    vfm = sb("vfm", [16, 128])
    oh = sb("oh", [16, 8])
    negfill = sb("negfill", [128, 1])
    selp = sb("selp", [16, 8, 16], BF16)
    selpf = sb("selpf", [16, 8, 16])
    hm = sb("hm", [16, 10])
    onesf = sb("onesf", [128, 128])
    ptab = sb("ptab", [128, 8], I32)
    idx4 = sb("idx4", [128, 8, 4], I32)
    idx8 = sb("idx8", [128, 8, 8], I32)
    qiTs = sb("qiTs", [128, 8, 2, 4, 2], BF16)
    riZ = sb("riZ", [16, 2, 4, 128], BF16)
    wB = sb("wB", [128, 8, 4])
    ssd = sb("ssd", [16, 4, 64])
    ss4 = sb("ss4", [16, 4])
    ss1 = sb("ss1", [16, 1])
    ssb = sb("ssb", [16, 8])
    SC = sb("SC", [128, 8, 129])
    mskS = sb("mskS", [128, 8, 129], BF16)
    cmpS = sc[:, 0:8 * 129].rearrange("p (a b) -> p a b", a=8)
    Mx = sb("Mx", [128, 8])
    MxT = sb("MxT", [8, 1])
    Mdiag = sb("Mdiag", [8, 8])
    QZ = sb("QZ", [16, 8, 128], BF16)
    Qblk = sb("Qblk", [128, 8, 8, 2], BF16)
    kTs = sb("kTs", [128, 16], BF16)
    KTself = sb("KTself", [128, 8, 128], BF16)
    Vself = sb("Vself", [128, 8, 128], BF16)
    vsbf = sb("vsbf", [16, 128], BF16)
    Ps = sb("Ps", [128, 33, 2, 8], BF16)
    Pacc = sb("Pacc", [128, 16])
    Ptmp = sb("Ptmp", [128, 16])
    den = sb("den", [16, 1])
    Osel = sb("Osel", [16, 64])
    Oexp = sb("Oexp", [16, 8, 64], BF16)
    T4 = rl[:, :].rearrange("p (s c) -> p s c", c=4)
    KIc = pT[:, 0:8, :].rearrange("p a b -> p (a b)").rearrange("p (s d) -> p s d", d=64)
    Kc = pT[:, 8:16, :].rearrange("p a b -> p (a b)").rearrange("p (s d) -> p s d", d=128)
    Vc = sc[:, 0:1024].bitcast(BF16).rearrange("p (s d) -> p s d", d=128)
    kiTc = msk[:, 0:1024].rearrange("p (a b) -> p a b", a=8)
    KTc = msk[:, 1024:2048].rearrange("p (a b) -> p a b", a=8)

    def DMA(q, out, in_, reads, writes, **kw):
        P.op(q, lambda e, out=out, in_=in_, kw=kw: e.dma_start(out=out, in_=in_, **kw),
             reads=reads, writes=writes, dma=True)

    def MM(out, lhsT, rhs, start, stop, reads, writes):
        P.op("pe", lambda e: e.matmul(out, lhsT=lhsT, rhs=rhs, start=start, stop=stop),
             reads=reads, writes=writes)

    def TR(out, in_, idn, reads, writes):
        P.op("pe", lambda e: e.transpose(out=out, in_=in_, identity=idn), reads=reads, writes=writes)

    def ACT(out, in_, func, reads, writes, bias=None, scale=None):
        kw = {}
        if bias is not None:
            kw["bias"] = bias
        if scale is not None:
            kw["scale"] = scale
        P.op("act", lambda e: e.activation(out=out, in_=in_, func=func, **kw), reads=reads, writes=writes)

    def TS(out, in0, s1, s2, op0, op1, reads, writes, accum_out=None, eng="dve"):
        kw = {}
        if op1 is not None:
            kw["op1"] = op1
        if accum_out is not None:
            kw["accum_out"] = accum_out
        P.op(eng, lambda e: e.tensor_scalar(out=out, in0=in0, scalar1=s1, scalar2=s2, op0=op0, **kw),
             reads=reads, writes=writes)

    def TT(out, in0, in1, op, reads, writes, eng="dve"):
        P.op(eng, lambda e: e.tensor_tensor(out=out, in0=in0, in1=in1, op=op), reads=reads, writes=writes)

    def STT(out, in0, scalar, in1, op0, op1, reads, writes):
        P.op("dve", lambda e: e.scalar_tensor_tensor(out=out, in0=in0, scalar=scalar, in1=in1, op0=op0, op1=op1),
             reads=reads, writes=writes)

    def CP(out, in_, reads, writes, eng="dve"):
        P.op(eng, lambda e: e.tensor_copy(out=out, in_=in_), reads=reads, writes=writes)

    def RED(out, in_, op, reads, writes, axis=AX.X, absval=None):
        P.op("dve", lambda e: e.tensor_reduce(out=out, in_=in_, axis=axis, op=op, apply_absolute_value=absval),
             reads=reads, writes=writes)

    def MEMSET(ap, val, writes, eng="dve"):
        P.op(eng, lambda e: e.memset(ap, val), writes=writes)

    def bc(ap, shape):
        return ap.to_broadcast(list(shape))

    def load_ln(gn, bn):
        DMA("sp", gcur[:], lnp[gn].partition_broadcast(128), [], ["gcur"])
        DMA("sp", bcur[:], lnp[bn].partition_broadcast(128), [], ["bcur"])
    DMA("sp", agB[:], a_ln_g.partition_broadcast(128), [], ["agB"])
    DMA("sp", abB[:], a_ln_b.partition_broadcast(128), [], ["abB"])
    DMA("sp", invf[:], c_invf, [], ["invf"])
    DMA("sp", wsf[:], a_ws.rearrange("g t s -> t g s"), [], ["wsf"])
    DMA("sp", bsT[:], a_bs.rearrange("g t -> t g"), [], ["bsT"], allow_slow_non_contiguous=True)
    DMA("sp", ws00[:], a_ws[:, 0, 0:1].rearrange("g o -> o g").partition_broadcast(16), [], ["ws00"],
        allow_slow_non_contiguous=True)
    DMA("sp", bs0[:], a_bs[:, 0:1].rearrange("g o -> o g").partition_broadcast(16), [], ["bs0"],
        allow_slow_non_contiguous=True)
    DMA("sp", bo[:], c_bo, [], ["bo"])
    DMA("sp", aall[:], c_aall, [], ["aall"])
    DMA("sp", rep[:], c_rep, [], ["rep"])
    DMA("sp", oh[:], c_oh, [], ["oh"])
    DMA("sp", negfill[:], c_negfill, [], ["negfill"])
    DMA("sp", selpf[:].rearrange("p a b -> p (a b)"), c_selp, [], ["selpf"])
    DMA("sp", hm[:], c_hm, [], ["hm"])
    DMA("sp", ptab[:], page_table.rearrange("(i p) o -> p (i o)", p=128), [], ["ptab"],
        allow_slow_non_contiguous=True)
    CP(selp[:], selpf[:], ["selpf"], ["selp"])
    for c in range(4):
        TS(idx4[:, :, c], ptab[:, :], 4.0, float(c), ALU.mult, ALU.add, ["ptab"], ["idx4"])
    for c in range(8):
        TS(idx8[:, :, c], ptab[:, :], 8.0, float(c), ALU.mult, ALU.add, ["ptab"], ["idx8"])
    MEMSET(onesf[:], 1.0, ["onesf"])
    MEMSET(identf[:], 0.0, ["identf"], eng="pool")
    P.op("pool", lambda e: e.affine_select(out=identf[:], in_=identf[:], pattern=[[-1, 128]], compare_op=ALU.not_equal,
                                           fill=1.0, base=0, channel_multiplier=1), reads=["identf"], writes=["identf"])
    CP(ident[:], identf[:], ["identf"], ["ident"], eng="pool")
    for g in range(4):
        P.op("pool", lambda e, g=g: e.affine_select(out=wsf[:, g, :], in_=wsf[:, g, :], pattern=[[-1, 128]],
                                                    compare_op=ALU.is_ge, fill=0.0, base=0, channel_multiplier=1),
             reads=["wsf"], writes=["wsf"])
    CP(wsb[:], wsf[:], ["wsf"], ["wsb"], eng="pool")
    for g in range(4):
        TR(psb(7)[:, g * 128:(g + 1) * 128], wsb[:, g, :], ident[:], ["wsb", "ident"], [pk(7)])
    CP(wsT[:].rearrange("p g t -> p (g t)"), psb(7)[:, 0:512], [pk(7)], ["wsT"])

    posi = sb("posi", [128, 16], I32)
    posf = sb("posf", [128, 16])
    _pad_top = sb("pad_top", [128, 64])
    ang = sc[:, 0:512].rearrange("p (a b) -> p a b", a=16)
    kq = sc[:, 512:1024].rearrange("p (a b) -> p a b", a=16)
    rr = sc[:, 1024:1536].rearrange("p (a b) -> p a b", a=16)
    m1 = sc[:, 1536:2048].rearrange("p (a b) -> p a b", a=16)
    kqi = msk[:, 0:1024].bitcast(I32).rearrange("p (a b) -> p a b", a=16)
    TWO_PI = 2.0 * np.pi
    C1 = 6.28125
    C2 = TWO_PI - C1

    def sincos(posf_ap, nj, sin_out, cos_out, tag):
        a = ang[:, 0:nj, :]
        k_ = kq[:, 0:nj, :]
        ki_ = kqi[:, 0:nj, :]
        r_ = rr[:, 0:nj, :]
        m_ = m1[:, 0:nj, :]
        TT(a, bc(posf_ap.unsqueeze(2), [128, nj, 32]), bc(invf[:].unsqueeze(1), [128, nj, 32]), ALU.mult,
           ["posf", "invf"], ["sc"])
        if debug and nj == 16:
            dbg_ang = dout("dbg_ang", [128, 512])
            DMA("sp", dbg_ang[:, :], sc[:, 0:512], ["sc"], [])
        TS(k_, a, 1.0 / TWO_PI, None, ALU.mult, None, ["sc"], ["sc"])
        CP(ki_, k_, ["sc"], ["msk"])
        CP(k_, ki_, ["msk"], ["sc"])
        STT(r_, k_, -C1, a, ALU.mult, ALU.add, ["sc", "sc"], ["sc"])
        STT(r_, k_, -C2, r_, ALU.mult, ALU.add, ["sc", "sc"], ["sc"])

        def wrap():
            TS(m_, r_, np.pi, -TWO_PI, ALU.is_gt, ALU.mult, ["sc"], ["sc"])
            TT(r_, r_, m_, ALU.add, ["sc", "sc"], ["sc"])
            TS(m_, r_, -np.pi, TWO_PI, ALU.is_lt, ALU.mult, ["sc"], ["sc"])
            TT(r_, r_, m_, ALU.add, ["sc", "sc"], ["sc"])
            TS(r_, r_, np.pi, -np.pi, ALU.min, ALU.max, ["sc"], ["sc"])
        wrap()
        ACT(sin_out, r_, AF.Sin, ["sc"], [tag + "sin"])
        TS(r_, r_, np.pi / 2, None, ALU.add, None, ["sc"], ["sc"])
        wrap()
        ACT(cos_out, r_, AF.Sin, ["sc"], [tag + "cos"])

    P.op("pool", lambda e: e.iota(posi[:], pattern=[[128, 16]], base=0, channel_multiplier=1), writes=["posi"])
    CP(posf[:], posi[:], ["posi"], ["posf"])
    if debug:
        dbg_small = dout("dbg_small", [128, 16])
        DMA("sp", dbg_small[:, :], posf[:], ["posf"], [])
    sincos(posf[:], 16, sinT[:], cosT[:], "T")
    MEMSET(posf[:, 0:1], 8192.0, ["posf"])
    sincos(posf[:, 0:1], 1, sinS[:], cosS[:], "S")

    def layernorm(zin, rows, width, gtile, btile, yout, zkeys, ykeys, gk, bk):
        par = ln_ctr[0] % 4
        ln_ctr[0] += 1
        stats, mv, rstd, nmr = stats_[par], mv_[par], rstd_[par], nmr_[par]
        ks, km, kr, kn = "stats%d" % par, "mv%d" % par, "rstd%d" % par, "nmr%d" % par
        nchunk = width // 512
        for c in range(nchunk):
            P.op("dve", lambda e, c=c: e.bn_stats(out=stats[:rows, c * 6:(c + 1) * 6], in_=zin[:, c * 512:(c + 1) * 512]),
                 reads=zkeys, writes=[ks])
        P.op("dve", lambda e: e.bn_aggr(out=mv[:rows, :], in_=stats[:rows, 0:6 * nchunk]), reads=[ks], writes=[km])
        TS(rstd[:rows, :], mv[:rows, 1:2], EPS, None, ALU.add, None, [km], [kr])
        ACT(rstd[:rows, :], rstd[:rows, :], AF.Sqrt, [kr], [kr])
        P.op("dve", lambda e: e.reciprocal(out=rstd[:rows, :], in_=rstd[:rows, :]), reads=[kr], writes=[kr])
        STT(nmr[:rows, :], mv[:rows, 0:1], -1.0, rstd[:rows, :], ALU.mult, ALU.mult, [km, kr], [kn])
        ACT(yout, zin, AF.Identity, list(zkeys) + [kr, kn], ykeys, bias=nmr[:rows, :], scale=rstd[:rows, :])
        TT(yout, yout, gtile[:rows, 0:width], ALU.mult, list(ykeys) + [gk], ykeys)
        TT(yout, yout, btile[:rows, 0:width], ALU.add, list(ykeys) + [bk], ykeys)

    def ln_batch(items, gtile, btile, gk, bk, post):
        n = len(items)
        sl = []
        for i in range(n):
            par = ln_ctr[0] % 4
            ln_ctr[0] += 1
            sl.append(par)

        def A(i):
            z, rows, keys = items[i]
            par = sl[i]
            stats, mv, rstd = stats_[par], mv_[par], rstd_[par]
            for c in range(2):
                P.op("dve", lambda e, c=c: e.bn_stats(out=stats[:rows, c * 6:(c + 1) * 6], in_=z[:, c * 512:(c + 1) * 512]),
                     reads=keys, writes=["stats%d" % par])
            P.op("dve", lambda e: e.bn_aggr(out=mv[:rows, :], in_=stats[:rows, 0:12]), reads=["stats%d" % par],
                 writes=["mv%d" % par])
            TS(rstd[:rows, :], mv[:rows, 1:2], EPS, None, ALU.add, None, ["mv%d" % par], ["rstd%d" % par])
            ACT(rstd[:rows, :], rstd[:rows, :], AF.Sqrt, ["rstd%d" % par], ["rstd%d" % par])

        def B(i):
            z, rows, keys = items[i]
            par = sl[i]
            mv, rstd, nmr = mv_[par], rstd_[par], nmr_[par]
            P.op("dve", lambda e: e.reciprocal(out=rstd[:rows, :], in_=rstd[:rows, :]), reads=["rstd%d" % par],
                 writes=["rstd%d" % par])
            STT(nmr[:rows, :], mv[:rows, 0:1], -1.0, rstd[:rows, :], ALU.mult, ALU.mult, ["mv%d" % par, "rstd%d" % par],
                ["nmr%d" % par])
            ACT(z, z, AF.Identity, list(keys) + ["rstd%d" % par, "nmr%d" % par], keys, bias=nmr[:rows, :], scale=rstd[:rows, :])

        def C(i):
            z, rows, keys = items[i]
            TT(z, z, gtile[:rows, 0:D], ALU.mult, list(keys) + [gk], keys)
            TT(z, z, btile[:rows, 0:D], ALU.add, list(keys) + [bk], keys)
            post(i)

        for step in range(n + 2):
            if step < n:
                A(step)
            if 0 <= step - 1 < n:
                B(step - 1)
            if 0 <= step - 2 < n:
                C(step - 2)

    def to_xT(t, rows, c0, pbank):
        CP(ybf[:rows, :], x_res[:rows, t, :], ["xr%d" % t], ["ybf"], eng="act_copy")
        for kc in range(8):
            TR(psb(pbank)[:, kc * 128:kc * 128 + rows], ybf[:rows, kc * 128:(kc + 1) * 128], ident[:rows, :rows],
               ["ybf", "ident"], [pk(pbank)])
        CP(xT[:, :, c0:c0 + rows], psb(pbank).rearrange("p (k n) -> p k n", k=8)[:, :, 0:rows], [pk(pbank)], ["xT%d" % t])

    _CP = CP

    def CP(out, in_, reads, writes, eng="dve"):
        if eng == "act_copy":
            P.op("act", lambda e: e.copy(out=out, in_=in_), reads=reads, writes=writes)
        else:
            _CP(out, in_, reads, writes, eng=eng)

    def ffn(fi, tiles, ntok, g_name, b_name, final):
        segs = []
        c = 0
        while c < ntok:
            n = min(512, ntok - c)
            segs.append((c, n))
            c += n
        groups = []
        for s, (cs0, ncs) in enumerate(SLICES):
            c = 0
            while c < ncs:
                ng = min(WG, ncs - c)
                groups.append((s, c, ng, cs0 + c))
                c += ng
        issued = [0]

        def issue_group(extra):
            k = issued[0]
            if k >= len(groups):
                return
            issued[0] += 1
            _, _, ng, cc0 = groups[k]
            b = k % 2
            for half in range(2):
                col0 = half * FF + cc0 * 128
                DMA("pool", wu[b][:, :, half, 0:ng * 128],
                    w_up_d[fi][:, col0:col0 + ng * 128].rearrange("(k p) n -> p k n", p=128),
                    [], ["wu%d" % b] + extra)

        gk = 0
        for s, (cs0, ncs) in enumerate(SLICES):
            if s == 0:
                DMA("pool", wd[:, 0:ncs, :], w_dn_d[fi][cs0 * 128:(cs0 + ncs) * 128, :].rearrange("(c p) n -> p c n", p=128),
                    [], ["wd", "w_in", "w_out"])
                issue_group(["w_in", "w_out"])
                issue_group(["w_in", "w_out"])
            else:
                DMA("pool", wd[:, 0:ncs, :], w_dn_d[fi][cs0 * 128:(cs0 + ncs) * 128, :].rearrange("(c p) n -> p c n", p=128),
                    [], ["wd"])
            while gk < len(groups) and groups[gk][0] == s:
                _, c, ng, cc0 = groups[gk]
                b = gk % 2
                wk = "wu%d" % b
                for cl in range(ng):
                    for si, (c0, n) in enumerate(segs):
                        pg, pu = (0, 1) if (si % 2 == 0) else (2, 3)
                        for kc in range(8):
                            MM(psf[pg][:, 0:n], wu[b][:, kc, 0, cl * 128:(cl + 1) * 128], xT[:, kc, c0:c0 + n], kc == 0, kc == 7,
                               [wk, "xTall"], [pk(pg)])
                        for kc in range(8):
                            MM(psf[pu][:, 0:n], wu[b][:, kc, 1, cl * 128:(cl + 1) * 128], xT[:, kc, c0:c0 + n], kc == 0, kc == 7,
                               [wk, "xTall"], [pk(pu)])
                        sgb = sg[si % 2]
                        ACT(sgb[:, 0:n], psf[pg][:, 0:n], AF.Silu, [pk(pg)], [("u_sb", "vg")[si % 2]])
                        STT(hT[:, c + cl, c0:c0 + n], sgb[:, 0:n], 0.5, psf[pu][:, 0:n], ALU.mult, ALU.mult,
                            [("u_sb", "vg")[si % 2], pk(pu)], ["hT"])
                gk += 1
                issue_group([])
            for ti, (t, rows, c0) in enumerate(tiles):
                pb = (4, 5) if ti % 2 == 0 else (6, 7)
                for half in range(2):
                    for c in range(ncs):
                        MM(psf[pb[half]][:rows, :], hT[:, c, c0:c0 + rows], wd[:, c, half * 512:(half + 1) * 512],
                           c == 0, c == ncs - 1, ["hT", "wd"], [pk(pb[half])])
                    xs = x_res[:rows, t, half * 512:(half + 1) * 512]
                    if s == 0:
                        STT(xs, xs, ALPHA, psf[pb[half]][:rows, :], ALU.mult, ALU.add,
                            ["xr%d" % t, pk(pb[half])], ["xr%d" % t])
                    else:
                        TT(xs, xs, psf[pb[half]][:rows, :], ALU.add, ["xr%d" % t, pk(pb[half])], ["xr%d" % t])
        load_ln(g_name, b_name)
        def post(i):
            t, rows, c0 = tiles[i]
            if final is None:
                to_xT(t, rows, c0, 4 if (t % 2 == 0) else 6)
            else:
                final(t, rows)
        ln_batch([(x_res[:rows, t, :], rows, ["xr%d" % t]) for (t, rows, c0) in tiles], gcur, bcur, "gcur", "bcur", post)

    _op = P.op

    def op_bridge(eng, fn, reads=(), writes=(), dma=False):
        writes = list(writes)
        if any(k.startswith("xT") and k != "xTall" for k in writes):
            writes.append("xTall")
        return _op(eng, fn, reads=reads, writes=writes, dma=dma)
    P.op = op_bridge

    def rope(src, nh, rows, cos_t, sin_t, outb, outf, skeys, okeys):
        cb = bc(cos_t.unsqueeze(1), [rows, nh, 32])
        sbb = bc(sin_t.unsqueeze(1), [rows, nh, 32])
        x1 = src[:, :, 0:32]
        x2 = src[:, :, 32:64]
        a = rt1[:rows, 0:nh, :]
        b = rt2[:rows, 0:nh, :]
        TT(a, x1, cb, ALU.mult, list(skeys) + ["Tcos", "Scos"], ["rt1"])
        TT(b, x2, sbb, ALU.mult, list(skeys) + ["Tsin", "Ssin"], ["rt2"])
        TT(outf[:, :, 0:32], a, b, ALU.subtract, ["rt1", "rt2"], okeys)
        TT(a, x2, cb, ALU.mult, list(skeys) + ["Tcos", "Scos"], ["rt1"])
        TT(b, x1, sbb, ALU.mult, list(skeys) + ["Tsin", "Ssin"], ["rt2"])
        TT(outf[:, :, 32:64], a, b, ALU.add, ["rt1", "rt2"], okeys)
        CP(outb, outf, okeys, okeys)

    def mixer_common(t, rows, c0, cos_t, sin_t):
        for nb in range(5):
            w0c = nb * 512
            wn = min(512, INW - w0c)
            for kc in range(8):
                MM(psf[nb][:rows, 0:wn], xT[:, kc, c0:c0 + rows], w_in[:, kc, w0c:w0c + wn], kc == 0, kc == 7,
                   ["xT%d" % t, "w_in"], [pk(nb)])
        ACT(u_sb[:rows, :], psf[0][:rows, :], AF.Gelu_apprx_tanh, [pk(0)], ["u_sb"])
        ACT(vg[:rows, :], psf[1][:rows, :], AF.Gelu_apprx_tanh, [pk(1)], ["vg"])
        layernorm(vg[:rows, :], rows, 512, agB, abB, vln[:rows, :], ["vg"], ["vln"], "agB", "abB")
        CP(vbf[:rows, :], vln[:rows, :], ["vln"], ["vbf"])
        for g in range(2):
            rope(psf[2][:rows, g * 256:(g + 1) * 256].rearrange("p (h d) -> p h d", d=64), 4, rows, cos_t, sin_t,
                 rq[:rows, g:8:2, :], rqf[:rows, g:8:2, :], [pk(2)], ["rq"])
        rope(psf[3][:rows, 0:128].rearrange("p (h d) -> p h d", d=64), 2, rows, cos_t, sin_t,
             rq[:rows, 8:10, :], rqf[:rows, 8:10, :], [pk(3)], ["rq"])
        rope(psf[3][:rows, 256:512].rearrange("p (h d) -> p h d", d=64), 4, rows, cos_t, sin_t,
             ri[:rows, 0:4, :], rif[:rows, 0:4, :], [pk(3)], ["ri"])
        rope(psf[4][:rows, 0:64].rearrange("p (h d) -> p h d", d=64), 1, rows, cos_t, sin_t,
             ri[:rows, 4:5, :], rif[:rows, 4:5, :], [pk(4)], ["ri"])
        CP(vf[:rows, :], psf[3][:rows, 128:256], [pk(3)], ["vf"], eng="act_copy")
        TS(wq[:rows, :], psf[4][:rows, 64:68], IDX_W_SCALE, None, ALU.mult, None, [pk(4)], ["wq"])

    def out_proj_ln2(t, rows, c0):
        for kc in range(8):
            TR(psb(5)[:, kc * 128:kc * 128 + rows], mix[:rows, kc * 128:(kc + 1) * 128], ident[:rows, :rows],
               ["mix", "ident"], [pk(5)])
        CP(mixT[:, :, 0:rows], psb(5).rearrange("p (k n) -> p k n", k=8)[:, :, 0:rows], [pk(5)], ["ybf"])
        for half in range(2):
            for kc in range(8):
                MM(psf[6 + half][:rows, :], mixT[:, kc, 0:rows], w_out[:, kc, half * 512:(half + 1) * 512], kc == 0, kc == 7,
                   ["ybf", "w_out"], [pk(6 + half)])
            xs = x_res[:rows, t, half * 512:(half + 1) * 512]
            STT(xs, xs, ALPHA, psf[6 + half][:rows, :], ALU.mult, ALU.add, ["xr%d" % t, pk(6 + half)], ["xr%d" % t])
        layernorm(x_res[:rows, t, :], rows, D, gcur, bcur, x_res[:rows, t, :],
                  ["xr%d" % t], ["xr%d" % t], "gcur", "bcur")
        pending_xT.append((t, rows, c0))

    pending_xT = []

    def flush_xT():
        while pending_xT:
            t_, rows_, c0_ = pending_xT.pop(0)
            to_xT(t_, rows_, c0_, 6)

    def rope_parts(src, nh, rows, cos_t, sin_t, outb, outf, skeys, okeys):
        cb = bc(cos_t.unsqueeze(1), [rows, nh, 32])
        sbb = bc(sin_t.unsqueeze(1), [rows, nh, 32])
        x1 = src[:, :, 0:32]
        x2 = src[:, :, 32:64]
        a_ = rt1[:rows, 0:nh, :]
        b_ = rt2[:rows, 0:nh, :]

        def p1():
            TT(a_, x1, cb, ALU.mult, list(skeys) + ["Tcos", "Scos"], ["rt1"])
            TT(b_, x2, sbb, ALU.mult, list(skeys) + ["Tsin", "Ssin"], ["rt2"])
            TT(outf[:, :, 0:32], a_, b_, ALU.subtract, ["rt1", "rt2"], okeys)

        def p2():
            TT(a_, x2, cb, ALU.mult, list(skeys) + ["Tcos", "Scos"], ["rt1"])
            TT(b_, x1, sbb, ALU.mult, list(skeys) + ["Tsin", "Ssin"], ["rt2"])
            TT(outf[:, :, 32:64], a_, b_, ALU.add, ["rt1", "rt2"], okeys)
            CP(outb, outf, okeys, okeys)
        return [p1, p2]

    def emit_inproj(t, rows, c0):
        for nb in range(5):
            w0c = nb * 512
            wn = min(512, INW - w0c)
            for kc in range(8):
                MM(psf[nb][:rows, 0:wn], xT[:, kc, c0:c0 + rows], w_in[:, kc, w0c:w0c + wn], kc == 0, kc == 7,
                   ["xT%d" % t, "w_in"], [pk(nb)])

    def mixer_prompt(t, c0, seq, j, preissued=False, nxt=None):
        rows = 128
        tok0 = seq * SEQ + j * 128
        cos_t = cosT[:, j, :]
        sin_t = sinT[:, j, :]
        if not preissued:
            emit_inproj(t, rows, c0)
        flush_xT()
        rope(psf[3][:rows, 256:512].rearrange("p (h d) -> p h d", d=64), 4, rows, cos_t, sin_t,
             ri[:rows, 0:4, :], rif[:rows, 0:4, :], [pk(3)], ["ri"])
        rope(psf[4][:rows, 0:64].rearrange("p (h d) -> p h d", d=64), 1, rows, cos_t, sin_t,
             ri[:rows, 4:5, :], rif[:rows, 4:5, :], [pk(4)], ["ri"])
        TS(wq[:rows, :], psf[4][:rows, 64:68], IDX_W_SCALE, None, ALU.mult, None, [pk(4)], ["wq"])
        DMA("sp", nki_p[tok0:tok0 + 128, :], rif[:, 4, :], ["ri"], [])
        for h in range(5):
            TR(psb(6)[0:64, (1 + h) * 128:(2 + h) * 128], ri[:, h, :], ident[:], ["ri", "ident"], [pk(6)])
        CP(qiT[:].rearrange("p h n -> p (h n)"), psb(6)[0:64, 128:640], [pk(6)], ["qiT"])
        CP(kidxT[:, j * 128:(j + 1) * 128], psb(6)[0:64, 640:768], [pk(6)], ["kidxT"])
        nk = (j + 1) * 128
        wdiag = vbf[:, :].rearrange("p (h n) -> p h n", h=4)
        rlb = msk[:, :].rearrange("p (h n) -> p h n", h=4)
        for h in range(4):
            TS(wdiag[:, h, :], identf[:, :], wq[:, h:h + 1], None, ALU.mult, None, ["identf", "wq"], ["vbf"])
        for k0 in range(0, nk, 512):
            n = min(512, nk - k0)
            for hp in range(2):
                for hh in range(2):
                    h = 2 * hp + hh
                    MM(psf[5 + hh][:, 0:n], qiT[:, h, :], kidxT[:, k0:k0 + n], True, True, ["qiT", "kidxT"], [pk(5 + hh)])
                for hh in range(2):
                    h = 2 * hp + hh
                    ACT(rlb[:, h, 0:n], psf[5 + hh][:, 0:n], AF.Relu, [pk(5 + hh)], ["msk"])
                for hh in range(2):
                    h = 2 * hp + hh
                    MM(psf[7][:, 0:n], wdiag[:, h, :], rlb[:, h, 0:n], h == 0, h == 3, ["vbf", "msk"], [pk(7)])
            CP(sc[:, k0:k0 + n], psf[7][:, 0:n], [pk(7)], ["sc"], eng="act_copy")
        fillers = []
        fillers.append(lambda: ACT(u_sb[:rows, :], psf[0][:rows, :], AF.Gelu_apprx_tanh, [pk(0)], ["u_sb"]))
        fillers.append(lambda: ACT(vg[:rows, :], psf[1][:rows, :], AF.Gelu_apprx_tanh, [pk(1)], ["vg"]))
        par = ln_ctr[0] % 4
        ln_ctr[0] += 1
        st_, mv2, rs_, nm_ = stats_[par], mv_[par], rstd_[par], nmr_[par]
        ks, km, kr, kn = "stats%d" % par, "mv%d" % par, "rstd%d" % par, "nmr%d" % par

        def lnA():
            P.op("dve", lambda e: e.bn_stats(out=st_[:rows, 0:6], in_=vg[:rows, :]), reads=["vg"], writes=[ks])
            P.op("dve", lambda e: e.bn_aggr(out=mv2[:rows, :], in_=st_[:rows, 0:6]), reads=[ks], writes=[km])
            TS(rs_[:rows, :], mv2[:rows, 1:2], EPS, None, ALU.add, None, [km], [kr])
            ACT(rs_[:rows, :], rs_[:rows, :], AF.Sqrt, [kr], [kr])

        def lnB():
            P.op("dve", lambda e: e.reciprocal(out=rs_[:rows, :], in_=rs_[:rows, :]), reads=[kr], writes=[kr])
            STT(nm_[:rows, :], mv2[:rows, 0:1], -1.0, rs_[:rows, :], ALU.mult, ALU.mult, [km, kr], [kn])
            ACT(vln[:rows, :], vg[:rows, :], AF.Identity, ["vg", kr, kn], ["vln"], bias=nm_[:rows, :], scale=rs_[:rows, :])

        def lnC():
            TT(vln[:rows, :], vln[:rows, :], agB[:rows, :], ALU.mult, ["vln", "agB"], ["vln"])
            TT(vln[:rows, :], vln[:rows, :], abB[:rows, :], ALU.add, ["vln", "abB"], ["vln"])
            CP(vbf[:rows, :], vln[:rows, :], ["vln"], ["vbf"])

        def gm_mm():
            for g in range(4):
                MM(psf[7][:, g * 128:(g + 1) * 128], wsT[:, g, :], vbf[:, g * 128:(g + 1) * 128], True, True,
                   ["wsT", "vbf"], [pk(7)])

        def gm_ev(g):
            STT(mix[:, g * 128:(g + 1) * 128], psf[7][:, g * 128:(g + 1) * 128], bsT[:, g:g + 1],
                u_sb[:, g * 128:(g + 1) * 128], ALU.add, ALU.mult, [pk(7), "bsT", "u_sb"], ["mix"])
        fillers += [lnA]
        rq0 = rope_parts(psf[2][:rows, 0:256].rearrange("p (h d) -> p h d", d=64), 4, rows, cos_t, sin_t,
                         rq[:rows, 0:8:2, :], rqf[:rows, 0:8:2, :], [pk(2)], ["rq"])
        rq1 = rope_parts(psf[2][:rows, 256:512].rearrange("p (h d) -> p h d", d=64), 4, rows, cos_t, sin_t,
                         rq[:rows, 1:8:2, :], rqf[:rows, 1:8:2, :], [pk(2)], ["rq"])
        rk = rope_parts(psf[3][:rows, 0:128].rearrange("p (h d) -> p h d", d=64), 2, rows, cos_t, sin_t,
                        rq[:rows, 8:10, :], rqf[:rows, 8:10, :], [pk(3)], ["rq"])
        fillers += [rq0[0], lnB, rq0[1], lnC, rq1[0], gm_mm, rq1[1]]
        fillers += [lambda: gm_ev(0), rk[0], lambda: gm_ev(1), rk[1], lambda: gm_ev(2), lambda: gm_ev(3)]

        def vstuff():
            CP(vf[:rows, :], psf[3][:rows, 128:256], [pk(3)], ["vf"], eng="act_copy")
            DMA("sp", nk_p[tok0:tok0 + 128, :], rqf[:, 8:10, :].rearrange("p h d -> p (h d)"), ["rq"], [])
            DMA("sp", nv_p[tok0:tok0 + 128, :], vf[:, :], ["vf"], [])

        def vaug_f():
            CP(vaug[:, j, :, 0:64], vf[:, :].rearrange("p (g d) -> p g d", g=2), ["vf"], ["vaug"])
            MEMSET(vaug[:, j, :, 64:65], 1.0, ["vaug"])

        def qtr():
            for hl in range(4):
                TR(psb(5)[:, hl * 128:(hl + 1) * 128], rq[:, 2 * hl:2 * hl + 2, :].rearrange("p h d -> p (h d)"), ident[:],
                   ["rq", "ident"], [pk(5)])
            CP(qT[:].rearrange("p h n -> p (h n)"), psb(5)[:, 0:512], [pk(5)], ["qT"], eng="act_copy")

        def ktr():
            TR(psb(6)[:, 0:128], rq[:, 8:10, :].rearrange("p h d -> p (h d)"), ident[:], ["rq", "ident"], [pk(6)])
            CP(kT[:, j * 128:(j + 1) * 128], psb(6)[:, 0:128], [pk(6)], ["kT"])
        fillers += [vstuff, vaug_f, qtr, ktr]

        def fill(nf=1):
            for _ in range(nf):
                if fillers:
                    fillers.pop(0)()
        P.op("pool", lambda e: e.affine_select(out=sc[:, j * 128:(j + 1) * 128], in_=sc[:, j * 128:(j + 1) * 128],
                                               pattern=[[-1, 128]], compare_op=ALU.is_ge, fill=-BIG, base=0,
                                               channel_multiplier=1), reads=["sc"], writes=["sc"])
        if j >= 2:
            nf = j * 128
            RED(lo[:, 0:1], sc[:, 0:nf], ALU.min, ["sc"], ["lo"])
            RED(w0[:, 0:1], sc[:, 0:nk], ALU.max, ["sc"], ["w0"])
            STT(w0[:, 0:1], w0[:, 0:1], 1.0, lo[:, 0:1], ALU.add, ALU.subtract, ["w0", "lo"], ["w0"])
            for it in range(NBIS + 2):
                TS(Wt[:, it:it + 1], w0[:, 0:1], 0.5 ** it, None, ALU.mult, None, ["w0"], ["Wt"])
            TT(mid[:, 0:1], lo[:, 0:1], Wt[:, 1:2], ALU.add, ["lo", "Wt"], ["mid"])
            for it in range(NBIS):
                TS(junk[:, 0:nk], sc[:, 0:nk], mid[:, 0:1], None, ALU.is_ge, ALU.add, ["sc", "mid"], ["msk", "cnt"],
                   accum_out=cnt[:, 0:1])
                fill(1)
                if it < NBIS - 1:
                    STT(tsel[:, 0:1], cnt[:, 0:1], TOPK - 0.5, Wt[:, it + 1:it + 2], ALU.is_ge, ALU.mult,
                        ["cnt", "Wt"], ["tsel"])
                    STT(mid[:, 0:1], mid[:, 0:1], Wt[:, it + 2:it + 3], tsel[:, 0:1], ALU.subtract, ALU.add,
                        ["mid", "Wt", "tsel"], ["mid"])
                else:
                    STT(tsel[:, 0:1], cnt[:, 0:1], TOPK - 0.5, Wt[:, it + 1:it + 2], ALU.is_lt, ALU.mult,
                        ["cnt", "Wt"], ["tsel"])
                    TT(lo[:, 0:1], mid[:, 0:1], tsel[:, 0:1], ALU.subtract, ["mid", "tsel"], ["lo"])
            fill(100)
            TS(msk[:, 0:nk], sc[:, 0:nk], lo[:, 0:1], None, ALU.is_ge, None, ["sc", "lo"], ["msk"])
        else:
            fill(100)
            TS(msk[:, 0:nk], sc[:, 0:nk], -1.0e29, None, ALU.is_ge, None, ["sc"], ["msk"])
        for kt in range(j + 1):
            pbk = 5 if kt < 8 else 6
            TR(psb(pbk)[:, (kt % 8) * 128:(kt % 8 + 1) * 128], msk[:, kt * 128:(kt + 1) * 128], ident[:],
               ["msk", "ident"], [pk(pbk)])
        n1 = min(j + 1, 8)
        CP(maskT[:, 0:n1, :].rearrange("p a b -> p (a b)"), psb(5)[:, 0:n1 * 128], [pk(5)], ["maskT"])
        if j + 1 > 8:
            n2 = j + 1 - 8
            CP(maskT[:, 8:8 + n2, :].rearrange("p a b -> p (a b)"), psb(6)[:, 0:n2 * 128], [pk(6)], ["maskT"])
        for g in range(2):
            pob = 4 if g == 0 else 7
            gs = slice(64 * g, 64 * g + 64)
            for hf in range(2):
                for kt in range(j + 1):
                    pl = kt % 4
                    MM(psf[pl][:, 0:256], kT[gs, kt * 128:(kt + 1) * 128],
                       qT[gs, 2 * hf:2 * hf + 2, :].rearrange("p h n -> p (h n)"),
                       True, True, ["kT", "qT"], [pk(pl)])
                    ACT(pT[:, kt, :], psf[pl][:, 0:256], AF.Exp, [pk(pl)], ["pT%d" % kt], scale=0.125)
                    TT(pT[:, kt, :].rearrange("p (h n) -> p h n", h=2), pT[:, kt, :].rearrange("p (h n) -> p h n", h=2),
                       bc(maskT[:, kt, :].unsqueeze(1), [128, 2, 128]), ALU.mult, ["pT%d" % kt, "maskT"], ["pT%d" % kt],
                       eng=("pool" if kt % 3 == 0 else "dve"))
                for hh in range(2):
                    hl = 2 * hf + hh
                    for kt in range(j + 1):
                        MM(psf[pob][:, hl * 65:(hl + 1) * 65], pT[:, kt, hh * 128:(hh + 1) * 128], vaug[:, kt, g, :],
                           kt == 0, kt == j, ["pT%d" % kt, "vaug"], [pk(pob)])
            pov = psf[pob][:, 0:260].rearrange("p (h c) -> p h c", c=65)
            P.op("dve", lambda e, pov=pov: e.reciprocal(out=rinv[:, 0:4], in_=pov[:, :, 64]),
                 reads=[pk(pob)], writes=["rinv"])
            TT(mix[:, 512 + g * 256:512 + (g + 1) * 256].rearrange("p (h d) -> p h d", d=64), pov[:, :, 0:64],
               bc(rinv[:, 0:4].unsqueeze(2), [128, 4, 64]), ALU.mult, [pk(pob), "rinv"], ["mix"])
            if g == 1 and nxt is not None:
                emit_inproj(nxt[0], 128, nxt[1])
        out_proj_ln2(t, rows, c0)

    def mixer_sample(t, c0):
        rows = NSMP
        mixer_common(t, rows, c0, cosS[:rows, 0, :], sinS[:rows, 0, :])
        DMA("sp", nk_s[:, :], rqf[:rows, 8:10, :].rearrange("p h d -> p (h d)"), ["rq"], [])
        DMA("sp", nv_s[:, :], vf[:rows, :], ["vf"], [])
        DMA("sp", nki_s[:, :], rif[:rows, 4, :], ["ri"], [])
        DMA("sp", av_s[:, :], vln[:rows, :], ["vln"], [])
        for g in range(4):
            TS(vg[:rows, g * 128:(g + 1) * 128], vln[:rows, g * 128:(g + 1) * 128], ws00[:, g:g + 1], bs0[:, g:g + 1],
               ALU.mult, ALU.add, ["vln", "ws00", "bs0"], ["vg"])
        TT(mix[:rows, 0:512], vg[:rows, :], u_sb[:rows, :], ALU.mult, ["vg", "u_sb"], ["mix"])
        MEMSET(riZ[:, :, :, :], 0.0, ["riZ"])
        CP(riZ[:, 0, :, 0:64], ri[:rows, 0:4, :], ["ri"], ["riZ"])
        CP(riZ[:, 1, :, 64:128], ri[:rows, 0:4, :], ["ri"], ["riZ"])
        for s2 in range(2):
            for h in range(4):
                TR(psb(5)[:, (s2 * 4 + h) * 16:(s2 * 4 + h + 1) * 16], riZ[:, s2, h, :], ident[:rows, :rows],
                   ["riZ", "ident"], [pk(5)])
        for s2 in range(2):
            CP(qiTs[:, :, s2, :, :].rearrange("p i h b -> p h i b"),
               psb(5)[:, s2 * 64:(s2 + 1) * 64].rearrange("p (h i b) -> p h i b", h=4, i=8), [pk(5)], ["qiTs"])
        for i in range(8):
            TS(wqm[:, :], wq[:rows, :], oh[:, i:i + 1], None, ALU.mult, None, ["wq", "oh"], ["wqm"])
            MM(psf[6][:, i * 4:(i + 1) * 4], rep[:, :], wqm[:, :], True, True, ["rep", "wqm"], [pk(6)])
        CP(wB[:].rearrange("p a b -> p (a b)"), psf[6][:, 0:32], [pk(6)], ["wB"])
        TT(ssd[:, :, :], rif[:rows, 0:4, :], bc(rif[:rows, 4:5, :], [rows, 4, 64]), ALU.mult, ["ri"], ["ssd"])
        RED(ss4[:, :], ssd[:, :, :], ALU.add, ["ssd"], ["ss4"])
        TS(ss4[:, :], ss4[:, :], 0.0, None, ALU.max, None, ["ss4"], ["ss4"])
        TT(ss4[:, :], ss4[:, :], wq[:rows, :], ALU.mult, ["ss4", "wq"], ["ss4"])
        RED(ss1[:, :], ss4[:, :], ALU.add, ["ss4"], ["ss1"])
        TS(ssb[:, :], oh[:, :], ss1[:, 0:1], None, ALU.mult, None, ["oh", "ss1"], ["ssb"])
        MM(psf[7][:, 0:8], aall[:, :], ssb[:, :], True, True, ["aall", "ssb"], [pk(7)])
        TS(SC[:, :, 128], psf[7][:, 0:8], negfill[:, 0:1], None, ALU.add, None, [pk(7), "negfill"], ["SC"])
        for i in range(8):
            for c4 in range(4):
                P.op("pool", lambda e, i=i, c4=c4: e.indirect_dma_start(
                    out=KIc.rearrange("p s d -> p (s d)"), out_offset=None,
                    in_=cache_kidx[:, :],
                    in_offset=bass.IndirectOffsetOnAxis(ap=idx4[:, i, c4:c4 + 1], axis=0)),
                    reads=["idx4"], writes=PTK, dma=True)
                for s8 in range(2):
                    for m in range(8):
                        TR(psb(5)[:, m * 128:(m + 1) * 128],
                           KIc[:, s8 * 16 + 2 * m:s8 * 16 + 2 * m + 2, :].rearrange("p s d -> p (s d)"), ident[:],
                           PTK + ["ident"], [pk(5)])
                    CP(kiTc.rearrange("p a b -> p (a b)"), psb(5)[:, :], [pk(5)], ["msk"], eng="act_copy")
                    for m in range(8):
                        sl = c4 * 32 + s8 * 16 + 2 * m
                        pbk = 0 if sl < 64 else 1
                        MM(psf[pbk][:, (sl % 64) * 8:(sl % 64) * 8 + 16], kiTc[:, m, :],
                           qiTs[:, i, :, :, :].rearrange("p s h b -> p (s h b)"), True, True,
                           ["msk", "qiTs"], [pk(pbk)])
            for b2 in range(2):
                ps_ = slice(64 * b2, 64 * b2 + 64)
                for hb in range(2):
                    src = psf[hb][ps_, :].rearrange("p (s h b) -> p s h b", h=4, b=2)[:, :, :, b2]
                    STT(T4[ps_, hb * 64:(hb + 1) * 64, :], src, 0.0, bc(wB[ps_, i, :].unsqueeze(1), [64, 64, 4]),
                        ALU.max, ALU.mult, [pk(hb), "wB"], ["rl"])
                RED(SC[ps_, i, 0:128], T4[ps_, :, :], ALU.add, ["rl"], ["SC"])
        RED(Mx[:, :], SC[:, :, 0:128], ALU.max, ["SC"], ["Mx"], absval=True)
        P.op("pe", lambda e: e.transpose(out=psf[2][0:8, 0:128], in_=Mx[:, :], identity=identf[:]),
             reads=["Mx", "identf"], writes=[pk(2)])
        RED(MxT[:, :], psf[2][0:8, 0:128], ALU.max, [pk(2)], ["MxT"])
        TS(MxT[:, :], MxT[:, :], 1.0, None, ALU.add, None, ["MxT"], ["MxT"])
        TS(Mdiag[:, :], identf[0:8, 0:8], MxT[:, 0:1], None, ALU.mult, None, ["identf", "MxT"], ["Mdiag"])
        MM(psf[2][:, 256:264], onesf[0:8, :], Mdiag[:, :], True, True, ["onesf", "Mdiag"], [pk(2)])
        TS(lo[:, :], psf[2][:, 256:264], -1.0, None, ALU.mult, None, [pk(2)], ["lo"])
        TS(w0[:, :], psf[2][:, 256:264], 2.0, None, ALU.mult, None, [pk(2)], ["w0"])
        for it in range(NBIS):
            TS(wdt[:, :], w0[:, :], 0.5 ** (it + 1), None, ALU.mult, None, ["w0"], ["wdt"])
            TT(mid[:, :], lo[:, :], wdt[:, :], ALU.add, ["lo", "wdt"], ["mid"])
            TT(cmpS[:, :, :], SC[:, :, :], bc(mid[:, :].unsqueeze(2), [128, 8, 129]), ALU.is_ge, ["SC", "mid"], ["sc"])
            RED(cnt[:, :], cmpS[:, :, :], ALU.add, ["sc"], ["cnt"])
            MM(psf[3][:, 0:8], bo[:, :], cnt[:, :], True, True, ["bo", "cnt"], [pk(3)])
            STT(tsel[:, :], psf[3][:, 0:8], TOPK - 0.5, wdt[:, :], ALU.is_ge, ALU.mult, [pk(3), "wdt"], ["tsel"])
            TT(lo[:, :], lo[:, :], tsel[:, :], ALU.add, ["lo", "tsel"], ["lo"])
        TT(mskS[:, :, :], SC[:, :, :], bc(lo[:, :].unsqueeze(2), [128, 8, 129]), ALU.is_ge, ["SC", "lo"], ["mskS"])
        MEMSET(QZ[:, :, :], 0.0, ["QZ"])
        CP(QZ[:, 0:4, 0:64], rq[:rows, 0:8:2, :], ["rq"], ["QZ"])
        CP(QZ[:, 4:8, 64:128], rq[:rows, 1:8:2, :], ["rq"], ["QZ"])
        for h in range(8):
            TR(psb(5)[:, h * 16:(h + 1) * 16], QZ[:, h, :], ident[:rows, :rows], ["QZ", "ident"], [pk(5)])
        CP(Qblk[:].rearrange("p i h b -> p h i b"), psb(5)[:, 0:128].rearrange("p (h i b) -> p h i b", h=8, i=8), [pk(5)], ["Qblk"])
        TR(psb(6)[:, 0:16], rq[:rows, 8:10, :].rearrange("p h d -> p (h d)"), ident[:rows, :rows], ["rq", "ident"], [pk(6)])
        CP(kTs[:, :], psb(6)[:, 0:16], [pk(6)], ["kTs"])
        MEMSET(KTself[:, :, :], 0.0, ["KTself"])
        CP(KTself[:, :, 0:128:64], kTs[:, :].rearrange("p (i b) -> p i b", b=2), ["kTs"], ["KTself"])
        CP(vsbf[:, :], vf[:rows, :], ["vf"], ["vsbf"])
        for i in range(8):
            TS(vfm[:, :], vf[:rows, :], oh[:, i:i + 1], None, ALU.mult, None, ["vf", "oh"], ["vfm"])
            MM(psf[7][:, 0:128], aall[:, :], vfm[:, :], True, True, ["aall", "vfm"], [pk(7)])
            CP(Vself[:, i, :], psf[7][:, 0:128], [pk(7)], ["Vself"])
        MEMSET(Ps[:, :, :, :], 0.0, ["Ps"])
        for i in range(8):
            MEMSET(Pacc[:, :], 0.0, ["Pacc"])
            for c4 in range(9):
                ns = 16 if c4 < 8 else 1
                if c4 < 8:
                    P.op("pool", lambda e, i=i, c4=c4: e.indirect_dma_start(
                        out=Kc.rearrange("p s d -> p (s d)"), out_offset=None,
                        in_=cache_k[:, :],
                        in_offset=bass.IndirectOffsetOnAxis(ap=idx8[:, i, c4:c4 + 1], axis=0)),
                        reads=["idx8"], writes=PTK, dma=True)
                    P.op("pool", lambda e, i=i, c4=c4: e.indirect_dma_start(
                        out=Vc.rearrange("p s d -> p (s d)"), out_offset=None,
                        in_=cache_v[:, :],
                        in_offset=bass.IndirectOffsetOnAxis(ap=idx8[:, i, c4:c4 + 1], axis=0)),
                        reads=["idx8"], writes=["sc"], dma=True)
                    for s8 in range(2):
                        for s in range(8):
                            TR(psb(5)[:, s * 128:(s + 1) * 128], Kc[:, s8 * 8 + s, :], ident[:], PTK + ["ident"], [pk(5)])
                        CP(KTc.rearrange("p a b -> p (a b)"), psb(5)[:, :], [pk(5)], ["msk"], eng="act_copy")
                        for s in range(8):
                            sl = s8 * 8 + s
                            MM(psf[0][:, sl * 16:(sl + 1) * 16], KTc[:, s, :],
                               Qblk[:, i, :, :].rearrange("p h b -> p (h b)"), True, True,
                               ["msk", "Qblk"], [pk(0)])
                else:
                    MM(psf[0][:, 0:16], KTself[:, i, :], Qblk[:, i, :, :].rearrange("p h b -> p (h b)"),
                       True, True, ["KTself", "Qblk"], [pk(0)])
                for b2 in range(2):
                    ps_ = slice(64 * b2, 64 * b2 + 64)
                    src = psf[0][ps_, 0:ns * 16].rearrange("p (s h b) -> p s h b", h=8, b=2)[:, :, :, b2]
                    ACT(T4[ps_, 0:ns * 2, :].rearrange("p (s a) c -> p s (a c)", a=2), src, AF.Exp, [pk(0)], ["rl"], scale=0.125)
                    mk = mskS[ps_, i, c4 * 16:c4 * 16 + ns]
                    TT(Ps[ps_, 0:ns, b2, :], T4[ps_, 0:ns * 2, :].rearrange("p (s a) c -> p s (a c)", a=2),
                       bc(mk.unsqueeze(2), [64, ns, 8]), ALU.mult, ["rl", "mskS"], ["Ps"])
                RED(Ptmp[:, :], Ps[:, 0:ns, :, :].rearrange("p s b h -> p (b h) s"), ALU.add, ["Ps"], ["Ptmp"])
                TT(Pacc[:, :], Pacc[:, :], Ptmp[:, :], ALU.add, ["Pacc", "Ptmp"], ["Pacc"])
                for s in range(ns):
                    rhs = Vc[:, s, :] if c4 < 8 else Vself[:, i, :]
                    MM(psf[1][0:16, 0:128], Ps[:, s, :, :].rearrange("p b h -> p (b h)"), rhs,
                       (c4 == 0 and s == 0), (c4 == 8), ["Ps", "sc", "Vself"], [pk(1)])
            MM(psf[1][0:16, 128:129], Pacc[:, :], onesf[:, 0:1], True, True, ["Pacc", "onesf"], [pk(1)])
            P.op("dve", lambda e: e.reciprocal(out=den[:, :], in_=psf[1][0:16, 128:129]), reads=[pk(1)], writes=["den"])
            TS(Osel[:, :], psf[1][0:16, 0:64], hm[:, 0:1], None, ALU.mult, None, [pk(1), "hm"], ["Osel"])
            STT(Osel[:, :], psf[1][0:16, 64:128], hm[:, 1:2], Osel[:, :], ALU.mult, ALU.add, [pk(1), "hm", "Osel"], ["Osel"])
            TS(Osel[:, :], Osel[:, :], den[:, 0:1], None, ALU.mult, None, ["Osel", "den"], ["Osel"])
            TT(Oexp[:, :, :], bc(Osel[:, :].unsqueeze(1), [16, 8, 64]), bc(hm[:, 2:10].unsqueeze(2), [16, 8, 64]),
               ALU.mult, ["Osel", "hm"], ["Oexp"])
            MM(psf[2][0:16, 0:512], selp[:, i, :], Oexp[:, :, :].rearrange("p h d -> p (h d)"), i == 0, i == 7,
               ["selp", "Oexp"], [pk(2)])
        CP(mix[:rows, 512:1024], psf[2][0:16, 0:512], [pk(2)], ["mix"])
        out_proj_ln2(t, rows, c0)

    def load_w_in_out():
        wv = w_in_d.rearrange("(k p) n -> p k n", p=128)
        for (a0, a1) in ((0, 1024), (1024, 2048), (2048, INW)):
            DMA("pool", w_in[:, :, a0:a1], wv[:, :, a0:a1], [], ["w_in"] + RKEYS)
        DMA("pool", w_out, w_out_d.rearrange("(k p) n -> p k n", p=128), [], ["w_out"] + RKEYS)

    if debug:
        DMA("sp", dbg_tab[:, 0:512], cosT[:].rearrange("p a b -> p (a b)"), ["Tcos"], [])
        DMA("sp", dbg_tab[:, 512:1024], sinT[:].rearrange("p a b -> p (a b)"), ["Tsin"], [])
        DMA("sp", dbg_tab[:, 1024:1056], cosS[:].rearrange("p a b -> p (a b)"), ["Scos"], [])
        DMA("sp", dbg_tab[:, 1056:1088], sinS[:].rearrange("p a b -> p (a b)"), ["Ssin"], [])
    ngroups = NSEQ * (16 // GT)
    if ngroups_run is not None:
        ngroups = ngroups_run
    for gi in range(ngroups):
        seq = gi // (16 // GT)
        j0 = (gi % (16 // GT)) * GT
        tiles = [(t, 128, t * 128) for t in range(GT)]
        has_s = (gi == 0)
        if has_s:
            tiles.append((GT, NSMP, GT * 128))
        ntok = GT * 128 + (NSMP if has_s else 0)
        for (t, rows, c0) in tiles:
            if t < GT:
                tok0 = seq * SEQ + (j0 + t) * 128
                DMA("sp", x_res[:, t, :], x_prompt[tok0:tok0 + 128, :], [], ["xr%d" % t])
            else:
                DMA("sp", x_res[:NSMP, t, :], x_sample[:, :], [], ["xr%d" % t])
            to_xT(t, rows, c0, 4 if (t % 2 == 0) else 6)
        ffn(0, tiles, ntok, "ln1_g", "ln1_b", None)
        if debug and gi == 0:
            DMA("sp", dbg_ln1[:, :], x_res[:].rearrange("p a b -> p (a b)"), ["xr%d" % t for t in range(GT + 1)], [])
        load_w_in_out()
        load_ln("ln2_g", "ln2_b")
        for (t, rows, c0) in tiles:
            if t < GT:
                nxt = (t + 1, c0 + 128) if t + 1 < GT else None
                mixer_prompt(t, c0, seq, j0 + t, preissued=(t > 0), nxt=nxt)
            else:
                flush_xT()
                mixer_sample(t, c0)
            if debug and gi == 0:
                DMA("sp", dbg_mix[:, t * D:(t + 1) * D], mix[:, :], ["mix"], [])
                DMA("sp", dbg_sc[:, t * SEQ:(t + 1) * SEQ], sc[:, :], ["sc"], [])
                DMA("sp", dbg_msk[:, t * SEQ:(t + 1) * SEQ], msk[:, :], ["msk"], [])
        flush_xT()
        if debug and gi == 0:
            DMA("sp", dbg_ln2[:, :], x_res[:].rearrange("p a b -> p (a b)"), ["xr%d" % t for t in range(GT + 1)], [])

        def final(t, rows, seq=seq, j0=j0):
            if t < GT:
                tok0 = seq * SEQ + (j0 + t) * 128
                DMA("sp", y_prompt[tok0:tok0 + 128, :], x_res[:, t, :], ["xr%d" % t], [])
            else:
                DMA("sp", y_sample[:, :], x_res[:NSMP, t, :], ["xr%d" % t], [])
        ffn(1, tiles, ntok, "ln3_g", "ln3_b", final)

    P.emit(nc, es, None)
    es.close()
    return nc


_NC = None


def _consts():
    half = 32
    invf = (10000.0 ** (-np.arange(half, dtype=np.float32) / half)).astype(np.float32)
    c_invf = np.tile(invf[None, :], (128, 1)).astype(np.float32)
    p = np.arange(128)
    c_bo = (p[:, None] // 64 == p[None, :] // 64).astype(np.float32)
    b = np.arange(16)
    aall = np.zeros((16, 128), np.float32)
    rep = np.zeros((16, 128), np.float32)
    selp = np.zeros((16, 8, 16), np.float32)
    for bb in range(16):
        aall[bb, 64 * (bb % 2)] = 1.0
        rep[bb, 64 * (bb % 2):64 * (bb % 2) + 64] = 1.0
    for i in range(8):
        for r in range(16):
            selp[r, i, 2 * i + r // 8] = 1.0
    oh = (b[:, None] // 2 == np.arange(8)[None, :]).astype(np.float32)
    negfill = np.full((128, 1), -BIG, np.float32)
    negfill[0, 0] = 0.0
    negfill[64, 0] = 0.0
    hm = np.zeros((16, 10), np.float32)
    for r in range(16):
        h = r % 8
        hm[r, 0] = 1.0 if h < 4 else 0.0
        hm[r, 1] = 0.0 if h < 4 else 1.0
        hm[r, 2 + h] = 1.0
    return dict(c_invf=c_invf, c_bo=c_bo, c_aall=aall, c_rep=rep, c_oh=oh,
                c_negfill=negfill, c_selp=selp.reshape(16, -1), c_hm=hm)


def kernel(x_prompt, x_sample, cache_k, cache_v, cache_kidx, page_table, ln1_g, ln1_b, ffn1_w_up, ffn1_w_down,
           ln2_g, ln2_b, w_in, a_ln_g, a_ln_b, a_ws, a_bs, w_out, ln3_g, ln3_b, ffn2_w_up, ffn2_w_down):
    global _NC
    if _NC is None:
        _NC = build_nc()
    nc = _NC
    f = lambda a: np.ascontiguousarray(np.asarray(a))
    consts = _consts()
    ck = f(cache_k).reshape(10240 * 8, 2048)
    cv = f(cache_v).reshape(10240 * 8, 2048)
    cki = f(cache_kidx).reshape(10240 * 4, 2048)
    shared = dict(
        cache_k=ck, cache_v=cv, cache_kidx=cki,
        ln1_g=f(ln1_g).reshape(1, D), ln1_b=f(ln1_b).reshape(1, D), ln2_g=f(ln2_g).reshape(1, D),
        ln2_b=f(ln2_b).reshape(1, D), ln3_g=f(ln3_g).reshape(1, D), ln3_b=f(ln3_b).reshape(1, D),
        a_ln_g=f(a_ln_g).reshape(1, 512), a_ln_b=f(a_ln_b).reshape(1, 512),
        ffn1_w_up=f(ffn1_w_up).reshape(D, 2 * FF), ffn2_w_up=f(ffn2_w_up).reshape(D, 2 * FF),
        ffn1_w_down=f(ffn1_w_down).reshape(FF, D), ffn2_w_down=f(ffn2_w_down).reshape(FF, D),
        w_in=f(w_in).reshape(D, INW), w_out=f(w_out).reshape(D, D),
        a_ws=f(a_ws).reshape(4, 128, 128), a_bs=f(a_bs).reshape(4, 128), **consts)
    xp = f(x_prompt)
    xs = f(x_sample).reshape(128, D)
    pt = f(page_table).astype(np.int32)
    in_maps = []
    for c in range(8):
        m = dict(shared)
        m["x_prompt"] = xp[2 * c:2 * c + 2].reshape(NSEQ * SEQ, D)
        m["x_sample"] = xs[16 * c:16 * c + 16]
        m["page_table"] = pt[16 * c:16 * c + 16].reshape(NSMP * NPAGE, 1)
        in_maps.append(m)
    res = run_bass_kernel_spmd(nc, in_maps, core_ids=list(range(8)))
    R = res.results
    cat = lambda n: np.concatenate([r[n] for r in R], axis=0)
    y_p = cat("y_prompt").reshape(16, SEQ, D)
    y_s = cat("y_sample").reshape(128, 1, D)
    nk_p = cat("nk_p").reshape(1, 16, SEQ, 2, 64)
    nv_p = cat("nv_p").reshape(1, 16, SEQ, 2, 64)
    nki_p = cat("nki_p").reshape(1, 16, SEQ, 64)
    nk_s = cat("nk_s").reshape(1, 128, 1, 2, 64)
    nv_s = cat("nv_s").reshape(1, 128, 1, 2, 64)
    nki_s = cat("nki_s").reshape(1, 128, 1, 64)
    av_s = cat("av_s").reshape(1, 128, 1, 512)
    return (y_p, y_s, nk_p, nv_p, nki_p, nk_s, nv_s, nki_s, av_s)
```

```python
import numpy as np
from contextlib import ExitStack
import concourse.bass as bass
import concourse.mybir as mybir
from concourse.bass_utils import run_bass_kernel_spmd

F32 = mybir.dt.float32
BF16 = mybir.dt.bfloat16
I32 = mybir.dt.int32
AF = mybir.ActivationFunctionType
ALU = mybir.AluOpType
AX = mybir.AxisListType

D = 1024
FF = 2816
NFC = 22
SLICES = [(0, 6), (6, 6), (12, 5), (17, 5)]
CPS = 6
WG = 3
INW = 2116
SEQ = 2048
NSEQ = 2
NSMP = 16
ALPHA = 2.0 ** 0.25
EPS = 1e-5
IDX_W_SCALE = 256.0 ** -0.5
TOPK = 256
NBIS = 22
BIG = 1.0e30
GT = 8
NPAGE = 64


class Op:
    __slots__ = ("eng", "fn", "reads", "writes", "dma", "deps", "signal", "sem", "val", "idx", "prev_val", "raw")


class Prog:
    ENGS = ("pe", "act", "dve", "pool", "sp")

    def __init__(self):
        self.ops = []
        self.last_w = {}
        self.readers = {}

    def op(self, eng, fn, reads=(), writes=(), dma=False):
        o = Op()
        o.eng, o.fn, o.dma = eng, fn, dma
        o.reads, o.writes = tuple(reads), tuple(writes)
        o.deps = set()
        o.raw = set()
        o.signal = dma
        o.idx = len(self.ops)
        for k in o.reads:
            w = self.last_w.get(k)
            if w is not None:
                o.deps.add(w)
                o.raw.add(w)
        for k in o.writes:
            w = self.last_w.get(k)
            if w is not None:
                o.deps.add(w)
            for r in self.readers.get(k, ()):
                o.deps.add(r)
        for k in o.reads:
            self.readers.setdefault(k, []).append(o.idx)
        for k in o.writes:
            self.last_w[k] = o.idx
            self.readers[k] = []
        o.deps.discard(o.idx)
        self.ops.append(o)
        return o

    def emit(self, nc, es, out_dma_ops):
        ops = self.ops
        for o in ops:
            nd = set()
            for d in o.deps:
                p = ops[d]
                if (not p.dma) and (not o.dma) and p.eng == o.eng:
                    if o.eng == "pe" or d not in o.raw:
                        continue
                nd.add(d)
                p.signal = True
            o.deps = nd
        esem = {e: es.enter_context(nc.semaphore("S_" + e)) for e in self.ENGS}
        NDS = 12
        dsem = {e: [es.enter_context(nc.semaphore("D_%s_%d" % (e, i))) for i in range(NDS)]
                for e in ("sp", "pool", "act")}
        ecount = {e: 0 for e in self.ENGS}
        dcount = {e: [0] * NDS for e in dsem}
        dnext = {e: 0 for e in dsem}
        for o in ops:
            if o.dma:
                j = dnext[o.eng]
                dnext[o.eng] = (j + 1) % NDS
                o.prev_val = dcount[o.eng][j]
                dcount[o.eng][j] += 16
                o.sem, o.val = dsem[o.eng][j], dcount[o.eng][j]
            elif o.signal:
                ecount[o.eng] += 1
                o.sem, o.val = esem[o.eng], ecount[o.eng]
        by_eng = {e: [o for o in ops if o.eng == e] for e in self.ENGS}
        final_waits = [(o.sem, o.val) for o in ops if o.dma]
        block = es.enter_context(nc.Block())

        def run(engname, eng):
            waited = {}

            def wait(sem, val):
                key = id(sem)
                if waited.get(key, 0) >= val:
                    return
                waited[key] = val
                eng.wait_ge(sem, val)

            for o in by_eng[engname]:
                for d in sorted(o.deps):
                    p = ops[d]
                    wait(p.sem, p.val)
                if o.dma and o.prev_val > 0:
                    wait(o.sem, o.prev_val)
                ins = o.fn(eng)
                if o.dma:
                    ins.then_inc(o.sem, 16)
                elif o.signal:
                    ins.then_inc(o.sem, 1)
            if engname == "sp":
                fw = {}
                for sem, val in final_waits:
                    fw[id(sem)] = (sem, max(val, fw.get(id(sem), (None, 0))[1]))
                for sem, val in fw.values():
                    eng.wait_ge(sem, val)

        @block.tensor
        def _(e):
            run("pe", e)

        @block.scalar
        def _(e):
            run("act", e)

        @block.vector
        def _(e):
            run("dve", e)

        @block.gpsimd
        def _(e):
            run("pool", e)

        @block.sync
        def _(e):
            run("sp", e)


def build_nc(debug=False, npool=10240, ngroups_run=None):
    nc = bass.Bass("TRN2", target_bir_lowering=False)
    P = Prog()
    es = ExitStack()

    def din(name, shape, dt=F32):
        return nc.dram_tensor(name, list(shape), dt, kind="ExternalInput").ap()

    def dout(name, shape, dt=F32):
        return nc.dram_tensor(name, list(shape), dt, kind="ExternalOutput").ap()

    NTOK = NSEQ * SEQ
    x_prompt = din("x_prompt", [NTOK, D])
    x_sample = din("x_sample", [NSMP, D])
    cache_k = din("cache_k", [npool * 8, 2048])
    cache_v = din("cache_v", [npool * 8, 2048])
    cache_kidx = din("cache_kidx", [npool * 4, 2048])
    page_table = din("page_table", [NSMP * NPAGE, 1], I32)
    lnp = {n: din(n, [1, D]) for n in ("ln1_g", "ln1_b", "ln2_g", "ln2_b", "ln3_g", "ln3_b")}
    a_ln_g = din("a_ln_g", [1, 512])
    a_ln_b = din("a_ln_b", [1, 512])
    w_up_d = [din("ffn1_w_up", [D, 2 * FF]), din("ffn2_w_up", [D, 2 * FF])]
    w_dn_d = [din("ffn1_w_down", [FF, D]), din("ffn2_w_down", [FF, D])]
    w_in_d = din("w_in", [D, INW])
    w_out_d = din("w_out", [D, D])
    a_ws = din("a_ws", [4, 128, 128])
    a_bs = din("a_bs", [4, 128])
    c_invf = din("c_invf", [128, 32])
    c_bo = din("c_bo", [128, 128])
    c_aall = din("c_aall", [16, 128])
    c_rep = din("c_rep", [16, 128])
    c_oh = din("c_oh", [16, 8])
    c_negfill = din("c_negfill", [128, 1])
    c_selp = din("c_selp", [16, 8 * 16])
    c_hm = din("c_hm", [16, 10])

    y_prompt = dout("y_prompt", [NTOK, D])
    y_sample = dout("y_sample", [NSMP, D])
    nk_p = dout("nk_p", [NTOK, 128])
    nv_p = dout("nv_p", [NTOK, 128])
    nki_p = dout("nki_p", [NTOK, 64])
    nk_s = dout("nk_s", [NSMP, 128])
    nv_s = dout("nv_s", [NSMP, 128])
    nki_s = dout("nki_s", [NSMP, 64])
    av_s = dout("av_s", [NSMP, 512])
    if debug:
        dbg_tab = dout("dbg_tab", [128, 4 * 512 + 128])
        dbg_ln1 = dout("dbg_ln1", [128, (GT + 1) * D])
        dbg_ln2 = dout("dbg_ln2", [128, (GT + 1) * D])
        dbg_mix = dout("dbg_mix", [128, (GT + 1) * D], BF16)
        dbg_sc = dout("dbg_sc", [128, (GT + 1) * SEQ])
        dbg_msk = dout("dbg_msk", [128, (GT + 1) * SEQ], BF16)

    def sb(name, shape, dt=F32):
        return es.enter_context(nc.sbuf_tensor(name, list(shape), dt))

    psf = [es.enter_context(nc.psum_tensor("ps%d" % i, [128, 512], F32)) for i in range(8)]

    def psb(i):
        return psf[i][:].bitcast(BF16)

    def pk(i):
        return "ps%d" % i

    NT = GT + 1
    NTK = GT * 128 + NSMP
    x_res = sb("x_res", [128, NT, D])
    xT = sb("xT", [128, 8, NTK], BF16)
    R_bytes = max(CPS * NTK * 2 + CPS * D * 2 + 2 * 8 * 2 * WG * 128 * 2, 8 * INW * 2 + 8 * D * 2)
    Rt = sb("Rregion", [128, R_bytes // 4], F32)
    Rb = Rt[:].bitcast(BF16)
    o0 = 0
    hT = Rb[:, o0:o0 + CPS * NTK].rearrange("p (c n) -> p c n", c=CPS)
    o0 += CPS * NTK
    wd = Rb[:, o0:o0 + CPS * D].rearrange("p (c n) -> p c n", c=CPS)
    o0 += CPS * D
    WUE = 8 * 2 * WG * 128
    wu = [Rb[:, o0 + i * WUE:o0 + (i + 1) * WUE].rearrange("p (k t n) -> p k t n", k=8, t=2) for i in range(2)]
    w_in = Rb[:, 0:8 * INW].rearrange("p (k n) -> p k n", k=8)
    w_out = Rb[:, 8 * INW:8 * INW + 8 * D].rearrange("p (k n) -> p k n", k=8)
    RKEYS = ["hT", "wd", "wu0", "wu1"]

    gcur = sb("gcur", [128, D])
    bcur = sb("bcur", [128, D])
    agB = sb("agB", [128, 512])
    abB = sb("abB", [128, 512])
    ident = sb("ident", [128, 128], BF16)
    identf = sb("identf", [128, 128])
    wsT = sb("wsT", [128, 4, 128], BF16)
    wsf = sb("wsf", [128, 4, 128])
    wsb = sb("wsb", [128, 4, 128], BF16)
    bsT = sb("bsT", [128, 4])
    ws00 = sb("ws00", [16, 4])
    bs0 = sb("bs0", [16, 4])
    cosT = sb("cosT", [128, 16, 32])
    sinT = sb("sinT", [128, 16, 32])
    cosS = sb("cosS", [128, 1, 32])
    sinS = sb("sinS", [128, 1, 32])
    invf = sb("invf", [128, 32])
    kT = sb("kT", [128, SEQ], BF16)
    kidxT = sb("kidxT", [64, SEQ], BF16)
    vaug = sb("vaug", [128, 16, 2, 65], BF16)
    ybf = sb("ybf", [128, D], BF16)
    stats_ = [sb("stats%d" % i, [128, 12]) for i in range(4)]
    mv_ = [sb("mv%d" % i, [128, 2]) for i in range(4)]
    rstd_ = [sb("rstd%d" % i, [128, 1]) for i in range(4)]
    nmr_ = [sb("nmr%d" % i, [128, 1]) for i in range(4)]
    ln_ctr = [0]
    u_sb = sb("u_sb", [128, 512])
    vg = sb("vg", [128, 512])
    sg = [u_sb, vg]
    vln = sb("vln", [128, 512])
    vbf = sb("vbf", [128, 512], BF16)
    mix = sb("mix", [128, D], BF16)
    mixT = ybf[:, :].rearrange("p (k n) -> p k n", k=8)
    rq = sb("rq", [128, 10, 64], BF16)
    rqf = sb("rqf", [128, 10, 64])
    ri = sb("ri", [128, 5, 64], BF16)
    rif = sb("rif", [128, 5, 64])
    rt1 = sb("rt1", [128, 10, 32])
    rt2 = sb("rt2", [128, 10, 32])
    vf = sb("vf", [128, 128])
    wq = sb("wq", [128, 4])
    qT = sb("qT", [128, 4, 128], BF16)
    qiT = sb("qiT", [64, 4, 128], BF16)
    sc = sb("sc", [128, SEQ])
    rl = sb("rl", [128, 512])
    msk = sb("msk", [128, SEQ], BF16)
    junk = msk
    maskT = sb("maskT", [128, 16, 128], BF16)
    pT = sb("pT", [128, 16, 256], BF16)
    PTK = ["pT"] + ["pT%d" % i for i in range(16)]
    ebuf = [sb("ebuf%d" % i, [128, 256], BF16) for i in range(2)]
    lo = sb("lo", [128, 8])
    Wt = sb("Wt", [128, NBIS + 2])
    pow2 = sb("pow2", [128, NBIS + 2])
    wdt = sb("wdt", [128, 8])
    w0 = sb("w0", [128, 8])
    mid = sb("mid", [128, 8])
    cnt = sb("cnt", [128, 8])
    tsel = sb("tsel", [128, 8])
    rinv = sb("rinv", [128, 8])
    bo = sb("bo", [128, 128])
    aall = sb("aall", [16, 128])
    rep = sb("rep", [16, 128])
    wqm = sb("wqm", [16, 4])
    vfm = sb("vfm", [16, 128])
    oh = sb("oh", [16, 8])
    negfill = sb("negfill", [128, 1])
    selp = sb("selp", [16, 8, 16], BF16)
    selpf = sb("selpf", [16, 8, 16])
    hm = sb("hm", [16, 10])
    onesf = sb("onesf", [128, 128])
    ptab = sb("ptab", [128, 8], I32)
    idx4 = sb("idx4", [128, 8, 4], I32)
    idx8 = sb("idx8", [128, 8, 8], I32)
    qiTs = sb("qiTs", [128, 8, 2, 4, 2], BF16)
    riZ = sb("riZ", [16, 2, 4, 128], BF16)
    wB = sb("wB", [128, 8, 4])
    ssd = sb("ssd", [16, 4, 64])
    ss4 = sb("ss4", [16, 4])
    ss1 = sb("ss1", [16, 1])
    ssb = sb("ssb", [16, 8])
    SC = sb("SC", [128, 8, 129])
    mskS = sb("mskS", [128, 8, 129], BF16)
    cmpS = sc[:, 0:8 * 129].rearrange("p (a b) -> p a b", a=8)
    Mx = sb("Mx", [128, 8])
    MxT = sb("MxT", [8, 1])
    Mdiag = sb("Mdiag", [8, 8])
    QZ = sb("QZ", [16, 8, 128], BF16)
    Qblk = sb("Qblk", [128, 8, 8, 2], BF16)
    kTs = sb("kTs", [128, 16], BF16)
    KTself = sb("KTself", [128, 8, 128], BF16)
    Vself = sb("Vself", [128, 8, 128], BF16)
    vsbf = sb("vsbf", [16, 128], BF16)
    Ps = sb("Ps", [128, 33, 2, 8], BF16)
    Pacc = sb("Pacc", [128, 16])
    Ptmp = sb("Ptmp", [128, 16])
    den = sb("den", [16, 1])
    Osel = sb("Osel", [16, 64])
    Oexp = sb("Oexp", [16, 8, 64], BF16)
    T4 = rl[:, :].rearrange("p (s c) -> p s c", c=4)
    KIc = pT[:, 0:8, :].rearrange("p a b -> p (a b)").rearrange("p (s d) -> p s d", d=64)
    Kc = pT[:, 8:16, :].rearrange("p a b -> p (a b)").rearrange("p (s d) -> p s d", d=128)
    Vc = sc[:, 0:1024].bitcast(BF16).rearrange("p (s d) -> p s d", d=128)
    kiTc = msk[:, 0:1024].rearrange("p (a b) -> p a b", a=8)
    KTc = msk[:, 1024:2048].rearrange("p (a b) -> p a b", a=8)

    def DMA(q, out, in_, reads, writes, **kw):
        P.op(q, lambda e, out=out, in_=in_, kw=kw: e.dma_start(out=out, in_=in_, **kw),
             reads=reads, writes=writes, dma=True)

    def MM(out, lhsT, rhs, start, stop, reads, writes):
        P.op("pe", lambda e: e.matmul(out, lhsT=lhsT, rhs=rhs, start=start, stop=stop),
             reads=reads, writes=writes)

    def TR(out, in_, idn, reads, writes):
        P.op("pe", lambda e: e.transpose(out=out, in_=in_, identity=idn), reads=reads, writes=writes)

    def ACT(out, in_, func, reads, writes, bias=None, scale=None):
        kw = {}
        if bias is not None:
            kw["bias"] = bias
        if scale is not None:
            kw["scale"] = scale
        P.op("act", lambda e: e.activation(out=out, in_=in_, func=func, **kw), reads=reads, writes=writes)

    def TS(out, in0, s1, s2, op0, op1, reads, writes, accum_out=None, eng="dve"):
        kw = {}
        if op1 is not None:
            kw["op1"] = op1
        if accum_out is not None:
            kw["accum_out"] = accum_out
        P.op(eng, lambda e: e.tensor_scalar(out=out, in0=in0, scalar1=s1, scalar2=s2, op0=op0, **kw),
             reads=reads, writes=writes)

    def TT(out, in0, in1, op, reads, writes, eng="dve"):
        P.op(eng, lambda e: e.tensor_tensor(out=out, in0=in0, in1=in1, op=op), reads=reads, writes=writes)

    def STT(out, in0, scalar, in1, op0, op1, reads, writes):
        P.op("dve", lambda e: e.scalar_tensor_tensor(out=out, in0=in0, scalar=scalar, in1=in1, op0=op0, op1=op1),
             reads=reads, writes=writes)

    def CP(out, in_, reads, writes, eng="dve"):
        P.op(eng, lambda e: e.tensor_copy(out=out, in_=in_), reads=reads, writes=writes)

    def RED(out, in_, op, reads, writes, axis=AX.X, absval=None):
        P.op("dve", lambda e: e.tensor_reduce(out=out, in_=in_, axis=axis, op=op, apply_absolute_value=absval),
             reads=reads, writes=writes)

    def MEMSET(ap, val, writes, eng="dve"):
        P.op(eng, lambda e: e.memset(ap, val), writes=writes)

    def bc(ap, shape):
        return ap.to_broadcast(list(shape))

    def load_ln(gn, bn):
        DMA("sp", gcur[:], lnp[gn].partition_broadcast(128), [], ["gcur"])
        DMA("sp", bcur[:], lnp[bn].partition_broadcast(128), [], ["bcur"])
    DMA("sp", agB[:], a_ln_g.partition_broadcast(128), [], ["agB"])
    DMA("sp", abB[:], a_ln_b.partition_broadcast(128), [], ["abB"])
    DMA("sp", invf[:], c_invf, [], ["invf"])
    DMA("sp", wsf[:], a_ws.rearrange("g t s -> t g s"), [], ["wsf"])
    DMA("sp", bsT[:], a_bs.rearrange("g t -> t g"), [], ["bsT"], allow_slow_non_contiguous=True)
    DMA("sp", ws00[:], a_ws[:, 0, 0:1].rearrange("g o -> o g").partition_broadcast(16), [], ["ws00"],
        allow_slow_non_contiguous=True)
    DMA("sp", bs0[:], a_bs[:, 0:1].rearrange("g o -> o g").partition_broadcast(16), [], ["bs0"],
        allow_slow_non_contiguous=True)
    DMA("sp", bo[:], c_bo, [], ["bo"])
    DMA("sp", aall[:], c_aall, [], ["aall"])
    DMA("sp", rep[:], c_rep, [], ["rep"])
    DMA("sp", oh[:], c_oh, [], ["oh"])
    DMA("sp", negfill[:], c_negfill, [], ["negfill"])
    DMA("sp", selpf[:].rearrange("p a b -> p (a b)"), c_selp, [], ["selpf"])
    DMA("sp", hm[:], c_hm, [], ["hm"])
    DMA("sp", ptab[:], page_table.rearrange("(i p) o -> p (i o)", p=128), [], ["ptab"],
        allow_slow_non_contiguous=True)
    CP(selp[:], selpf[:], ["selpf"], ["selp"])
    for c in range(4):
        TS(idx4[:, :, c], ptab[:, :], 4.0, float(c), ALU.mult, ALU.add, ["ptab"], ["idx4"])
    for c in range(8):
        TS(idx8[:, :, c], ptab[:, :], 8.0, float(c), ALU.mult, ALU.add, ["ptab"], ["idx8"])
    MEMSET(onesf[:], 1.0, ["onesf"])
    for it in range(NBIS + 2):
        MEMSET(pow2[:, it:it + 1], 0.5 ** it, ["pow2"])
    MEMSET(identf[:], 0.0, ["identf"], eng="pool")
    P.op("pool", lambda e: e.affine_select(out=identf[:], in_=identf[:], pattern=[[-1, 128]], compare_op=ALU.not_equal,
                                           fill=1.0, base=0, channel_multiplier=1), reads=["identf"], writes=["identf"])
    CP(ident[:], identf[:], ["identf"], ["ident"], eng="pool")
    for g in range(4):
        P.op("pool", lambda e, g=g: e.affine_select(out=wsf[:, g, :], in_=wsf[:, g, :], pattern=[[-1, 128]],
                                                    compare_op=ALU.is_ge, fill=0.0, base=0, channel_multiplier=1),
             reads=["wsf"], writes=["wsf"])
    CP(wsb[:], wsf[:], ["wsf"], ["wsb"], eng="pool")
    for g in range(4):
        TR(psb(7)[:, g * 128:(g + 1) * 128], wsb[:, g, :], ident[:], ["wsb", "ident"], [pk(7)])
    CP(wsT[:].rearrange("p g t -> p (g t)"), psb(7)[:, 0:512], [pk(7)], ["wsT"])

    posi = sb("posi", [128, 16], I32)
    posf = sb("posf", [128, 16])
    _pad_top = sb("pad_top", [128, 64])
    ang = sc[:, 0:512].rearrange("p (a b) -> p a b", a=16)
    kq = sc[:, 512:1024].rearrange("p (a b) -> p a b", a=16)
    rr = sc[:, 1024:1536].rearrange("p (a b) -> p a b", a=16)
    m1 = sc[:, 1536:2048].rearrange("p (a b) -> p a b", a=16)
    kqi = msk[:, 0:1024].bitcast(I32).rearrange("p (a b) -> p a b", a=16)
    TWO_PI = 2.0 * np.pi
    C1 = 6.28125
    C2 = TWO_PI - C1

    def sincos(posf_ap, nj, sin_out, cos_out, tag):
        a = ang[:, 0:nj, :]
        k_ = kq[:, 0:nj, :]
        ki_ = kqi[:, 0:nj, :]
        r_ = rr[:, 0:nj, :]
        m_ = m1[:, 0:nj, :]
        TT(a, bc(posf_ap.unsqueeze(2), [128, nj, 32]), bc(invf[:].unsqueeze(1), [128, nj, 32]), ALU.mult,
           ["posf", "invf"], ["sc"])
        if debug and nj == 16:
            dbg_ang = dout("dbg_ang", [128, 512])
            DMA("sp", dbg_ang[:, :], sc[:, 0:512], ["sc"], [])
        TS(k_, a, 1.0 / TWO_PI, None, ALU.mult, None, ["sc"], ["sc"])
        CP(ki_, k_, ["sc"], ["msk"])
        CP(k_, ki_, ["msk"], ["sc"])
        STT(r_, k_, -C1, a, ALU.mult, ALU.add, ["sc", "sc"], ["sc"])
        STT(r_, k_, -C2, r_, ALU.mult, ALU.add, ["sc", "sc"], ["sc"])

        def wrap():
            TS(m_, r_, np.pi, -TWO_PI, ALU.is_gt, ALU.mult, ["sc"], ["sc"])
            TT(r_, r_, m_, ALU.add, ["sc", "sc"], ["sc"])
            TS(m_, r_, -np.pi, TWO_PI, ALU.is_lt, ALU.mult, ["sc"], ["sc"])
            TT(r_, r_, m_, ALU.add, ["sc", "sc"], ["sc"])
            TS(r_, r_, np.pi, -np.pi, ALU.min, ALU.max, ["sc"], ["sc"])
        wrap()
        ACT(sin_out, r_, AF.Sin, ["sc"], [tag + "sin"])
        TS(r_, r_, np.pi / 2, None, ALU.add, None, ["sc"], ["sc"])
        wrap()
        ACT(cos_out, r_, AF.Sin, ["sc"], [tag + "cos"])

    P.op("pool", lambda e: e.iota(posi[:], pattern=[[128, 16]], base=0, channel_multiplier=1), writes=["posi"])
    CP(posf[:], posi[:], ["posi"], ["posf"])
    if debug:
        dbg_small = dout("dbg_small", [128, 16])
        DMA("sp", dbg_small[:, :], posf[:], ["posf"], [])
    sincos(posf[:], 16, sinT[:], cosT[:], "T")
    MEMSET(posf[:, 0:1], 8192.0, ["posf"])
    sincos(posf[:, 0:1], 1, sinS[:], cosS[:], "S")

    def layernorm(zin, rows, width, gtile, btile, yout, zkeys, ykeys, gk, bk):
        par = ln_ctr[0] % 4
        ln_ctr[0] += 1
        stats, mv, rstd, nmr = stats_[par], mv_[par], rstd_[par], nmr_[par]
        ks, km, kr, kn = "stats%d" % par, "mv%d" % par, "rstd%d" % par, "nmr%d" % par
        nchunk = width // 512
        for c in range(nchunk):
            P.op("dve", lambda e, c=c: e.bn_stats(out=stats[:rows, c * 6:(c + 1) * 6], in_=zin[:, c * 512:(c + 1) * 512]),
                 reads=zkeys, writes=[ks])
        P.op("dve", lambda e: e.bn_aggr(out=mv[:rows, :], in_=stats[:rows, 0:6 * nchunk]), reads=[ks], writes=[km])
        TS(rstd[:rows, :], mv[:rows, 1:2], EPS, None, ALU.add, None, [km], [kr])
        ACT(rstd[:rows, :], rstd[:rows, :], AF.Sqrt, [kr], [kr])
        P.op("dve", lambda e: e.reciprocal(out=rstd[:rows, :], in_=rstd[:rows, :]), reads=[kr], writes=[kr])
        STT(nmr[:rows, :], mv[:rows, 0:1], -1.0, rstd[:rows, :], ALU.mult, ALU.mult, [km, kr], [kn])
        ACT(yout, zin, AF.Identity, list(zkeys) + [kr, kn], ykeys, bias=nmr[:rows, :], scale=rstd[:rows, :])
        TT(yout, yout, gtile[:rows, 0:width], ALU.mult, list(ykeys) + [gk], ykeys)
        TT(yout, yout, btile[:rows, 0:width], ALU.add, list(ykeys) + [bk], ykeys)

    def ln_batch(items, gtile, btile, gk, bk, post):
        n = len(items)
        sl = []
        for i in range(n):
            par = ln_ctr[0] % 4
            ln_ctr[0] += 1
            sl.append(par)

        def A(i):
            z, rows, keys = items[i]
            par = sl[i]
            stats, mv, rstd = stats_[par], mv_[par], rstd_[par]
            for c in range(2):
                P.op("dve", lambda e, c=c: e.bn_stats(out=stats[:rows, c * 6:(c + 1) * 6], in_=z[:, c * 512:(c + 1) * 512]),
                     reads=keys, writes=["stats%d" % par])
            P.op("dve", lambda e: e.bn_aggr(out=mv[:rows, :], in_=stats[:rows, 0:12]), reads=["stats%d" % par],
                 writes=["mv%d" % par])
            TS(rstd[:rows, :], mv[:rows, 1:2], EPS, None, ALU.add, None, ["mv%d" % par], ["rstd%d" % par])
            ACT(rstd[:rows, :], rstd[:rows, :], AF.Sqrt, ["rstd%d" % par], ["rstd%d" % par])

        def B(i):
            z, rows, keys = items[i]
            par = sl[i]
            mv, rstd, nmr = mv_[par], rstd_[par], nmr_[par]
            P.op("dve", lambda e: e.reciprocal(out=rstd[:rows, :], in_=rstd[:rows, :]), reads=["rstd%d" % par],
                 writes=["rstd%d" % par])
            STT(nmr[:rows, :], mv[:rows, 0:1], -1.0, rstd[:rows, :], ALU.mult, ALU.mult, ["mv%d" % par, "rstd%d" % par],
                ["nmr%d" % par])
            ACT(z, z, AF.Identity, list(keys) + ["rstd%d" % par, "nmr%d" % par], keys, bias=nmr[:rows, :], scale=rstd[:rows, :])

        def C(i):
            z, rows, keys = items[i]
            TT(z, z, gtile[:rows, 0:D], ALU.mult, list(keys) + [gk], keys)
            TT(z, z, btile[:rows, 0:D], ALU.add, list(keys) + [bk], keys)
            post(i)

        for step in range(n + 2):
            if step < n:
                A(step)
            if 0 <= step - 1 < n:
                B(step - 1)
            if 0 <= step - 2 < n:
                C(step - 2)

    def to_xT(t, rows, c0, pbank):
        CP(ybf[:rows, :], x_res[:rows, t, :], ["xr%d" % t], ["ybf"], eng="act_copy")
        for kc in range(8):
            TR(psb(pbank)[:, kc * 128:kc * 128 + rows], ybf[:rows, kc * 128:(kc + 1) * 128], ident[:rows, :rows],
               ["ybf", "ident"], [pk(pbank)])
        CP(xT[:, :, c0:c0 + rows], psb(pbank).rearrange("p (k n) -> p k n", k=8)[:, :, 0:rows], [pk(pbank)], ["xT%d" % t])

    _CP = CP

    def CP(out, in_, reads, writes, eng="dve"):
        if eng == "act_copy":
            P.op("act", lambda e: e.copy(out=out, in_=in_), reads=reads, writes=writes)
        else:
            _CP(out, in_, reads, writes, eng=eng)

    def ffn(fi, tiles, ntok, g_name, b_name, final):
        segs = []
        c = 0
        while c < ntok:
            n = min(512, ntok - c)
            segs.append((c, n))
            c += n
        groups = []
        for s, (cs0, ncs) in enumerate(SLICES):
            c = 0
            while c < ncs:
                ng = min(WG, ncs - c)
                groups.append((s, c, ng, cs0 + c))
                c += ng
        issued = [0]

        def issue_group(extra):
            k = issued[0]
            if k >= len(groups):
                return
            issued[0] += 1
            _, _, ng, cc0 = groups[k]
            b = k % 2
            for half in range(2):
                col0 = half * FF + cc0 * 128
                DMA("pool", wu[b][:, :, half, 0:ng * 128],
                    w_up_d[fi][:, col0:col0 + ng * 128].rearrange("(k p) n -> p k n", p=128),
                    [], ["wu%d" % b] + extra)

        gk = 0
        for s, (cs0, ncs) in enumerate(SLICES):
            if s == 0:
                DMA("pool", wd[:, 0:ncs, :], w_dn_d[fi][cs0 * 128:(cs0 + ncs) * 128, :].rearrange("(c p) n -> p c n", p=128),
                    [], ["wd", "w_in", "w_out"])
                issue_group(["w_in", "w_out"])
                issue_group(["w_in", "w_out"])
            else:
                DMA("pool", wd[:, 0:ncs, :], w_dn_d[fi][cs0 * 128:(cs0 + ncs) * 128, :].rearrange("(c p) n -> p c n", p=128),
                    [], ["wd"])
            while gk < len(groups) and groups[gk][0] == s:
                _, c, ng, cc0 = groups[gk]
                b = gk % 2
                wk = "wu%d" % b
                for cl in range(ng):
                    for si, (c0, n) in enumerate(segs):
                        pg, pu = (0, 1) if (si % 2 == 0) else (2, 3)
                        for kc in range(8):
                            MM(psf[pg][:, 0:n], wu[b][:, kc, 0, cl * 128:(cl + 1) * 128], xT[:, kc, c0:c0 + n], kc == 0, kc == 7,
                               [wk, "xTall"], [pk(pg)])
                        for kc in range(8):
                            MM(psf[pu][:, 0:n], wu[b][:, kc, 1, cl * 128:(cl + 1) * 128], xT[:, kc, c0:c0 + n], kc == 0, kc == 7,
                               [wk, "xTall"], [pk(pu)])
                        sgb = sg[si % 2]
                        ACT(sgb[:, 0:n], psf[pg][:, 0:n], AF.Silu, [pk(pg)], [("u_sb", "vg")[si % 2]])
                        STT(hT[:, c + cl, c0:c0 + n], sgb[:, 0:n], 0.5, psf[pu][:, 0:n], ALU.mult, ALU.mult,
                            [("u_sb", "vg")[si % 2], pk(pu)], ["hT"])
                gk += 1
                issue_group([])
            for ti, (t, rows, c0) in enumerate(tiles):
                pb = (4, 5) if ti % 2 == 0 else (6, 7)
                for half in range(2):
                    for c in range(ncs):
                        MM(psf[pb[half]][:rows, :], hT[:, c, c0:c0 + rows], wd[:, c, half * 512:(half + 1) * 512],
                           c == 0, c == ncs - 1, ["hT", "wd"], [pk(pb[half])])
                    xs = x_res[:rows, t, half * 512:(half + 1) * 512]
                    if s == 0:
                        STT(xs, xs, ALPHA, psf[pb[half]][:rows, :], ALU.mult, ALU.add,
                            ["xr%d" % t, pk(pb[half])], ["xr%d" % t])
                    else:
                        TT(xs, xs, psf[pb[half]][:rows, :], ALU.add, ["xr%d" % t, pk(pb[half])], ["xr%d" % t])
        load_ln(g_name, b_name)
        def post(i):
            t, rows, c0 = tiles[i]
            if final is None:
                to_xT(t, rows, c0, 4 if (t % 2 == 0) else 6)
            else:
                final(t, rows)
        ln_batch([(x_res[:rows, t, :], rows, ["xr%d" % t]) for (t, rows, c0) in tiles], gcur, bcur, "gcur", "bcur", post)

    _op = P.op

    def op_bridge(eng, fn, reads=(), writes=(), dma=False):
        writes = list(writes)
        if any(k.startswith("xT") and k != "xTall" for k in writes):
            writes.append("xTall")
        return _op(eng, fn, reads=reads, writes=writes, dma=dma)
    P.op = op_bridge

    def rope(src, nh, rows, cos_t, sin_t, outb, outf, skeys, okeys):
        cb = bc(cos_t.unsqueeze(1), [rows, nh, 32])
        sbb = bc(sin_t.unsqueeze(1), [rows, nh, 32])
        x1 = src[:, :, 0:32]
        x2 = src[:, :, 32:64]
        a = rt1[:rows, 0:nh, :]
        b = rt2[:rows, 0:nh, :]
        TT(a, x1, cb, ALU.mult, list(skeys) + ["Tcos", "Scos"], ["rt1"])
        TT(b, x2, sbb, ALU.mult, list(skeys) + ["Tsin", "Ssin"], ["rt2"])
        TT(outf[:, :, 0:32], a, b, ALU.subtract, ["rt1", "rt2"], okeys)
        TT(a, x2, cb, ALU.mult, list(skeys) + ["Tcos", "Scos"], ["rt1"])
        TT(b, x1, sbb, ALU.mult, list(skeys) + ["Tsin", "Ssin"], ["rt2"])
        TT(outf[:, :, 32:64], a, b, ALU.add, ["rt1", "rt2"], okeys)
        CP(outb, outf, okeys, okeys)

    def mixer_common(t, rows, c0, cos_t, sin_t):
        for nb in range(5):
            w0c = nb * 512
            wn = min(512, INW - w0c)
            for kc in range(8):
                MM(psf[nb][:rows, 0:wn], xT[:, kc, c0:c0 + rows], w_in[:, kc, w0c:w0c + wn], kc == 0, kc == 7,
                   ["xT%d" % t, "w_in"], [pk(nb)])
        ACT(u_sb[:rows, :], psf[0][:rows, :], AF.Gelu_apprx_tanh, [pk(0)], ["u_sb"])
        ACT(vg[:rows, :], psf[1][:rows, :], AF.Gelu_apprx_tanh, [pk(1)], ["vg"])
        layernorm(vg[:rows, :], rows, 512, agB, abB, vln[:rows, :], ["vg"], ["vln"], "agB", "abB")
        CP(vbf[:rows, :], vln[:rows, :], ["vln"], ["vbf"])
        for g in range(2):
            rope(psf[2][:rows, g * 256:(g + 1) * 256].rearrange("p (h d) -> p h d", d=64), 4, rows, cos_t, sin_t,
                 rq[:rows, g:8:2, :], rqf[:rows, g:8:2, :], [pk(2)], ["rq"])
        rope(psf[3][:rows, 0:128].rearrange("p (h d) -> p h d", d=64), 2, rows, cos_t, sin_t,
             rq[:rows, 8:10, :], rqf[:rows, 8:10, :], [pk(3)], ["rq"])
        rope(psf[3][:rows, 256:512].rearrange("p (h d) -> p h d", d=64), 4, rows, cos_t, sin_t,
             ri[:rows, 0:4, :], rif[:rows, 0:4, :], [pk(3)], ["ri"])
        rope(psf[4][:rows, 0:64].rearrange("p (h d) -> p h d", d=64), 1, rows, cos_t, sin_t,
             ri[:rows, 4:5, :], rif[:rows, 4:5, :], [pk(4)], ["ri"])
        CP(vf[:rows, :], psf[3][:rows, 128:256], [pk(3)], ["vf"], eng="act_copy")
        TS(wq[:rows, :], psf[4][:rows, 64:68], IDX_W_SCALE, None, ALU.mult, None, [pk(4)], ["wq"])

    def out_proj_ln2(t, rows, c0):
        for kc in range(8):
            TR(psb(5)[:, kc * 128:kc * 128 + rows], mix[:rows, kc * 128:(kc + 1) * 128], ident[:rows, :rows],
               ["mix", "ident"], [pk(5)])
        CP(mixT[:, :, 0:rows], psb(5).rearrange("p (k n) -> p k n", k=8)[:, :, 0:rows], [pk(5)], ["ybf"])
        for half in range(2):
            for kc in range(8):
                MM(psf[half][:rows, :], mixT[:, kc, 0:rows], w_out[:, kc, half * 512:(half + 1) * 512], kc == 0, kc == 7,
                   ["ybf", "w_out"], [pk(half)])
            xs = x_res[:rows, t, half * 512:(half + 1) * 512]
            STT(xs, xs, ALPHA, psf[half][:rows, :], ALU.mult, ALU.add, ["xr%d" % t, pk(half)], ["xr%d" % t])
        layernorm(x_res[:rows, t, :], rows, D, gcur, bcur, x_res[:rows, t, :],
                  ["xr%d" % t], ["xr%d" % t], "gcur", "bcur")
        pending_xT.append((t, rows, c0))

    pending_xT = []

    def flush_xT():
        while pending_xT:
            t_, rows_, c0_ = pending_xT.pop(0)
            to_xT(t_, rows_, c0_, 6)

    def rope_parts(src, nh, rows, cos_t, sin_t, outb, outf, skeys, okeys):
        cb = bc(cos_t.unsqueeze(1), [rows, nh, 32])
        sbb = bc(sin_t.unsqueeze(1), [rows, nh, 32])
        x1 = src[:, :, 0:32]
        x2 = src[:, :, 32:64]
        a_ = rt1[:rows, 0:nh, :]
        b_ = rt2[:rows, 0:nh, :]

        def p1():
            TT(a_, x1, cb, ALU.mult, list(skeys) + ["Tcos", "Scos"], ["rt1"])
            TT(b_, x2, sbb, ALU.mult, list(skeys) + ["Tsin", "Ssin"], ["rt2"])
            TT(outf[:, :, 0:32], a_, b_, ALU.subtract, ["rt1", "rt2"], okeys)

        def p2():
            TT(a_, x2, cb, ALU.mult, list(skeys) + ["Tcos", "Scos"], ["rt1"])
            TT(b_, x1, sbb, ALU.mult, list(skeys) + ["Tsin", "Ssin"], ["rt2"])
            TT(outf[:, :, 32:64], a_, b_, ALU.add, ["rt1", "rt2"], okeys)
            CP(outb, outf, okeys, okeys)
        return [p1, p2]

    def mixer_prompt(t, c0, seq, j):
        rows = 128
        tok0 = seq * SEQ + j * 128
        cos_t = cosT[:, j, :]
        sin_t = sinT[:, j, :]
        for nb in range(5):
            w0c = nb * 512
            wn = min(512, INW - w0c)
            for kc in range(8):
                MM(psf[nb][:rows, 0:wn], xT[:, kc, c0:c0 + rows], w_in[:, kc, w0c:w0c + wn], kc == 0, kc == 7,
                   ["xT%d" % t, "w_in"], [pk(nb)])
        flush_xT()
        rope(psf[3][:rows, 256:512].rearrange("p (h d) -> p h d", d=64), 4, rows, cos_t, sin_t,
             ri[:rows, 0:4, :], rif[:rows, 0:4, :], [pk(3)], ["ri"])
        rope(psf[4][:rows, 0:64].rearrange("p (h d) -> p h d", d=64), 1, rows, cos_t, sin_t,
             ri[:rows, 4:5, :], rif[:rows, 4:5, :], [pk(4)], ["ri"])
        TS(wq[:rows, :], psf[4][:rows, 64:68], IDX_W_SCALE, None, ALU.mult, None, [pk(4)], ["wq"])
        DMA("sp", nki_p[tok0:tok0 + 128, :], rif[:, 4, :], ["ri"], [])
        for h in range(5):
            TR(psb(6)[0:64, (1 + h) * 128:(2 + h) * 128], ri[:, h, :], ident[:], ["ri", "ident"], [pk(6)])
        CP(qiT[:].rearrange("p h n -> p (h n)"), psb(6)[0:64, 128:640], [pk(6)], ["qiT"])
        CP(kidxT[:, j * 128:(j + 1) * 128], psb(6)[0:64, 640:768], [pk(6)], ["kidxT"])
        nk = (j + 1) * 128
        wdiag = vbf[:, :].rearrange("p (h n) -> p h n", h=4)
        rlb = msk[:, :].rearrange("p (h n) -> p h n", h=4)
        for h in range(4):
            TS(wdiag[:, h, :], identf[:, :], wq[:, h:h + 1], None, ALU.mult, None, ["identf", "wq"], ["vbf"])
        for k0 in range(0, nk, 512):
            n = min(512, nk - k0)
            for hp in range(2):
                for hh in range(2):
                    h = 2 * hp + hh
                    MM(psf[5 + hh][:, 0:n], qiT[:, h, :], kidxT[:, k0:k0 + n], True, True, ["qiT", "kidxT"], [pk(5 + hh)])
                for hh in range(2):
                    h = 2 * hp + hh
                    ACT(rlb[:, h, 0:n], psf[5 + hh][:, 0:n], AF.Relu, [pk(5 + hh)], ["msk"])
                for hh in range(2):
                    h = 2 * hp + hh
                    MM(psf[7][:, 0:n], wdiag[:, h, :], rlb[:, h, 0:n], h == 0, h == 3, ["vbf", "msk"], [pk(7)])
            CP(sc[:, k0:k0 + n], psf[7][:, 0:n], [pk(7)], ["sc"], eng="act_copy")
        fillers = []
        fillers.append(lambda: ACT(u_sb[:rows, :], psf[0][:rows, :], AF.Gelu_apprx_tanh, [pk(0)], ["u_sb"]))
        fillers.append(lambda: ACT(vg[:rows, :], psf[1][:rows, :], AF.Gelu_apprx_tanh, [pk(1)], ["vg"]))
        par = ln_ctr[0] % 4
        ln_ctr[0] += 1
        st_, mv2, rs_, nm_ = stats_[par], mv_[par], rstd_[par], nmr_[par]
        ks, km, kr, kn = "stats%d" % par, "mv%d" % par, "rstd%d" % par, "nmr%d" % par

        def lnA():
            P.op("dve", lambda e: e.bn_stats(out=st_[:rows, 0:6], in_=vg[:rows, :]), reads=["vg"], writes=[ks])
            P.op("dve", lambda e: e.bn_aggr(out=mv2[:rows, :], in_=st_[:rows, 0:6]), reads=[ks], writes=[km])
            TS(rs_[:rows, :], mv2[:rows, 1:2], EPS, None, ALU.add, None, [km], [kr])
            ACT(rs_[:rows, :], rs_[:rows, :], AF.Sqrt, [kr], [kr])

        def lnB():
            P.op("dve", lambda e: e.reciprocal(out=rs_[:rows, :], in_=rs_[:rows, :]), reads=[kr], writes=[kr])
            STT(nm_[:rows, :], mv2[:rows, 0:1], -1.0, rs_[:rows, :], ALU.mult, ALU.mult, [km, kr], [kn])
            ACT(vln[:rows, :], vg[:rows, :], AF.Identity, ["vg", kr, kn], ["vln"], bias=nm_[:rows, :], scale=rs_[:rows, :])

        def lnC():
            TT(vln[:rows, :], vln[:rows, :], agB[:rows, :], ALU.mult, ["vln", "agB"], ["vln"])
            TT(vln[:rows, :], vln[:rows, :], abB[:rows, :], ALU.add, ["vln", "abB"], ["vln"])
            CP(vbf[:rows, :], vln[:rows, :], ["vln"], ["vbf"])

        def gm_mm():
            for g in range(4):
                MM(psf[7][:, g * 128:(g + 1) * 128], wsT[:, g, :], vbf[:, g * 128:(g + 1) * 128], True, True,
                   ["wsT", "vbf"], [pk(7)])

        def gm_ev(g):
            STT(mix[:, g * 128:(g + 1) * 128], psf[7][:, g * 128:(g + 1) * 128], bsT[:, g:g + 1],
                u_sb[:, g * 128:(g + 1) * 128], ALU.add, ALU.mult, [pk(7), "bsT", "u_sb"], ["mix"])
        fillers += [lnA]
        rq0 = rope_parts(psf[2][:rows, 0:256].rearrange("p (h d) -> p h d", d=64), 4, rows, cos_t, sin_t,
                         rq[:rows, 0:8:2, :], rqf[:rows, 0:8:2, :], [pk(2)], ["rq"])
        rq1 = rope_parts(psf[2][:rows, 256:512].rearrange("p (h d) -> p h d", d=64), 4, rows, cos_t, sin_t,
                         rq[:rows, 1:8:2, :], rqf[:rows, 1:8:2, :], [pk(2)], ["rq"])
        rk = rope_parts(psf[3][:rows, 0:128].rearrange("p (h d) -> p h d", d=64), 2, rows, cos_t, sin_t,
                        rq[:rows, 8:10, :], rqf[:rows, 8:10, :], [pk(3)], ["rq"])
        fillers += [rq0[0], lnB, rq0[1], lnC, rq1[0], gm_mm, rq1[1]]
        fillers += [lambda: gm_ev(0), rk[0], lambda: gm_ev(1), rk[1], lambda: gm_ev(2), lambda: gm_ev(3)]

        def vstuff():
            CP(vf[:rows, :], psf[3][:rows, 128:256], [pk(3)], ["vf"], eng="act_copy")
            DMA("sp", nk_p[tok0:tok0 + 128, :], rqf[:, 8:10, :].rearrange("p h d -> p (h d)"), ["rq"], [])
            DMA("sp", nv_p[tok0:tok0 + 128, :], vf[:, :], ["vf"], [])

        def vaug_f():
            CP(vaug[:, j, :, 0:64], vf[:, :].rearrange("p (g d) -> p g d", g=2), ["vf"], ["vaug"])
            MEMSET(vaug[:, j, :, 64:65], 1.0, ["vaug"])

        def qtr():
            for hl in range(4):
                TR(psb(5)[:, hl * 128:(hl + 1) * 128], rq[:, 2 * hl:2 * hl + 2, :].rearrange("p h d -> p (h d)"), ident[:],
                   ["rq", "ident"], [pk(5)])
            CP(qT[:].rearrange("p h n -> p (h n)"), psb(5)[:, 0:512], [pk(5)], ["qT"], eng="act_copy")

        def ktr():
            TR(psb(6)[:, 0:128], rq[:, 8:10, :].rearrange("p h d -> p (h d)"), ident[:], ["rq", "ident"], [pk(6)])
            CP(kT[:, j * 128:(j + 1) * 128], psb(6)[:, 0:128], [pk(6)], ["kT"])
        fillers += [vstuff, vaug_f, qtr, ktr]

        def fill(nf=1):
            for _ in range(nf):
                if fillers:
                    fillers.pop(0)()
        P.op("pool", lambda e: e.affine_select(out=sc[:, j * 128:(j + 1) * 128], in_=sc[:, j * 128:(j + 1) * 128],
                                               pattern=[[-1, 128]], compare_op=ALU.is_ge, fill=-BIG, base=0,
                                               channel_multiplier=1), reads=["sc"], writes=["sc"])
        if j >= 2:
            nf = j * 128
            RED(lo[:, 0:1], sc[:, 0:nf], ALU.min, ["sc"], ["lo"])
            RED(w0[:, 0:1], sc[:, 0:nk], ALU.max, ["sc"], ["w0"])
            STT(w0[:, 0:1], w0[:, 0:1], 1.0, lo[:, 0:1], ALU.add, ALU.subtract, ["w0", "lo"], ["w0"])
            TS(Wt[:, :], pow2[:, :], w0[:, 0:1], None, ALU.mult, None, ["w0", "pow2"], ["Wt"])
            TT(mid[:, 0:1], lo[:, 0:1], Wt[:, 1:2], ALU.add, ["lo", "Wt"], ["mid"])
            for it in range(NBIS):
                TS(junk[:, 0:nk], sc[:, 0:nk], mid[:, 0:1], None, ALU.is_ge, ALU.add, ["sc", "mid"], ["msk", "cnt"],
                   accum_out=cnt[:, 0:1])
                fill(1)
                if it < NBIS - 1:
                    STT(tsel[:, 0:1], cnt[:, 0:1], TOPK - 0.5, Wt[:, it + 1:it + 2], ALU.is_ge, ALU.mult,
                        ["cnt", "Wt"], ["tsel"])
                    STT(mid[:, 0:1], mid[:, 0:1], Wt[:, it + 2:it + 3], tsel[:, 0:1], ALU.subtract, ALU.add,
                        ["mid", "Wt", "tsel"], ["mid"])
                else:
                    STT(tsel[:, 0:1], cnt[:, 0:1], TOPK - 0.5, Wt[:, it + 1:it + 2], ALU.is_lt, ALU.mult,
                        ["cnt", "Wt"], ["tsel"])
                    TT(lo[:, 0:1], mid[:, 0:1], tsel[:, 0:1], ALU.subtract, ["mid", "tsel"], ["lo"])
            fill(100)
            TS(msk[:, 0:nk], sc[:, 0:nk], lo[:, 0:1], None, ALU.is_ge, None, ["sc", "lo"], ["msk"])
        else:
            fill(100)
            TS(msk[:, 0:nk], sc[:, 0:nk], -1.0e29, None, ALU.is_ge, None, ["sc"], ["msk"])
        for kt in range(j + 1):
            pbk = 5 if kt < 8 else 6
            TR(psb(pbk)[:, (kt % 8) * 128:(kt % 8 + 1) * 128], msk[:, kt * 128:(kt + 1) * 128], ident[:],
               ["msk", "ident"], [pk(pbk)])
        n1 = min(j + 1, 8)
        CP(maskT[:, 0:n1, :].rearrange("p a b -> p (a b)"), psb(5)[:, 0:n1 * 128], [pk(5)], ["maskT"])
        if j + 1 > 8:
            n2 = j + 1 - 8
            CP(maskT[:, 8:8 + n2, :].rearrange("p a b -> p (a b)"), psb(6)[:, 0:n2 * 128], [pk(6)], ["maskT"])
        for g in range(2):
            pob = 4 if g == 0 else 7
            gs = slice(64 * g, 64 * g + 64)
            for hf in range(2):
                for kt in range(j + 1):
                    pl = kt % 4
                    MM(psf[pl][:, 0:256], kT[gs, kt * 128:(kt + 1) * 128],
                       qT[gs, 2 * hf:2 * hf + 2, :].rearrange("p h n -> p (h n)"),
                       True, True, ["kT", "qT"], [pk(pl)])
                    ACT(pT[:, kt, :], psf[pl][:, 0:256], AF.Exp, [pk(pl)], ["pT%d" % kt], scale=0.125)
                    TT(pT[:, kt, :].rearrange("p (h n) -> p h n", h=2), pT[:, kt, :].rearrange("p (h n) -> p h n", h=2),
                       bc(maskT[:, kt, :].unsqueeze(1), [128, 2, 128]), ALU.mult, ["pT%d" % kt, "maskT"], ["pT%d" % kt],
                       eng=("pool" if kt % 3 == 0 else "dve"))
                for hh in range(2):
                    hl = 2 * hf + hh
                    for kt in range(j + 1):
                        MM(psf[pob][:, hl * 65:(hl + 1) * 65], pT[:, kt, hh * 128:(hh + 1) * 128], vaug[:, kt, g, :],
                           kt == 0, kt == j, ["pT%d" % kt, "vaug"], [pk(pob)])
            pov = psf[pob][:, 0:260].rearrange("p (h c) -> p h c", c=65)
            P.op("dve", lambda e, pov=pov: e.reciprocal(out=rinv[:, 0:4], in_=pov[:, :, 64]),
                 reads=[pk(pob)], writes=["rinv"])
            TT(mix[:, 512 + g * 256:512 + (g + 1) * 256].rearrange("p (h d) -> p h d", d=64), pov[:, :, 0:64],
               bc(rinv[:, 0:4].unsqueeze(2), [128, 4, 64]), ALU.mult, [pk(pob), "rinv"], ["mix"])
        out_proj_ln2(t, rows, c0)

    def mixer_sample(t, c0):
        rows = NSMP
        mixer_common(t, rows, c0, cosS[:rows, 0, :], sinS[:rows, 0, :])
        DMA("sp", nk_s[:, :], rqf[:rows, 8:10, :].rearrange("p h d -> p (h d)"), ["rq"], [])
        DMA("sp", nv_s[:, :], vf[:rows, :], ["vf"], [])
        DMA("sp", nki_s[:, :], rif[:rows, 4, :], ["ri"], [])
        DMA("sp", av_s[:, :], vln[:rows, :], ["vln"], [])
        for g in range(4):
            TS(vg[:rows, g * 128:(g + 1) * 128], vln[:rows, g * 128:(g + 1) * 128], ws00[:, g:g + 1], bs0[:, g:g + 1],
               ALU.mult, ALU.add, ["vln", "ws00", "bs0"], ["vg"])
        TT(mix[:rows, 0:512], vg[:rows, :], u_sb[:rows, :], ALU.mult, ["vg", "u_sb"], ["mix"])
        MEMSET(riZ[:, :, :, :], 0.0, ["riZ"])
        CP(riZ[:, 0, :, 0:64], ri[:rows, 0:4, :], ["ri"], ["riZ"])
        CP(riZ[:, 1, :, 64:128], ri[:rows, 0:4, :], ["ri"], ["riZ"])
        for s2 in range(2):
            for h in range(4):
                TR(psb(5)[:, (s2 * 4 + h) * 16:(s2 * 4 + h + 1) * 16], riZ[:, s2, h, :], ident[:rows, :rows],
                   ["riZ", "ident"], [pk(5)])
        for s2 in range(2):
            CP(qiTs[:, :, s2, :, :].rearrange("p i h b -> p h i b"),
               psb(5)[:, s2 * 64:(s2 + 1) * 64].rearrange("p (h i b) -> p h i b", h=4, i=8), [pk(5)], ["qiTs"])
        for i in range(8):
            TS(wqm[:, :], wq[:rows, :], oh[:, i:i + 1], None, ALU.mult, None, ["wq", "oh"], ["wqm"])
            MM(psf[6][:, i * 4:(i + 1) * 4], rep[:, :], wqm[:, :], True, True, ["rep", "wqm"], [pk(6)])
        CP(wB[:].rearrange("p a b -> p (a b)"), psf[6][:, 0:32], [pk(6)], ["wB"])
        TT(ssd[:, :, :], rif[:rows, 0:4, :], bc(rif[:rows, 4:5, :], [rows, 4, 64]), ALU.mult, ["ri"], ["ssd"])
        RED(ss4[:, :], ssd[:, :, :], ALU.add, ["ssd"], ["ss4"])
        TS(ss4[:, :], ss4[:, :], 0.0, None, ALU.max, None, ["ss4"], ["ss4"])
        TT(ss4[:, :], ss4[:, :], wq[:rows, :], ALU.mult, ["ss4", "wq"], ["ss4"])
        RED(ss1[:, :], ss4[:, :], ALU.add, ["ss4"], ["ss1"])
        TS(ssb[:, :], oh[:, :], ss1[:, 0:1], None, ALU.mult, None, ["oh", "ss1"], ["ssb"])
        MM(psf[7][:, 0:8], aall[:, :], ssb[:, :], True, True, ["aall", "ssb"], [pk(7)])
        TS(SC[:, :, 128], psf[7][:, 0:8], negfill[:, 0:1], None, ALU.add, None, [pk(7), "negfill"], ["SC"])
        for i in range(8):
            for c4 in range(4):
                P.op("pool", lambda e, i=i, c4=c4: e.indirect_dma_start(
                    out=KIc.rearrange("p s d -> p (s d)"), out_offset=None,
                    in_=cache_kidx[:, :],
                    in_offset=bass.IndirectOffsetOnAxis(ap=idx4[:, i, c4:c4 + 1], axis=0)),
                    reads=["idx4"], writes=PTK, dma=True)
                for s8 in range(2):
                    for m in range(8):
                        TR(psb(5)[:, m * 128:(m + 1) * 128],
                           KIc[:, s8 * 16 + 2 * m:s8 * 16 + 2 * m + 2, :].rearrange("p s d -> p (s d)"), ident[:],
                           PTK + ["ident"], [pk(5)])
                    CP(kiTc.rearrange("p a b -> p (a b)"), psb(5)[:, :], [pk(5)], ["msk"], eng="act_copy")
                    for m in range(8):
                        sl = c4 * 32 + s8 * 16 + 2 * m
                        pbk = 0 if sl < 64 else 1
                        MM(psf[pbk][:, (sl % 64) * 8:(sl % 64) * 8 + 16], kiTc[:, m, :],
                           qiTs[:, i, :, :, :].rearrange("p s h b -> p (s h b)"), True, True,
                           ["msk", "qiTs"], [pk(pbk)])
            for b2 in range(2):
                ps_ = slice(64 * b2, 64 * b2 + 64)
                for hb in range(2):
                    src = psf[hb][ps_, :].rearrange("p (s h b) -> p s h b", h=4, b=2)[:, :, :, b2]
                    STT(T4[ps_, hb * 64:(hb + 1) * 64, :], src, 0.0, bc(wB[ps_, i, :].unsqueeze(1), [64, 64, 4]),
                        ALU.max, ALU.mult, [pk(hb), "wB"], ["rl"])
                RED(SC[ps_, i, 0:128], T4[ps_, :, :], ALU.add, ["rl"], ["SC"])
        RED(Mx[:, :], SC[:, :, 0:128], ALU.max, ["SC"], ["Mx"], absval=True)
        P.op("pe", lambda e: e.transpose(out=psf[2][0:8, 0:128], in_=Mx[:, :], identity=identf[:]),
             reads=["Mx", "identf"], writes=[pk(2)])
        RED(MxT[:, :], psf[2][0:8, 0:128], ALU.max, [pk(2)], ["MxT"])
        TS(MxT[:, :], MxT[:, :], 1.0, None, ALU.add, None, ["MxT"], ["MxT"])
        TS(Mdiag[:, :], identf[0:8, 0:8], MxT[:, 0:1], None, ALU.mult, None, ["identf", "MxT"], ["Mdiag"])
        MM(psf[2][:, 256:264], onesf[0:8, :], Mdiag[:, :], True, True, ["onesf", "Mdiag"], [pk(2)])
        TS(lo[:, :], psf[2][:, 256:264], -1.0, None, ALU.mult, None, [pk(2)], ["lo"])
        TS(w0[:, :], psf[2][:, 256:264], 2.0, None, ALU.mult, None, [pk(2)], ["w0"])
        for it in range(NBIS):
            TS(wdt[:, :], w0[:, :], 0.5 ** (it + 1), None, ALU.mult, None, ["w0"], ["wdt"])
            TT(mid[:, :], lo[:, :], wdt[:, :], ALU.add, ["lo", "wdt"], ["mid"])
            TT(cmpS[:, :, :], SC[:, :, :], bc(mid[:, :].unsqueeze(2), [128, 8, 129]), ALU.is_ge, ["SC", "mid"], ["sc"])
            RED(cnt[:, :], cmpS[:, :, :], ALU.add, ["sc"], ["cnt"])
            MM(psf[3][:, 0:8], bo[:, :], cnt[:, :], True, True, ["bo", "cnt"], [pk(3)])
            STT(tsel[:, :], psf[3][:, 0:8], TOPK - 0.5, wdt[:, :], ALU.is_ge, ALU.mult, [pk(3), "wdt"], ["tsel"])
            TT(lo[:, :], lo[:, :], tsel[:, :], ALU.add, ["lo", "tsel"], ["lo"])
        TT(mskS[:, :, :], SC[:, :, :], bc(lo[:, :].unsqueeze(2), [128, 8, 129]), ALU.is_ge, ["SC", "lo"], ["mskS"])
        MEMSET(QZ[:, :, :], 0.0, ["QZ"])
        CP(QZ[:, 0:4, 0:64], rq[:rows, 0:8:2, :], ["rq"], ["QZ"])
        CP(QZ[:, 4:8, 64:128], rq[:rows, 1:8:2, :], ["rq"], ["QZ"])
        for h in range(8):
            TR(psb(5)[:, h * 16:(h + 1) * 16], QZ[:, h, :], ident[:rows, :rows], ["QZ", "ident"], [pk(5)])
        CP(Qblk[:].rearrange("p i h b -> p h i b"), psb(5)[:, 0:128].rearrange("p (h i b) -> p h i b", h=8, i=8), [pk(5)], ["Qblk"])
        TR(psb(6)[:, 0:16], rq[:rows, 8:10, :].rearrange("p h d -> p (h d)"), ident[:rows, :rows], ["rq", "ident"], [pk(6)])
        CP(kTs[:, :], psb(6)[:, 0:16], [pk(6)], ["kTs"])
        MEMSET(KTself[:, :, :], 0.0, ["KTself"])
        CP(KTself[:, :, 0:128:64], kTs[:, :].rearrange("p (i b) -> p i b", b=2), ["kTs"], ["KTself"])
        CP(vsbf[:, :], vf[:rows, :], ["vf"], ["vsbf"])
        for i in range(8):
            TS(vfm[:, :], vf[:rows, :], oh[:, i:i + 1], None, ALU.mult, None, ["vf", "oh"], ["vfm"])
            MM(psf[7][:, 0:128], aall[:, :], vfm[:, :], True, True, ["aall", "vfm"], [pk(7)])
            CP(Vself[:, i, :], psf[7][:, 0:128], [pk(7)], ["Vself"])
        MEMSET(Ps[:, :, :, :], 0.0, ["Ps"])
        for i in range(8):
            MEMSET(Pacc[:, :], 0.0, ["Pacc"])
            for c4 in range(9):
                ns = 16 if c4 < 8 else 1
                if c4 < 8:
                    P.op("pool", lambda e, i=i, c4=c4: e.indirect_dma_start(
                        out=Kc.rearrange("p s d -> p (s d)"), out_offset=None,
                        in_=cache_k[:, :],
                        in_offset=bass.IndirectOffsetOnAxis(ap=idx8[:, i, c4:c4 + 1], axis=0)),
                        reads=["idx8"], writes=PTK, dma=True)
                    P.op("pool", lambda e, i=i, c4=c4: e.indirect_dma_start(
                        out=Vc.rearrange("p s d -> p (s d)"), out_offset=None,
                        in_=cache_v[:, :],
                        in_offset=bass.IndirectOffsetOnAxis(ap=idx8[:, i, c4:c4 + 1], axis=0)),
                        reads=["idx8"], writes=["sc"], dma=True)
                    for s8 in range(2):
                        for s in range(8):
                            TR(psb(5)[:, s * 128:(s + 1) * 128], Kc[:, s8 * 8 + s, :], ident[:], PTK + ["ident"], [pk(5)])
                        CP(KTc.rearrange("p a b -> p (a b)"), psb(5)[:, :], [pk(5)], ["msk"], eng="act_copy")
                        for s in range(8):
                            sl = s8 * 8 + s
                            MM(psf[0][:, sl * 16:(sl + 1) * 16], KTc[:, s, :],
                               Qblk[:, i, :, :].rearrange("p h b -> p (h b)"), True, True,
                               ["msk", "Qblk"], [pk(0)])
                else:
                    MM(psf[0][:, 0:16], KTself[:, i, :], Qblk[:, i, :, :].rearrange("p h b -> p (h b)"),
                       True, True, ["KTself", "Qblk"], [pk(0)])
                for b2 in range(2):
                    ps_ = slice(64 * b2, 64 * b2 + 64)
                    src = psf[0][ps_, 0:ns * 16].rearrange("p (s h b) -> p s h b", h=8, b=2)[:, :, :, b2]
                    ACT(T4[ps_, 0:ns * 2, :].rearrange("p (s a) c -> p s (a c)", a=2), src, AF.Exp, [pk(0)], ["rl"], scale=0.125)
                    mk = mskS[ps_, i, c4 * 16:c4 * 16 + ns]
                    TT(Ps[ps_, 0:ns, b2, :], T4[ps_, 0:ns * 2, :].rearrange("p (s a) c -> p s (a c)", a=2),
                       bc(mk.unsqueeze(2), [64, ns, 8]), ALU.mult, ["rl", "mskS"], ["Ps"])
                RED(Ptmp[:, :], Ps[:, 0:ns, :, :].rearrange("p s b h -> p (b h) s"), ALU.add, ["Ps"], ["Ptmp"])
                TT(Pacc[:, :], Pacc[:, :], Ptmp[:, :], ALU.add, ["Pacc", "Ptmp"], ["Pacc"])
                for s in range(ns):
                    rhs = Vc[:, s, :] if c4 < 8 else Vself[:, i, :]
                    MM(psf[1][0:16, 0:128], Ps[:, s, :, :].rearrange("p b h -> p (b h)"), rhs,
                       (c4 == 0 and s == 0), (c4 == 8), ["Ps", "sc", "Vself"], [pk(1)])
            MM(psf[1][0:16, 128:129], Pacc[:, :], onesf[:, 0:1], True, True, ["Pacc", "onesf"], [pk(1)])
            P.op("dve", lambda e: e.reciprocal(out=den[:, :], in_=psf[1][0:16, 128:129]), reads=[pk(1)], writes=["den"])
            TS(Osel[:, :], psf[1][0:16, 0:64], hm[:, 0:1], None, ALU.mult, None, [pk(1), "hm"], ["Osel"])
            STT(Osel[:, :], psf[1][0:16, 64:128], hm[:, 1:2], Osel[:, :], ALU.mult, ALU.add, [pk(1), "hm", "Osel"], ["Osel"])
            TS(Osel[:, :], Osel[:, :], den[:, 0:1], None, ALU.mult, None, ["Osel", "den"], ["Osel"])
            TT(Oexp[:, :, :], bc(Osel[:, :].unsqueeze(1), [16, 8, 64]), bc(hm[:, 2:10].unsqueeze(2), [16, 8, 64]),
               ALU.mult, ["Osel", "hm"], ["Oexp"])
            MM(psf[2][0:16, 0:512], selp[:, i, :], Oexp[:, :, :].rearrange("p h d -> p (h d)"), i == 0, i == 7,
               ["selp", "Oexp"], [pk(2)])
        CP(mix[:rows, 512:1024], psf[2][0:16, 0:512], [pk(2)], ["mix"])
        out_proj_ln2(t, rows, c0)

    def load_w_in_out():
        wv = w_in_d.rearrange("(k p) n -> p k n", p=128)
        for (a0, a1) in ((0, 1024), (1024, 2048), (2048, INW)):
            DMA("pool", w_in[:, :, a0:a1], wv[:, :, a0:a1], [], ["w_in"] + RKEYS)
        DMA("pool", w_out, w_out_d.rearrange("(k p) n -> p k n", p=128), [], ["w_out"] + RKEYS)

    if debug:
        DMA("sp", dbg_tab[:, 0:512], cosT[:].rearrange("p a b -> p (a b)"), ["Tcos"], [])
        DMA("sp", dbg_tab[:, 512:1024], sinT[:].rearrange("p a b -> p (a b)"), ["Tsin"], [])
        DMA("sp", dbg_tab[:, 1024:1056], cosS[:].rearrange("p a b -> p (a b)"), ["Scos"], [])
        DMA("sp", dbg_tab[:, 1056:1088], sinS[:].rearrange("p a b -> p (a b)"), ["Ssin"], [])
    ngroups = NSEQ * (16 // GT)
    if ngroups_run is not None:
        ngroups = ngroups_run
    for gi in range(ngroups):
        seq = gi // (16 // GT)
        j0 = (gi % (16 // GT)) * GT
        tiles = [(t, 128, t * 128) for t in range(GT)]
        has_s = (gi == 0)
        if has_s:
            tiles.append((GT, NSMP, GT * 128))
        ntok = GT * 128 + (NSMP if has_s else 0)
        for (t, rows, c0) in tiles:
            if t < GT:
                tok0 = seq * SEQ + (j0 + t) * 128
                DMA("sp", x_res[:, t, :], x_prompt[tok0:tok0 + 128, :], [], ["xr%d" % t])
            else:
                DMA("sp", x_res[:NSMP, t, :], x_sample[:, :], [], ["xr%d" % t])
            to_xT(t, rows, c0, 4 if (t % 2 == 0) else 6)
        ffn(0, tiles, ntok, "ln1_g", "ln1_b", None)
        if debug and gi == 0:
            DMA("sp", dbg_ln1[:, :], x_res[:].rearrange("p a b -> p (a b)"), ["xr%d" % t for t in range(GT + 1)], [])
        load_w_in_out()
        load_ln("ln2_g", "ln2_b")
        for (t, rows, c0) in tiles:
            if t < GT:
                mixer_prompt(t, c0, seq, j0 + t)
            else:
                flush_xT()
                mixer_sample(t, c0)
            if debug and gi == 0:
                DMA("sp", dbg_mix[:, t * D:(t + 1) * D], mix[:, :], ["mix"], [])
                DMA("sp", dbg_sc[:, t * SEQ:(t + 1) * SEQ], sc[:, :], ["sc"], [])
                DMA("sp", dbg_msk[:, t * SEQ:(t + 1) * SEQ], msk[:, :], ["msk"], [])
        flush_xT()
        if debug and gi == 0:
            DMA("sp", dbg_ln2[:, :], x_res[:].rearrange("p a b -> p (a b)"), ["xr%d" % t for t in range(GT + 1)], [])

        def final(t, rows, seq=seq, j0=j0):
            if t < GT:
                tok0 = seq * SEQ + (j0 + t) * 128
                DMA("sp", y_prompt[tok0:tok0 + 128, :], x_res[:, t, :], ["xr%d" % t], [])
            else:
                DMA("sp", y_sample[:, :], x_res[:NSMP, t, :], ["xr%d" % t], [])
        ffn(1, tiles, ntok, "ln3_g", "ln3_b", final)

    P.emit(nc, es, None)
    es.close()
    return nc


_NC = None


def _consts():
    half = 32
    invf = (10000.0 ** (-np.arange(half, dtype=np.float32) / half)).astype(np.float32)
    c_invf = np.tile(invf[None, :], (128, 1)).astype(np.float32)
    p = np.arange(128)
    c_bo = (p[:, None] // 64 == p[None, :] // 64).astype(np.float32)
    b = np.arange(16)
    aall = np.zeros((16, 128), np.float32)
    rep = np.zeros((16, 128), np.float32)
    selp = np.zeros((16, 8, 16), np.float32)
    for bb in range(16):
        aall[bb, 64 * (bb % 2)] = 1.0
        rep[bb, 64 * (bb % 2):64 * (bb % 2) + 64] = 1.0
    for i in range(8):
        for r in range(16):
            selp[r, i, 2 * i + r // 8] = 1.0
    oh = (b[:, None] // 2 == np.arange(8)[None, :]).astype(np.float32)
    negfill = np.full((128, 1), -BIG, np.float32)
    negfill[0, 0] = 0.0
    negfill[64, 0] = 0.0
    hm = np.zeros((16, 10), np.float32)
    for r in range(16):
        h = r % 8
        hm[r, 0] = 1.0 if h < 4 else 0.0
        hm[r, 1] = 0.0 if h < 4 else 1.0
        hm[r, 2 + h] = 1.0
    return dict(c_invf=c_invf, c_bo=c_bo, c_aall=aall, c_rep=rep, c_oh=oh,
                c_negfill=negfill, c_selp=selp.reshape(16, -1), c_hm=hm)


def kernel(x_prompt, x_sample, cache_k, cache_v, cache_kidx, page_table, ln1_g, ln1_b, ffn1_w_up, ffn1_w_down,
           ln2_g, ln2_b, w_in, a_ln_g, a_ln_b, a_ws, a_bs, w_out, ln3_g, ln3_b, ffn2_w_up, ffn2_w_down):
    global _NC
    if _NC is None:
        _NC = build_nc()
    nc = _NC
    f = lambda a: np.ascontiguousarray(np.asarray(a))
    consts = _consts()
    ck = f(cache_k).reshape(10240 * 8, 2048)
    cv = f(cache_v).reshape(10240 * 8, 2048)
    cki = f(cache_kidx).reshape(10240 * 4, 2048)
    shared = dict(
        cache_k=ck, cache_v=cv, cache_kidx=cki,
        ln1_g=f(ln1_g).reshape(1, D), ln1_b=f(ln1_b).reshape(1, D), ln2_g=f(ln2_g).reshape(1, D),
        ln2_b=f(ln2_b).reshape(1, D), ln3_g=f(ln3_g).reshape(1, D), ln3_b=f(ln3_b).reshape(1, D),
        a_ln_g=f(a_ln_g).reshape(1, 512), a_ln_b=f(a_ln_b).reshape(1, 512),
        ffn1_w_up=f(ffn1_w_up).reshape(D, 2 * FF), ffn2_w_up=f(ffn2_w_up).reshape(D, 2 * FF),
        ffn1_w_down=f(ffn1_w_down).reshape(FF, D), ffn2_w_down=f(ffn2_w_down).reshape(FF, D),
        w_in=f(w_in).reshape(D, INW), w_out=f(w_out).reshape(D, D),
        a_ws=f(a_ws).reshape(4, 128, 128), a_bs=f(a_bs).reshape(4, 128), **consts)
    xp = f(x_prompt)
    xs = f(x_sample).reshape(128, D)
    pt = f(page_table).astype(np.int32)
    in_maps = []
    for c in range(8):
        m = dict(shared)
        m["x_prompt"] = xp[2 * c:2 * c + 2].reshape(NSEQ * SEQ, D)
        m["x_sample"] = xs[16 * c:16 * c + 16]
        m["page_table"] = pt[16 * c:16 * c + 16].reshape(NSMP * NPAGE, 1)
        in_maps.append(m)
    res = run_bass_kernel_spmd(nc, in_maps, core_ids=list(range(8)))
    R = res.results
    cat = lambda n: np.concatenate([r[n] for r in R], axis=0)
    y_p = cat("y_prompt").reshape(16, SEQ, D)
    y_s = cat("y_sample").reshape(128, 1, D)
    nk_p = cat("nk_p").reshape(1, 16, SEQ, 2, 64)
    nv_p = cat("nv_p").reshape(1, 16, SEQ, 2, 64)
    nki_p = cat("nki_p").reshape(1, 16, SEQ, 64)
    nk_s = cat("nk_s").reshape(1, 128, 1, 2, 64)
    nv_s = cat("nv_s").reshape(1, 128, 1, 2, 64)
    nki_s = cat("nki_s").reshape(1, 128, 1, 64)
    av_s = cat("av_s").reshape(1, 128, 1, 512)
    return (y_p, y_s, nk_p, nv_p, nki_p, nk_s, nv_s, nki_s, av_s)
```

```python
import numpy as np
from contextlib import ExitStack
import concourse.bass as bass
import concourse.mybir as mybir
from concourse.bass_utils import run_bass_kernel_spmd

F32 = mybir.dt.float32
BF16 = mybir.dt.bfloat16
I32 = mybir.dt.int32
AF = mybir.ActivationFunctionType
ALU = mybir.AluOpType
AX = mybir.AxisListType

D = 1024
FF = 2816
NFC = 22
SLICES = [(0, 6), (6, 6), (12, 5), (17, 5)]
CPS = 6
WG = 3
INW = 2116
SEQ = 2048
NSEQ = 2
NSMP = 16
ALPHA = 2.0 ** 0.25
EPS = 1e-5
IDX_W_SCALE = 256.0 ** -0.5
TOPK = 256
NBIS = 22
BIG = 1.0e30
GT = 8
NPAGE = 64


class Op:
    __slots__ = ("eng", "fn", "reads", "writes", "dma", "deps", "signal", "sem", "val", "idx", "prev_val", "raw")


class Prog:
    ENGS = ("pe", "act", "dve", "pool", "sp")

    def __init__(self):
        self.ops = []
        self.last_w = {}
        self.readers = {}

    def op(self, eng, fn, reads=(), writes=(), dma=False):
        o = Op()
        o.eng, o.fn, o.dma = eng, fn, dma
        o.reads, o.writes = tuple(reads), tuple(writes)
        o.deps = set()
        o.raw = set()
        o.signal = dma
        o.idx = len(self.ops)
        for k in o.reads:
            w = self.last_w.get(k)
            if w is not None:
                o.deps.add(w)
                o.raw.add(w)
        for k in o.writes:
            w = self.last_w.get(k)
            if w is not None:
                o.deps.add(w)
            for r in self.readers.get(k, ()):
                o.deps.add(r)
        for k in o.reads:
            self.readers.setdefault(k, []).append(o.idx)
        for k in o.writes:
            self.last_w[k] = o.idx
            self.readers[k] = []
        o.deps.discard(o.idx)
        self.ops.append(o)
        return o

    def emit(self, nc, es, out_dma_ops):
        ops = self.ops
        for o in ops:
            nd = set()
            for d in o.deps:
                p = ops[d]
                if (not p.dma) and (not o.dma) and p.eng == o.eng:
                    if o.eng == "pe" or d not in o.raw:
                        continue
                nd.add(d)
                p.signal = True
            o.deps = nd
        esem = {e: es.enter_context(nc.semaphore("S_" + e)) for e in self.ENGS}
        NDS = 12
        dsem = {e: [es.enter_context(nc.semaphore("D_%s_%d" % (e, i))) for i in range(NDS)]
                for e in ("sp", "pool", "act")}
        ecount = {e: 0 for e in self.ENGS}
        dcount = {e: [0] * NDS for e in dsem}
        dnext = {e: 0 for e in dsem}
        for o in ops:
            if o.dma:
                j = dnext[o.eng]
                dnext[o.eng] = (j + 1) % NDS
                o.prev_val = dcount[o.eng][j]
                dcount[o.eng][j] += 16
                o.sem, o.val = dsem[o.eng][j], dcount[o.eng][j]
            elif o.signal:
                ecount[o.eng] += 1
                o.sem, o.val = esem[o.eng], ecount[o.eng]
        by_eng = {e: [o for o in ops if o.eng == e] for e in self.ENGS}
        final_waits = [(o.sem, o.val) for o in ops if o.dma]
        block = es.enter_context(nc.Block())

        def run(engname, eng):
            waited = {}

            def wait(sem, val):
                key = id(sem)
                if waited.get(key, 0) >= val:
                    return
                waited[key] = val
                eng.wait_ge(sem, val)

            for o in by_eng[engname]:
                for d in sorted(o.deps):
                    p = ops[d]
                    wait(p.sem, p.val)
                if o.dma and o.prev_val > 0:
                    wait(o.sem, o.prev_val)
                ins = o.fn(eng)
                if o.dma:
                    ins.then_inc(o.sem, 16)
                elif o.signal:
                    ins.then_inc(o.sem, 1)
            if engname == "sp":
                fw = {}
                for sem, val in final_waits:
                    fw[id(sem)] = (sem, max(val, fw.get(id(sem), (None, 0))[1]))
                for sem, val in fw.values():
                    eng.wait_ge(sem, val)

        @block.tensor
        def _(e):
            run("pe", e)

        @block.scalar
        def _(e):
            run("act", e)

        @block.vector
        def _(e):
            run("dve", e)

        @block.gpsimd
        def _(e):
            run("pool", e)

        @block.sync
        def _(e):
            run("sp", e)


def build_nc(debug=False, npool=10240, ngroups_run=None):
    nc = bass.Bass("TRN2", target_bir_lowering=False)
    P = Prog()
    es = ExitStack()

    def din(name, shape, dt=F32):
        return nc.dram_tensor(name, list(shape), dt, kind="ExternalInput").ap()

    def dout(name, shape, dt=F32):
        return nc.dram_tensor(name, list(shape), dt, kind="ExternalOutput").ap()

    NTOK = NSEQ * SEQ
    x_prompt = din("x_prompt", [NTOK, D])
    x_sample = din("x_sample", [NSMP, D])
    cache_k = din("cache_k", [npool * 8, 2048])
    cache_v = din("cache_v", [npool * 8, 2048])
    cache_kidx = din("cache_kidx", [npool * 4, 2048])
    page_table = din("page_table", [NSMP * NPAGE, 1], I32)
    lnp = {n: din(n, [1, D]) for n in ("ln1_g", "ln1_b", "ln2_g", "ln2_b", "ln3_g", "ln3_b")}
    a_ln_g = din("a_ln_g", [1, 512])
    a_ln_b = din("a_ln_b", [1, 512])
    w_up_d = [din("ffn1_w_up", [D, 2 * FF]), din("ffn2_w_up", [D, 2 * FF])]
    w_dn_d = [din("ffn1_w_down", [FF, D]), din("ffn2_w_down", [FF, D])]
    w_in_d = din("w_in", [D, INW])
    w_out_d = din("w_out", [D, D])
    a_ws = din("a_ws", [4, 128, 128])
    a_bs = din("a_bs", [4, 128])
    c_invf = din("c_invf", [128, 32])
    c_bo = din("c_bo", [128, 128])
    c_aall = din("c_aall", [16, 128])
    c_rep = din("c_rep", [16, 128])
    c_oh = din("c_oh", [16, 8])
    c_negfill = din("c_negfill", [128, 1])
    c_selp = din("c_selp", [16, 8 * 16])
    c_hm = din("c_hm", [16, 10])

    y_prompt = dout("y_prompt", [NTOK, D])
    y_sample = dout("y_sample", [NSMP, D])
    nk_p = dout("nk_p", [NTOK, 128])
    nv_p = dout("nv_p", [NTOK, 128])
    nki_p = dout("nki_p", [NTOK, 64])
    nk_s = dout("nk_s", [NSMP, 128])
    nv_s = dout("nv_s", [NSMP, 128])
    nki_s = dout("nki_s", [NSMP, 64])
    av_s = dout("av_s", [NSMP, 512])
    if debug:
        dbg_tab = dout("dbg_tab", [128, 4 * 512 + 128])
        dbg_ln1 = dout("dbg_ln1", [128, (GT + 1) * D])
        dbg_ln2 = dout("dbg_ln2", [128, (GT + 1) * D])
        dbg_mix = dout("dbg_mix", [128, (GT + 1) * D], BF16)
        dbg_sc = dout("dbg_sc", [128, (GT + 1) * SEQ])
        dbg_msk = dout("dbg_msk", [128, (GT + 1) * SEQ], BF16)

    def sb(name, shape, dt=F32):
        return es.enter_context(nc.sbuf_tensor(name, list(shape), dt))

    psf = [es.enter_context(nc.psum_tensor("ps%d" % i, [128, 512], F32)) for i in range(8)]

    def psb(i):
        return psf[i][:].bitcast(BF16)

    def pk(i):
        return "ps%d" % i

    NT = GT + 1
    NTK = GT * 128 + NSMP
    x_res = sb("x_res", [128, NT, D])
    xT = sb("xT", [128, 8, NTK], BF16)
    R_bytes = max(CPS * NTK * 2 + CPS * D * 2 + 2 * 8 * 2 * WG * 128 * 2, 8 * INW * 2 + 8 * D * 2)
    Rt = sb("Rregion", [128, R_bytes // 4], F32)
    Rb = Rt[:].bitcast(BF16)
    o0 = 0
    hT = Rb[:, o0:o0 + CPS * NTK].rearrange("p (c n) -> p c n", c=CPS)
    o0 += CPS * NTK
    wd = Rb[:, o0:o0 + CPS * D].rearrange("p (c n) -> p c n", c=CPS)
    o0 += CPS * D
    WUE = 8 * 2 * WG * 128
    wu = [Rb[:, o0 + i * WUE:o0 + (i + 1) * WUE].rearrange("p (k t n) -> p k t n", k=8, t=2) for i in range(2)]
    w_in = Rb[:, 0:8 * INW].rearrange("p (k n) -> p k n", k=8)
    w_out = Rb[:, 8 * INW:8 * INW + 8 * D].rearrange("p (k n) -> p k n", k=8)
    RKEYS = ["hT", "wd", "wu0", "wu1"]

    gcur = sb("gcur", [128, D])
    bcur = sb("bcur", [128, D])
    agB = sb("agB", [128, 512])
    abB = sb("abB", [128, 512])
    ident = sb("ident", [128, 128], BF16)
    identf = sb("identf", [128, 128])
    wsT = sb("wsT", [128, 4, 128], BF16)
    wsf = sb("wsf", [128, 4, 128])
    wsb = sb("wsb", [128, 4, 128], BF16)
    bsT = sb("bsT", [128, 4])
    ws00 = sb("ws00", [16, 4])
    bs0 = sb("bs0", [16, 4])
    cosT = sb("cosT", [128, 16, 32])
    sinT = sb("sinT", [128, 16, 32])
    cosS = sb("cosS", [128, 1, 32])
    sinS = sb("sinS", [128, 1, 32])
    invf = sb("invf", [128, 32])
    kT = sb("kT", [128, SEQ], BF16)
    kidxT = sb("kidxT", [64, SEQ], BF16)
    vaug = sb("vaug", [128, 16, 2, 65], BF16)
    ybf = sb("ybf", [128, D], BF16)
    stats_ = [sb("stats%d" % i, [128, 12]) for i in range(4)]
    mv_ = [sb("mv%d" % i, [128, 2]) for i in range(4)]
    rstd_ = [sb("rstd%d" % i, [128, 1]) for i in range(4)]
    nmr_ = [sb("nmr%d" % i, [128, 1]) for i in range(4)]
    ln_ctr = [0]
    u_sb = sb("u_sb", [128, 512])
    vg = sb("vg", [128, 512])
    sg = [u_sb, vg]
    vln = sb("vln", [128, 512])
    vbf = sb("vbf", [128, 512], BF16)
    mix = sb("mix", [128, D], BF16)
    mixT = ybf[:, :].rearrange("p (k n) -> p k n", k=8)
    rq = sb("rq", [128, 10, 64], BF16)
    rqf = sb("rqf", [128, 10, 64])
    ri = sb("ri", [128, 5, 64], BF16)
    rif = sb("rif", [128, 5, 64])
    rt1 = sb("rt1", [128, 10, 32])
    rt2 = sb("rt2", [128, 10, 32])
    vf = sb("vf", [128, 128])
    wq = sb("wq", [128, 4])
    qT = sb("qT", [128, 4, 128], BF16)
    qiT = sb("qiT", [64, 4, 128], BF16)
    sc = sb("sc", [128, SEQ])
    rl = sb("rl", [128, 512])
    msk = sb("msk", [128, SEQ], BF16)
    junk = msk
    maskT = sb("maskT", [128, 16, 128], BF16)
    pT = sb("pT", [128, 16, 256], BF16)
    PTK = ["pT"] + ["pT%d" % i for i in range(16)]
    ebuf = [sb("ebuf%d" % i, [128, 256], BF16) for i in range(2)]
    lo = sb("lo", [128, 8])
    Wt = sb("Wt", [128, NBIS + 2])
    wdt = sb("wdt", [128, 8])
    w0 = sb("w0", [128, 8])
    mid = sb("mid", [128, 8])
    cnt = sb("cnt", [128, 8])
    tsel = sb("tsel", [128, 8])
    rinv = sb("rinv", [128, 8])
    bo = sb("bo", [128, 128])
    aall = sb("aall", [16, 128])
    rep = sb("rep", [16, 128])
    wqm = sb("wqm", [16, 4])
    vfm = sb("vfm", [16, 128])
    oh = sb("oh", [16, 8])
    negfill = sb("negfill", [128, 1])
    selp = sb("selp", [16, 8, 16], BF16)
    selpf = sb("selpf", [16, 8, 16])
    hm = sb("hm", [16, 10])
    onesf = sb("onesf", [128, 128])
    ptab = sb("ptab", [128, 8], I32)
    idx4 = sb("idx4", [128, 8, 4], I32)
    idx8 = sb("idx8", [128, 8, 8], I32)
    qiTs = sb("qiTs", [128, 8, 2, 4, 2], BF16)
    riZ = sb("riZ", [16, 2, 4, 128], BF16)
    wB = sb("wB", [128, 8, 4])
    ssd = sb("ssd", [16, 4, 64])
    ss4 = sb("ss4", [16, 4])
    ss1 = sb("ss1", [16, 1])
    ssb = sb("ssb", [16, 8])
    SC = sb("SC", [128, 8, 129])
    mskS = sb("mskS", [128, 8, 129], BF16)
    cmpS = sc[:, 0:8 * 129].rearrange("p (a b) -> p a b", a=8)
    Mx = sb("Mx", [128, 8])
    MxT = sb("MxT", [8, 1])
    Mdiag = sb("Mdiag", [8, 8])
    QZ = sb("QZ", [16, 8, 128], BF16)
    Qblk = sb("Qblk", [128, 8, 8, 2], BF16)
    kTs = sb("kTs", [128, 16], BF16)
    KTself = sb("KTself", [128, 8, 128], BF16)
    Vself = sb("Vself", [128, 8, 128], BF16)
    vsbf = sb("vsbf", [16, 128], BF16)
    Ps = sb("Ps", [128, 33, 2, 8], BF16)
    Pacc = sb("Pacc", [128, 16])
    Ptmp = sb("Ptmp", [128, 16])
    den = sb("den", [16, 1])
    Osel = sb("Osel", [16, 64])
    Oexp = sb("Oexp", [16, 8, 64], BF16)
    T4 = rl[:, :].rearrange("p (s c) -> p s c", c=4)
    KIc = pT[:, 0:8, :].rearrange("p a b -> p (a b)").rearrange("p (s d) -> p s d", d=64)
    Kc = pT[:, 8:16, :].rearrange("p a b -> p (a b)").rearrange("p (s d) -> p s d", d=128)
    KIc2 = [KIc, pT[:, 8:16, :].rearrange("p a b -> p (a b)").rearrange("p (s d) -> p s d", d=64)]
    Kc2 = [Kc, pT[:, 0:8, :].rearrange("p a b -> p (a b)").rearrange("p (s d) -> p s d", d=128)]
    KIK = [["pTa"], ["pTb"]]
    KCK = [["pTb"], ["pTa"]]
    bar = sb("bar", [128, 1])
    Vc = sc[:, 0:1024].bitcast(BF16).rearrange("p (s d) -> p s d", d=128)
    kiTc = msk[:, 0:1024].rearrange("p (a b) -> p a b", a=8)
    KTc = msk[:, 1024:2048].rearrange("p (a b) -> p a b", a=8)

    def DMA(q, out, in_, reads, writes, **kw):
        P.op(q, lambda e, out=out, in_=in_, kw=kw: e.dma_start(out=out, in_=in_, **kw),
             reads=reads, writes=writes, dma=True)

    def MM(out, lhsT, rhs, start, stop, reads, writes):
        P.op("pe", lambda e: e.matmul(out, lhsT=lhsT, rhs=rhs, start=start, stop=stop),
             reads=reads, writes=writes)

    def TR(out, in_, idn, reads, writes):
        P.op("pe", lambda e: e.transpose(out=out, in_=in_, identity=idn), reads=reads, writes=writes)

    def ACT(out, in_, func, reads, writes, bias=None, scale=None):
        kw = {}
        if bias is not None:
            kw["bias"] = bias
        if scale is not None:
            kw["scale"] = scale
        P.op("act", lambda e: e.activation(out=out, in_=in_, func=func, **kw), reads=reads, writes=writes)

    def TS(out, in0, s1, s2, op0, op1, reads, writes, accum_out=None, eng="dve"):
        kw = {}
        if op1 is not None:
            kw["op1"] = op1
        if accum_out is not None:
            kw["accum_out"] = accum_out
        P.op(eng, lambda e: e.tensor_scalar(out=out, in0=in0, scalar1=s1, scalar2=s2, op0=op0, **kw),
             reads=reads, writes=writes)

    def TT(out, in0, in1, op, reads, writes, eng="dve"):
        P.op(eng, lambda e: e.tensor_tensor(out=out, in0=in0, in1=in1, op=op), reads=reads, writes=writes)

    def STT(out, in0, scalar, in1, op0, op1, reads, writes):
        P.op("dve", lambda e: e.scalar_tensor_tensor(out=out, in0=in0, scalar=scalar, in1=in1, op0=op0, op1=op1),
             reads=reads, writes=writes)

    def CP(out, in_, reads, writes, eng="dve"):
        P.op(eng, lambda e: e.tensor_copy(out=out, in_=in_), reads=reads, writes=writes)

    def RED(out, in_, op, reads, writes, axis=AX.X, absval=None):
        P.op("dve", lambda e: e.tensor_reduce(out=out, in_=in_, axis=axis, op=op, apply_absolute_value=absval),
             reads=reads, writes=writes)

    def MEMSET(ap, val, writes, eng="dve"):
        P.op(eng, lambda e: e.memset(ap, val), writes=writes)

    def bc(ap, shape):
        return ap.to_broadcast(list(shape))

    def load_ln(gn, bn):
        DMA("sp", gcur[:], lnp[gn].partition_broadcast(128), [], ["gcur"])
        DMA("sp", bcur[:], lnp[bn].partition_broadcast(128), [], ["bcur"])
    DMA("sp", agB[:], a_ln_g.partition_broadcast(128), [], ["agB"])
    DMA("sp", abB[:], a_ln_b.partition_broadcast(128), [], ["abB"])
    DMA("sp", invf[:], c_invf, [], ["invf"])
    DMA("sp", wsf[:], a_ws.rearrange("g t s -> t g s"), [], ["wsf"])
    DMA("sp", bsT[:], a_bs.rearrange("g t -> t g"), [], ["bsT"], allow_slow_non_contiguous=True)
    DMA("sp", ws00[:], a_ws[:, 0, 0:1].rearrange("g o -> o g").partition_broadcast(16), [], ["ws00"],
        allow_slow_non_contiguous=True)
    DMA("sp", bs0[:], a_bs[:, 0:1].rearrange("g o -> o g").partition_broadcast(16), [], ["bs0"],
        allow_slow_non_contiguous=True)
    DMA("sp", bo[:], c_bo, [], ["bo"])
    DMA("sp", aall[:], c_aall, [], ["aall"])
    DMA("sp", rep[:], c_rep, [], ["rep"])
    DMA("sp", oh[:], c_oh, [], ["oh"])
    DMA("sp", negfill[:], c_negfill, [], ["negfill"])
    DMA("sp", selpf[:].rearrange("p a b -> p (a b)"), c_selp, [], ["selpf"])
    DMA("sp", hm[:], c_hm, [], ["hm"])
    DMA("sp", ptab[:], page_table.rearrange("(i p) o -> p (i o)", p=128), [], ["ptab"],
        allow_slow_non_contiguous=True)
    CP(selp[:], selpf[:], ["selpf"], ["selp"])
    for c in range(4):
        TS(idx4[:, :, c], ptab[:, :], 4.0, float(c), ALU.mult, ALU.add, ["ptab"], ["idx4"])
    for c in range(8):
        TS(idx8[:, :, c], ptab[:, :], 8.0, float(c), ALU.mult, ALU.add, ["ptab"], ["idx8"])
    MEMSET(onesf[:], 1.0, ["onesf"])
    MEMSET(identf[:], 0.0, ["identf"], eng="pool")
    P.op("pool", lambda e: e.affine_select(out=identf[:], in_=identf[:], pattern=[[-1, 128]], compare_op=ALU.not_equal,
                                           fill=1.0, base=0, channel_multiplier=1), reads=["identf"], writes=["identf"])
    CP(ident[:], identf[:], ["identf"], ["ident"], eng="pool")
    for g in range(4):
        P.op("pool", lambda e, g=g: e.affine_select(out=wsf[:, g, :], in_=wsf[:, g, :], pattern=[[-1, 128]],
                                                    compare_op=ALU.is_ge, fill=0.0, base=0, channel_multiplier=1),
             reads=["wsf"], writes=["wsf"])
    CP(wsb[:], wsf[:], ["wsf"], ["wsb"], eng="pool")
    for g in range(4):
        TR(psb(7)[:, g * 128:(g + 1) * 128], wsb[:, g, :], ident[:], ["wsb", "ident"], [pk(7)])
    CP(wsT[:].rearrange("p g t -> p (g t)"), psb(7)[:, 0:512], [pk(7)], ["wsT"])

    posi = sb("posi", [128, 16], I32)
    posf = sb("posf", [128, 16])
    _pad_top = sb("pad_top", [128, 64])
    ang = sc[:, 0:512].rearrange("p (a b) -> p a b", a=16)
    kq = sc[:, 512:1024].rearrange("p (a b) -> p a b", a=16)
    rr = sc[:, 1024:1536].rearrange("p (a b) -> p a b", a=16)
    m1 = sc[:, 1536:2048].rearrange("p (a b) -> p a b", a=16)
    kqi = msk[:, 0:1024].bitcast(I32).rearrange("p (a b) -> p a b", a=16)
    TWO_PI = 2.0 * np.pi
    C1 = 6.28125
    C2 = TWO_PI - C1

    def sincos(posf_ap, nj, sin_out, cos_out, tag):
        a = ang[:, 0:nj, :]
        k_ = kq[:, 0:nj, :]
        ki_ = kqi[:, 0:nj, :]
        r_ = rr[:, 0:nj, :]
        m_ = m1[:, 0:nj, :]
        TT(a, bc(posf_ap.unsqueeze(2), [128, nj, 32]), bc(invf[:].unsqueeze(1), [128, nj, 32]), ALU.mult,
           ["posf", "invf"], ["sc"])
        if debug and nj == 16:
            dbg_ang = dout("dbg_ang", [128, 512])
            DMA("sp", dbg_ang[:, :], sc[:, 0:512], ["sc"], [])
        TS(k_, a, 1.0 / TWO_PI, None, ALU.mult, None, ["sc"], ["sc"])
        CP(ki_, k_, ["sc"], ["msk"])
        CP(k_, ki_, ["msk"], ["sc"])
        STT(r_, k_, -C1, a, ALU.mult, ALU.add, ["sc", "sc"], ["sc"])
        STT(r_, k_, -C2, r_, ALU.mult, ALU.add, ["sc", "sc"], ["sc"])

        def wrap():
            TS(m_, r_, np.pi, -TWO_PI, ALU.is_gt, ALU.mult, ["sc"], ["sc"])
            TT(r_, r_, m_, ALU.add, ["sc", "sc"], ["sc"])
            TS(m_, r_, -np.pi, TWO_PI, ALU.is_lt, ALU.mult, ["sc"], ["sc"])
            TT(r_, r_, m_, ALU.add, ["sc", "sc"], ["sc"])
            TS(r_, r_, np.pi, -np.pi, ALU.min, ALU.max, ["sc"], ["sc"])
        wrap()
        ACT(sin_out, r_, AF.Sin, ["sc"], [tag + "sin"])
        TS(r_, r_, np.pi / 2, None, ALU.add, None, ["sc"], ["sc"])
        wrap()
        ACT(cos_out, r_, AF.Sin, ["sc"], [tag + "cos"])

    P.op("pool", lambda e: e.iota(posi[:], pattern=[[128, 16]], base=0, channel_multiplier=1), writes=["posi"])
    CP(posf[:], posi[:], ["posi"], ["posf"])
    if debug:
        dbg_small = dout("dbg_small", [128, 16])
        DMA("sp", dbg_small[:, :], posf[:], ["posf"], [])
    sincos(posf[:], 16, sinT[:], cosT[:], "T")
    MEMSET(posf[:, 0:1], 8192.0, ["posf"])
    sincos(posf[:, 0:1], 1, sinS[:], cosS[:], "S")

    def layernorm(zin, rows, width, gtile, btile, yout, zkeys, ykeys, gk, bk):
        par = ln_ctr[0] % 4
        ln_ctr[0] += 1
        stats, mv, rstd, nmr = stats_[par], mv_[par], rstd_[par], nmr_[par]
        ks, km, kr, kn = "stats%d" % par, "mv%d" % par, "rstd%d" % par, "nmr%d" % par
        nchunk = width // 512
        for c in range(nchunk):
            P.op("dve", lambda e, c=c: e.bn_stats(out=stats[:rows, c * 6:(c + 1) * 6], in_=zin[:, c * 512:(c + 1) * 512]),
                 reads=zkeys, writes=[ks])
        P.op("dve", lambda e: e.bn_aggr(out=mv[:rows, :], in_=stats[:rows, 0:6 * nchunk]), reads=[ks], writes=[km])
        TS(rstd[:rows, :], mv[:rows, 1:2], EPS, None, ALU.add, None, [km], [kr])
        ACT(rstd[:rows, :], rstd[:rows, :], AF.Sqrt, [kr], [kr])
        P.op("dve", lambda e: e.reciprocal(out=rstd[:rows, :], in_=rstd[:rows, :]), reads=[kr], writes=[kr])
        STT(nmr[:rows, :], mv[:rows, 0:1], -1.0, rstd[:rows, :], ALU.mult, ALU.mult, [km, kr], [kn])
        ACT(yout, zin, AF.Identity, list(zkeys) + [kr, kn], ykeys, bias=nmr[:rows, :], scale=rstd[:rows, :])
        TT(yout, yout, gtile[:rows, 0:width], ALU.mult, list(ykeys) + [gk], ykeys)
        TT(yout, yout, btile[:rows, 0:width], ALU.add, list(ykeys) + [bk], ykeys)

    def ln_batch(items, gtile, btile, gk, bk, post):
        n = len(items)
        sl = []
        for i in range(n):
            par = ln_ctr[0] % 4
            ln_ctr[0] += 1
            sl.append(par)

        def A(i):
            z, rows, keys = items[i]
            par = sl[i]
            stats, mv, rstd = stats_[par], mv_[par], rstd_[par]
            for c in range(2):
                P.op("dve", lambda e, c=c: e.bn_stats(out=stats[:rows, c * 6:(c + 1) * 6], in_=z[:, c * 512:(c + 1) * 512]),
                     reads=keys, writes=["stats%d" % par])
            P.op("dve", lambda e: e.bn_aggr(out=mv[:rows, :], in_=stats[:rows, 0:12]), reads=["stats%d" % par],
                 writes=["mv%d" % par])
            TS(rstd[:rows, :], mv[:rows, 1:2], EPS, None, ALU.add, None, ["mv%d" % par], ["rstd%d" % par])
            ACT(rstd[:rows, :], rstd[:rows, :], AF.Sqrt, ["rstd%d" % par], ["rstd%d" % par])

        def B(i):
            z, rows, keys = items[i]
            par = sl[i]
            mv, rstd, nmr = mv_[par], rstd_[par], nmr_[par]
            P.op("dve", lambda e: e.reciprocal(out=rstd[:rows, :], in_=rstd[:rows, :]), reads=["rstd%d" % par],
                 writes=["rstd%d" % par])
            STT(nmr[:rows, :], mv[:rows, 0:1], -1.0, rstd[:rows, :], ALU.mult, ALU.mult, ["mv%d" % par, "rstd%d" % par],
                ["nmr%d" % par])
            ACT(z, z, AF.Identity, list(keys) + ["rstd%d" % par, "nmr%d" % par], keys, bias=nmr[:rows, :], scale=rstd[:rows, :])

        def C(i):
            z, rows, keys = items[i]
            TT(z, z, gtile[:rows, 0:D], ALU.mult, list(keys) + [gk], keys)
            TT(z, z, btile[:rows, 0:D], ALU.add, list(keys) + [bk], keys)
            post(i)

        for step in range(n + 2):
            if step < n:
                A(step)
            if 0 <= step - 1 < n:
                B(step - 1)
            if 0 <= step - 2 < n:
                C(step - 2)

    def to_xT(t, rows, c0, pbank):
        CP(ybf[:rows, :], x_res[:rows, t, :], ["xr%d" % t], ["ybf"], eng="act_copy")
        for kc in range(8):
            TR(psb(pbank)[:, kc * 128:kc * 128 + rows], ybf[:rows, kc * 128:(kc + 1) * 128], ident[:rows, :rows],
               ["ybf", "ident"], [pk(pbank)])
        CP(xT[:, :, c0:c0 + rows], psb(pbank).rearrange("p (k n) -> p k n", k=8)[:, :, 0:rows], [pk(pbank)], ["xT%d" % t])

    _CP = CP

    def CP(out, in_, reads, writes, eng="dve"):
        if eng == "act_copy":
            P.op("act", lambda e: e.copy(out=out, in_=in_), reads=reads, writes=writes)
        else:
            _CP(out, in_, reads, writes, eng=eng)

    def ffn(fi, tiles, ntok, g_name, b_name, final):
        segs = []
        c = 0
        while c < ntok:
            n = min(512, ntok - c)
            segs.append((c, n))
            c += n
        groups = []
        for s, (cs0, ncs) in enumerate(SLICES):
            c = 0
            while c < ncs:
                ng = min(WG, ncs - c)
                groups.append((s, c, ng, cs0 + c))
                c += ng
        issued = [0]

        def issue_group(extra):
            k = issued[0]
            if k >= len(groups):
                return
            issued[0] += 1
            _, _, ng, cc0 = groups[k]
            b = k % 2
            for half in range(2):
                col0 = half * FF + cc0 * 128
                DMA("pool", wu[b][:, :, half, 0:ng * 128],
                    w_up_d[fi][:, col0:col0 + ng * 128].rearrange("(k p) n -> p k n", p=128),
                    [], ["wu%d" % b] + extra)

        gk = 0
        for s, (cs0, ncs) in enumerate(SLICES):
            if s == 0:
                DMA("pool", wd[:, 0:ncs, :], w_dn_d[fi][cs0 * 128:(cs0 + ncs) * 128, :].rearrange("(c p) n -> p c n", p=128),
                    [], ["wd", "w_in", "w_out"])
                issue_group(["w_in", "w_out"])
                issue_group(["w_in", "w_out"])
            else:
                DMA("pool", wd[:, 0:ncs, :], w_dn_d[fi][cs0 * 128:(cs0 + ncs) * 128, :].rearrange("(c p) n -> p c n", p=128),
                    [], ["wd"])
            while gk < len(groups) and groups[gk][0] == s:
                _, c, ng, cc0 = groups[gk]
                b = gk % 2
                wk = "wu%d" % b
                for cl in range(ng):
                    for si, (c0, n) in enumerate(segs):
                        pg, pu = (0, 1) if (si % 2 == 0) else (2, 3)
                        for kc in range(8):
                            MM(psf[pg][:, 0:n], wu[b][:, kc, 0, cl * 128:(cl + 1) * 128], xT[:, kc, c0:c0 + n], kc == 0, kc == 7,
                               [wk, "xTall"], [pk(pg)])
                        for kc in range(8):
                            MM(psf[pu][:, 0:n], wu[b][:, kc, 1, cl * 128:(cl + 1) * 128], xT[:, kc, c0:c0 + n], kc == 0, kc == 7,
                               [wk, "xTall"], [pk(pu)])
                        sgb = sg[si % 2]
                        ACT(sgb[:, 0:n], psf[pg][:, 0:n], AF.Silu, [pk(pg)], [("u_sb", "vg")[si % 2]])
                        STT(hT[:, c + cl, c0:c0 + n], sgb[:, 0:n], 0.5, psf[pu][:, 0:n], ALU.mult, ALU.mult,
                            [("u_sb", "vg")[si % 2], pk(pu)], ["hT"])
                gk += 1
                issue_group([])
            for ti, (t, rows, c0) in enumerate(tiles):
                pb = (4, 5) if ti % 2 == 0 else (6, 7)
                for half in range(2):
                    for c in range(ncs):
                        MM(psf[pb[half]][:rows, :], hT[:, c, c0:c0 + rows], wd[:, c, half * 512:(half + 1) * 512],
                           c == 0, c == ncs - 1, ["hT", "wd"], [pk(pb[half])])
                    xs = x_res[:rows, t, half * 512:(half + 1) * 512]
                    if s == 0:
                        STT(xs, xs, ALPHA, psf[pb[half]][:rows, :], ALU.mult, ALU.add,
                            ["xr%d" % t, pk(pb[half])], ["xr%d" % t])
                    else:
                        TT(xs, xs, psf[pb[half]][:rows, :], ALU.add, ["xr%d" % t, pk(pb[half])], ["xr%d" % t])
        load_ln(g_name, b_name)
        def post(i):
            t, rows, c0 = tiles[i]
            if final is None:
                to_xT(t, rows, c0, 4 if (t % 2 == 0) else 6)
            else:
                final(t, rows)
        ln_batch([(x_res[:rows, t, :], rows, ["xr%d" % t]) for (t, rows, c0) in tiles], gcur, bcur, "gcur", "bcur", post)

    _op = P.op

    def op_bridge(eng, fn, reads=(), writes=(), dma=False):
        writes = list(writes)
        if any(k.startswith("xT") and k != "xTall" for k in writes):
            writes.append("xTall")
        return _op(eng, fn, reads=reads, writes=writes, dma=dma)
    P.op = op_bridge

    def rope(src, nh, rows, cos_t, sin_t, outb, outf, skeys, okeys):
        cb = bc(cos_t.unsqueeze(1), [rows, nh, 32])
        sbb = bc(sin_t.unsqueeze(1), [rows, nh, 32])
        x1 = src[:, :, 0:32]
        x2 = src[:, :, 32:64]
        a = rt1[:rows, 0:nh, :]
        b = rt2[:rows, 0:nh, :]
        TT(a, x1, cb, ALU.mult, list(skeys) + ["Tcos", "Scos"], ["rt1"])
        TT(b, x2, sbb, ALU.mult, list(skeys) + ["Tsin", "Ssin"], ["rt2"])
        TT(outf[:, :, 0:32], a, b, ALU.subtract, ["rt1", "rt2"], okeys)
        TT(a, x2, cb, ALU.mult, list(skeys) + ["Tcos", "Scos"], ["rt1"])
        TT(b, x1, sbb, ALU.mult, list(skeys) + ["Tsin", "Ssin"], ["rt2"])
        TT(outf[:, :, 32:64], a, b, ALU.add, ["rt1", "rt2"], okeys)
        CP(outb, outf, okeys, okeys)

    def mixer_common(t, rows, c0, cos_t, sin_t):
        for nb in range(5):
            w0c = nb * 512
            wn = min(512, INW - w0c)
            for kc in range(8):
                MM(psf[nb][:rows, 0:wn], xT[:, kc, c0:c0 + rows], w_in[:, kc, w0c:w0c + wn], kc == 0, kc == 7,
                   ["xT%d" % t, "w_in"], [pk(nb)])
        ACT(u_sb[:rows, :], psf[0][:rows, :], AF.Gelu_apprx_tanh, [pk(0)], ["u_sb"])
        ACT(vg[:rows, :], psf[1][:rows, :], AF.Gelu_apprx_tanh, [pk(1)], ["vg"])
        layernorm(vg[:rows, :], rows, 512, agB, abB, vln[:rows, :], ["vg"], ["vln"], "agB", "abB")
        CP(vbf[:rows, :], vln[:rows, :], ["vln"], ["vbf"])
        for g in range(2):
            rope(psf[2][:rows, g * 256:(g + 1) * 256].rearrange("p (h d) -> p h d", d=64), 4, rows, cos_t, sin_t,
                 rq[:rows, g:8:2, :], rqf[:rows, g:8:2, :], [pk(2)], ["rq"])
        rope(psf[3][:rows, 0:128].rearrange("p (h d) -> p h d", d=64), 2, rows, cos_t, sin_t,
             rq[:rows, 8:10, :], rqf[:rows, 8:10, :], [pk(3)], ["rq"])
        rope(psf[3][:rows, 256:512].rearrange("p (h d) -> p h d", d=64), 4, rows, cos_t, sin_t,
             ri[:rows, 0:4, :], rif[:rows, 0:4, :], [pk(3)], ["ri"])
        rope(psf[4][:rows, 0:64].rearrange("p (h d) -> p h d", d=64), 1, rows, cos_t, sin_t,
             ri[:rows, 4:5, :], rif[:rows, 4:5, :], [pk(4)], ["ri"])
        CP(vf[:rows, :], psf[3][:rows, 128:256], [pk(3)], ["vf"], eng="act_copy")
        TS(wq[:rows, :], psf[4][:rows, 64:68], IDX_W_SCALE, None, ALU.mult, None, [pk(4)], ["wq"])

    def out_proj_ln2(t, rows, c0):
        for kc in range(8):
            TR(psb(5)[:, kc * 128:kc * 128 + rows], mix[:rows, kc * 128:(kc + 1) * 128], ident[:rows, :rows],
               ["mix", "ident"], [pk(5)])
        CP(mixT[:, :, 0:rows], psb(5).rearrange("p (k n) -> p k n", k=8)[:, :, 0:rows], [pk(5)], ["ybf"])
        for half in range(2):
            for kc in range(8):
                MM(psf[half][:rows, :], mixT[:, kc, 0:rows], w_out[:, kc, half * 512:(half + 1) * 512], kc == 0, kc == 7,
                   ["ybf", "w_out"], [pk(half)])
            xs = x_res[:rows, t, half * 512:(half + 1) * 512]
            STT(xs, xs, ALPHA, psf[half][:rows, :], ALU.mult, ALU.add, ["xr%d" % t, pk(half)], ["xr%d" % t])
        layernorm(x_res[:rows, t, :], rows, D, gcur, bcur, x_res[:rows, t, :],
                  ["xr%d" % t], ["xr%d" % t], "gcur", "bcur")
        pending_xT.append((t, rows, c0))

    pending_xT = []

    def flush_xT():
        while pending_xT:
            t_, rows_, c0_ = pending_xT.pop(0)
            to_xT(t_, rows_, c0_, 6)

    def rope_parts(src, nh, rows, cos_t, sin_t, outb, outf, skeys, okeys):
        cb = bc(cos_t.unsqueeze(1), [rows, nh, 32])
        sbb = bc(sin_t.unsqueeze(1), [rows, nh, 32])
        x1 = src[:, :, 0:32]
        x2 = src[:, :, 32:64]
        a_ = rt1[:rows, 0:nh, :]
        b_ = rt2[:rows, 0:nh, :]

        def p1():
            TT(a_, x1, cb, ALU.mult, list(skeys) + ["Tcos", "Scos"], ["rt1"])
            TT(b_, x2, sbb, ALU.mult, list(skeys) + ["Tsin", "Ssin"], ["rt2"])
            TT(outf[:, :, 0:32], a_, b_, ALU.subtract, ["rt1", "rt2"], okeys)

        def p2():
            TT(a_, x2, cb, ALU.mult, list(skeys) + ["Tcos", "Scos"], ["rt1"])
            TT(b_, x1, sbb, ALU.mult, list(skeys) + ["Tsin", "Ssin"], ["rt2"])
            TT(outf[:, :, 32:64], a_, b_, ALU.add, ["rt1", "rt2"], okeys)
            CP(outb, outf, okeys, okeys)
        return [p1, p2]

    def mixer_prompt(t, c0, seq, j):
        rows = 128
        tok0 = seq * SEQ + j * 128
        cos_t = cosT[:, j, :]
        sin_t = sinT[:, j, :]
        for nb in range(5):
            w0c = nb * 512
            wn = min(512, INW - w0c)
            for kc in range(8):
                MM(psf[nb][:rows, 0:wn], xT[:, kc, c0:c0 + rows], w_in[:, kc, w0c:w0c + wn], kc == 0, kc == 7,
                   ["xT%d" % t, "w_in"], [pk(nb)])
        flush_xT()
        rope(psf[3][:rows, 256:512].rearrange("p (h d) -> p h d", d=64), 4, rows, cos_t, sin_t,
             ri[:rows, 0:4, :], rif[:rows, 0:4, :], [pk(3)], ["ri"])
        rope(psf[4][:rows, 0:64].rearrange("p (h d) -> p h d", d=64), 1, rows, cos_t, sin_t,
             ri[:rows, 4:5, :], rif[:rows, 4:5, :], [pk(4)], ["ri"])
        TS(wq[:rows, :], psf[4][:rows, 64:68], IDX_W_SCALE, None, ALU.mult, None, [pk(4)], ["wq"])
        DMA("sp", nki_p[tok0:tok0 + 128, :], rif[:, 4, :], ["ri"], [])
        for h in range(5):
            TR(psb(6)[0:64, (1 + h) * 128:(2 + h) * 128], ri[:, h, :], ident[:], ["ri", "ident"], [pk(6)])
        CP(qiT[:].rearrange("p h n -> p (h n)"), psb(6)[0:64, 128:640], [pk(6)], ["qiT"])
        CP(kidxT[:, j * 128:(j + 1) * 128], psb(6)[0:64, 640:768], [pk(6)], ["kidxT"])
        nk = (j + 1) * 128
        wdiag = vbf[:, :].rearrange("p (h n) -> p h n", h=4)
        rlb = msk[:, :].rearrange("p (h n) -> p h n", h=4)
        for h in range(4):
            TS(wdiag[:, h, :], identf[:, :], wq[:, h:h + 1], None, ALU.mult, None, ["identf", "wq"], ["vbf"])
        for k0 in range(0, nk, 512):
            n = min(512, nk - k0)
            for hp in range(2):
                for hh in range(2):
                    h = 2 * hp + hh
                    MM(psf[5 + hh][:, 0:n], qiT[:, h, :], kidxT[:, k0:k0 + n], True, True, ["qiT", "kidxT"], [pk(5 + hh)])
                for hh in range(2):
                    h = 2 * hp + hh
                    ACT(rlb[:, h, 0:n], psf[5 + hh][:, 0:n], AF.Relu, [pk(5 + hh)], ["msk"])
                for hh in range(2):
                    h = 2 * hp + hh
                    MM(psf[7][:, 0:n], wdiag[:, h, :], rlb[:, h, 0:n], h == 0, h == 3, ["vbf", "msk"], [pk(7)])
            CP(sc[:, k0:k0 + n], psf[7][:, 0:n], [pk(7)], ["sc"], eng="act_copy")
        fillers = []
        fillers.append(lambda: ACT(u_sb[:rows, :], psf[0][:rows, :], AF.Gelu_apprx_tanh, [pk(0)], ["u_sb"]))
        fillers.append(lambda: ACT(vg[:rows, :], psf[1][:rows, :], AF.Gelu_apprx_tanh, [pk(1)], ["vg"]))
        par = ln_ctr[0] % 4
        ln_ctr[0] += 1
        st_, mv2, rs_, nm_ = stats_[par], mv_[par], rstd_[par], nmr_[par]
        ks, km, kr, kn = "stats%d" % par, "mv%d" % par, "rstd%d" % par, "nmr%d" % par

        def lnA():
            P.op("dve", lambda e: e.bn_stats(out=st_[:rows, 0:6], in_=vg[:rows, :]), reads=["vg"], writes=[ks])
            P.op("dve", lambda e: e.bn_aggr(out=mv2[:rows, :], in_=st_[:rows, 0:6]), reads=[ks], writes=[km])
            TS(rs_[:rows, :], mv2[:rows, 1:2], EPS, None, ALU.add, None, [km], [kr])
            ACT(rs_[:rows, :], rs_[:rows, :], AF.Sqrt, [kr], [kr])

        def lnB():
            P.op("dve", lambda e: e.reciprocal(out=rs_[:rows, :], in_=rs_[:rows, :]), reads=[kr], writes=[kr])
            STT(nm_[:rows, :], mv2[:rows, 0:1], -1.0, rs_[:rows, :], ALU.mult, ALU.mult, [km, kr], [kn])
            ACT(vln[:rows, :], vg[:rows, :], AF.Identity, ["vg", kr, kn], ["vln"], bias=nm_[:rows, :], scale=rs_[:rows, :])

        def lnC():
            TT(vln[:rows, :], vln[:rows, :], agB[:rows, :], ALU.mult, ["vln", "agB"], ["vln"])
            TT(vln[:rows, :], vln[:rows, :], abB[:rows, :], ALU.add, ["vln", "abB"], ["vln"])
            CP(vbf[:rows, :], vln[:rows, :], ["vln"], ["vbf"])

        def gm_mm():
            for g in range(4):
                MM(psf[7][:, g * 128:(g + 1) * 128], wsT[:, g, :], vbf[:, g * 128:(g + 1) * 128], True, True,
                   ["wsT", "vbf"], [pk(7)])

        def gm_ev(g):
            STT(mix[:, g * 128:(g + 1) * 128], psf[7][:, g * 128:(g + 1) * 128], bsT[:, g:g + 1],
                u_sb[:, g * 128:(g + 1) * 128], ALU.add, ALU.mult, [pk(7), "bsT", "u_sb"], ["mix"])
        fillers += [lnA]
        rq0 = rope_parts(psf[2][:rows, 0:256].rearrange("p (h d) -> p h d", d=64), 4, rows, cos_t, sin_t,
                         rq[:rows, 0:8:2, :], rqf[:rows, 0:8:2, :], [pk(2)], ["rq"])
        rq1 = rope_parts(psf[2][:rows, 256:512].rearrange("p (h d) -> p h d", d=64), 4, rows, cos_t, sin_t,
                         rq[:rows, 1:8:2, :], rqf[:rows, 1:8:2, :], [pk(2)], ["rq"])
        rk = rope_parts(psf[3][:rows, 0:128].rearrange("p (h d) -> p h d", d=64), 2, rows, cos_t, sin_t,
                        rq[:rows, 8:10, :], rqf[:rows, 8:10, :], [pk(3)], ["rq"])
        fillers += [rq0[0], lnB, rq0[1], lnC, rq1[0], gm_mm, rq1[1]]
        fillers += [lambda: gm_ev(0), rk[0], lambda: gm_ev(1), rk[1], lambda: gm_ev(2), lambda: gm_ev(3)]

        def vstuff():
            CP(vf[:rows, :], psf[3][:rows, 128:256], [pk(3)], ["vf"], eng="act_copy")
            DMA("sp", nk_p[tok0:tok0 + 128, :], rqf[:, 8:10, :].rearrange("p h d -> p (h d)"), ["rq"], [])
            DMA("sp", nv_p[tok0:tok0 + 128, :], vf[:, :], ["vf"], [])

        def vaug_f():
            CP(vaug[:, j, :, 0:64], vf[:, :].rearrange("p (g d) -> p g d", g=2), ["vf"], ["vaug"])
            MEMSET(vaug[:, j, :, 64:65], 1.0, ["vaug"])

        def qtr():
            for hl in range(4):
                TR(psb(5)[:, hl * 128:(hl + 1) * 128], rq[:, 2 * hl:2 * hl + 2, :].rearrange("p h d -> p (h d)"), ident[:],
                   ["rq", "ident"], [pk(5)])
            CP(qT[:].rearrange("p h n -> p (h n)"), psb(5)[:, 0:512], [pk(5)], ["qT"], eng="act_copy")

        def ktr():
            TR(psb(6)[:, 0:128], rq[:, 8:10, :].rearrange("p h d -> p (h d)"), ident[:], ["rq", "ident"], [pk(6)])
            CP(kT[:, j * 128:(j + 1) * 128], psb(6)[:, 0:128], [pk(6)], ["kT"])
        fillers += [vstuff, vaug_f, qtr, ktr]

        def fill(nf=1):
            for _ in range(nf):
                if fillers:
                    fillers.pop(0)()
        P.op("pool", lambda e: e.affine_select(out=sc[:, j * 128:(j + 1) * 128], in_=sc[:, j * 128:(j + 1) * 128],
                                               pattern=[[-1, 128]], compare_op=ALU.is_ge, fill=-BIG, base=0,
                                               channel_multiplier=1), reads=["sc"], writes=["sc"])
        if j >= 2:
            nf = j * 128
            RED(lo[:, 0:1], sc[:, 0:nf], ALU.min, ["sc"], ["lo"])
            RED(w0[:, 0:1], sc[:, 0:nk], ALU.max, ["sc"], ["w0"])
            STT(w0[:, 0:1], w0[:, 0:1], 1.0, lo[:, 0:1], ALU.add, ALU.subtract, ["w0", "lo"], ["w0"])
            for it in range(NBIS + 2):
                TS(Wt[:, it:it + 1], w0[:, 0:1], 0.5 ** it, None, ALU.mult, None, ["w0"], ["Wt"])
            TT(mid[:, 0:1], lo[:, 0:1], Wt[:, 1:2], ALU.add, ["lo", "Wt"], ["mid"])
            for it in range(NBIS):
                TS(junk[:, 0:nk], sc[:, 0:nk], mid[:, 0:1], None, ALU.is_ge, ALU.add, ["sc", "mid"], ["msk", "cnt"],
                   accum_out=cnt[:, 0:1])
                fill(1)
                if it < NBIS - 1:
                    STT(tsel[:, 0:1], cnt[:, 0:1], TOPK - 0.5, Wt[:, it + 1:it + 2], ALU.is_ge, ALU.mult,
                        ["cnt", "Wt"], ["tsel"])
                    STT(mid[:, 0:1], mid[:, 0:1], Wt[:, it + 2:it + 3], tsel[:, 0:1], ALU.subtract, ALU.add,
                        ["mid", "Wt", "tsel"], ["mid"])
                else:
                    STT(tsel[:, 0:1], cnt[:, 0:1], TOPK - 0.5, Wt[:, it + 1:it + 2], ALU.is_lt, ALU.mult,
                        ["cnt", "Wt"], ["tsel"])
                    TT(lo[:, 0:1], mid[:, 0:1], tsel[:, 0:1], ALU.subtract, ["mid", "tsel"], ["lo"])
            fill(100)
            TS(msk[:, 0:nk], sc[:, 0:nk], lo[:, 0:1], None, ALU.is_ge, None, ["sc", "lo"], ["msk"])
        else:
            fill(100)
            TS(msk[:, 0:nk], sc[:, 0:nk], -1.0e29, None, ALU.is_ge, None, ["sc"], ["msk"])
        for kt in range(j + 1):
            pbk = 5 if kt < 8 else 6
            TR(psb(pbk)[:, (kt % 8) * 128:(kt % 8 + 1) * 128], msk[:, kt * 128:(kt + 1) * 128], ident[:],
               ["msk", "ident"], [pk(pbk)])
        n1 = min(j + 1, 8)
        CP(maskT[:, 0:n1, :].rearrange("p a b -> p (a b)"), psb(5)[:, 0:n1 * 128], [pk(5)], ["maskT"])
        if j + 1 > 8:
            n2 = j + 1 - 8
            CP(maskT[:, 8:8 + n2, :].rearrange("p a b -> p (a b)"), psb(6)[:, 0:n2 * 128], [pk(6)], ["maskT"])
        for g in range(2):
            pob = 4 if g == 0 else 7
            gs = slice(64 * g, 64 * g + 64)
            for hf in range(2):
                for kt in range(j + 1):
                    pl = kt % 4
                    MM(psf[pl][:, 0:256], kT[gs, kt * 128:(kt + 1) * 128],
                       qT[gs, 2 * hf:2 * hf + 2, :].rearrange("p h n -> p (h n)"),
                       True, True, ["kT", "qT"], [pk(pl)])
                    ACT(pT[:, kt, :], psf[pl][:, 0:256], AF.Exp, [pk(pl)], ["pT%d" % kt], scale=0.125)
                    TT(pT[:, kt, :].rearrange("p (h n) -> p h n", h=2), pT[:, kt, :].rearrange("p (h n) -> p h n", h=2),
                       bc(maskT[:, kt, :].unsqueeze(1), [128, 2, 128]), ALU.mult, ["pT%d" % kt, "maskT"], ["pT%d" % kt],
                       eng=("pool" if kt % 3 == 0 else "dve"))
                for hh in range(2):
                    hl = 2 * hf + hh
                    for kt in range(j + 1):
                        MM(psf[pob][:, hl * 65:(hl + 1) * 65], pT[:, kt, hh * 128:(hh + 1) * 128], vaug[:, kt, g, :],
                           kt == 0, kt == j, ["pT%d" % kt, "vaug"], [pk(pob)])
            pov = psf[pob][:, 0:260].rearrange("p (h c) -> p h c", c=65)
            P.op("dve", lambda e, pov=pov: e.reciprocal(out=rinv[:, 0:4], in_=pov[:, :, 64]),
                 reads=[pk(pob)], writes=["rinv"])
            TT(mix[:, 512 + g * 256:512 + (g + 1) * 256].rearrange("p (h d) -> p h d", d=64), pov[:, :, 0:64],
               bc(rinv[:, 0:4].unsqueeze(2), [128, 4, 64]), ALU.mult, [pk(pob), "rinv"], ["mix"])
        out_proj_ln2(t, rows, c0)

    def mixer_sample(t, c0):
        rows = NSMP
        mixer_common(t, rows, c0, cosS[:rows, 0, :], sinS[:rows, 0, :])
        DMA("sp", nk_s[:, :], rqf[:rows, 8:10, :].rearrange("p h d -> p (h d)"), ["rq"], [])
        DMA("sp", nv_s[:, :], vf[:rows, :], ["vf"], [])
        DMA("sp", nki_s[:, :], rif[:rows, 4, :], ["ri"], [])
        DMA("sp", av_s[:, :], vln[:rows, :], ["vln"], [])
        for g in range(4):
            TS(vg[:rows, g * 128:(g + 1) * 128], vln[:rows, g * 128:(g + 1) * 128], ws00[:, g:g + 1], bs0[:, g:g + 1],
               ALU.mult, ALU.add, ["vln", "ws00", "bs0"], ["vg"])
        TT(mix[:rows, 0:512], vg[:rows, :], u_sb[:rows, :], ALU.mult, ["vg", "u_sb"], ["mix"])
        MEMSET(riZ[:, :, :, :], 0.0, ["riZ"])
        CP(riZ[:, 0, :, 0:64], ri[:rows, 0:4, :], ["ri"], ["riZ"])
        CP(riZ[:, 1, :, 64:128], ri[:rows, 0:4, :], ["ri"], ["riZ"])
        for s2 in range(2):
            for h in range(4):
                TR(psb(5)[:, (s2 * 4 + h) * 16:(s2 * 4 + h + 1) * 16], riZ[:, s2, h, :], ident[:rows, :rows],
                   ["riZ", "ident"], [pk(5)])
        for s2 in range(2):
            CP(qiTs[:, :, s2, :, :].rearrange("p i h b -> p h i b"),
               psb(5)[:, s2 * 64:(s2 + 1) * 64].rearrange("p (h i b) -> p h i b", h=4, i=8), [pk(5)], ["qiTs"])
        for i in range(8):
            TS(wqm[:, :], wq[:rows, :], oh[:, i:i + 1], None, ALU.mult, None, ["wq", "oh"], ["wqm"])
            MM(psf[6][:, i * 4:(i + 1) * 4], rep[:, :], wqm[:, :], True, True, ["rep", "wqm"], [pk(6)])
        CP(wB[:].rearrange("p a b -> p (a b)"), psf[6][:, 0:32], [pk(6)], ["wB"])
        TT(ssd[:, :, :], rif[:rows, 0:4, :], bc(rif[:rows, 4:5, :], [rows, 4, 64]), ALU.mult, ["ri"], ["ssd"])
        RED(ss4[:, :], ssd[:, :, :], ALU.add, ["ssd"], ["ss4"])
        TS(ss4[:, :], ss4[:, :], 0.0, None, ALU.max, None, ["ss4"], ["ss4"])
        TT(ss4[:, :], ss4[:, :], wq[:rows, :], ALU.mult, ["ss4", "wq"], ["ss4"])
        RED(ss1[:, :], ss4[:, :], ALU.add, ["ss4"], ["ss1"])
        TS(ssb[:, :], oh[:, :], ss1[:, 0:1], None, ALU.mult, None, ["oh", "ss1"], ["ssb"])
        MM(psf[7][:, 0:8], aall[:, :], ssb[:, :], True, True, ["aall", "ssb"], [pk(7)])
        TS(SC[:, :, 128], psf[7][:, 0:8], negfill[:, 0:1], None, ALU.add, None, [pk(7), "negfill"], ["SC"])
        MEMSET(bar[:, :], 0.0, PTK + ["pTa", "pTb", "bar"])
        for i in range(8):
            for c4 in range(4):
                gp = (i * 4 + c4) % 2
                P.op("pool", lambda e, i=i, c4=c4, gp=gp: e.indirect_dma_start(
                    out=KIc2[gp].rearrange("p s d -> p (s d)"), out_offset=None,
                    in_=cache_kidx[:, :],
                    in_offset=bass.IndirectOffsetOnAxis(ap=idx4[:, i, c4:c4 + 1], axis=0)),
                    reads=["idx4"], writes=KIK[gp], dma=True)
                for s8 in range(2):
                    for m in range(8):
                        TR(psb(5)[:, m * 128:(m + 1) * 128],
                           KIc2[gp][:, s8 * 16 + 2 * m:s8 * 16 + 2 * m + 2, :].rearrange("p s d -> p (s d)"), ident[:],
                           KIK[gp] + ["ident"], [pk(5)])
                    CP(kiTc.rearrange("p a b -> p (a b)"), psb(5)[:, :], [pk(5)], ["msk"], eng="act_copy")
                    for m in range(8):
                        sl = c4 * 32 + s8 * 16 + 2 * m
                        pbk = 0 if sl < 64 else 1
                        MM(psf[pbk][:, (sl % 64) * 8:(sl % 64) * 8 + 16], kiTc[:, m, :],
                           qiTs[:, i, :, :, :].rearrange("p s h b -> p (s h b)"), True, True,
                           ["msk", "qiTs"], [pk(pbk)])
            for b2 in range(2):
                ps_ = slice(64 * b2, 64 * b2 + 64)
                for hb in range(2):
                    src = psf[hb][ps_, :].rearrange("p (s h b) -> p s h b", h=4, b=2)[:, :, :, b2]
                    STT(T4[ps_, hb * 64:(hb + 1) * 64, :], src, 0.0, bc(wB[ps_, i, :].unsqueeze(1), [64, 64, 4]),
                        ALU.max, ALU.mult, [pk(hb), "wB"], ["rl"])
                RED(SC[ps_, i, 0:128], T4[ps_, :, :], ALU.add, ["rl"], ["SC"])
        RED(Mx[:, :], SC[:, :, 0:128], ALU.max, ["SC"], ["Mx"], absval=True)
        P.op("pe", lambda e: e.transpose(out=psf[2][0:8, 0:128], in_=Mx[:, :], identity=identf[:]),
             reads=["Mx", "identf"], writes=[pk(2)])
        RED(MxT[:, :], psf[2][0:8, 0:128], ALU.max, [pk(2)], ["MxT"])
        TS(MxT[:, :], MxT[:, :], 1.0, None, ALU.add, None, ["MxT"], ["MxT"])
        TS(Mdiag[:, :], identf[0:8, 0:8], MxT[:, 0:1], None, ALU.mult, None, ["identf", "MxT"], ["Mdiag"])
        MM(psf[2][:, 256:264], onesf[0:8, :], Mdiag[:, :], True, True, ["onesf", "Mdiag"], [pk(2)])
        TS(lo[:, :], psf[2][:, 256:264], -1.0, None, ALU.mult, None, [pk(2)], ["lo"])
        TS(w0[:, :], psf[2][:, 256:264], 2.0, None, ALU.mult, None, [pk(2)], ["w0"])
        for it in range(NBIS):
            TS(wdt[:, :], w0[:, :], 0.5 ** (it + 1), None, ALU.mult, None, ["w0"], ["wdt"])
            TT(mid[:, :], lo[:, :], wdt[:, :], ALU.add, ["lo", "wdt"], ["mid"])
            TT(cmpS[:, :, :], SC[:, :, :], bc(mid[:, :].unsqueeze(2), [128, 8, 129]), ALU.is_ge, ["SC", "mid"], ["sc"])
            RED(cnt[:, :], cmpS[:, :, :], ALU.add, ["sc"], ["cnt"])
            MM(psf[3][:, 0:8], bo[:, :], cnt[:, :], True, True, ["bo", "cnt"], [pk(3)])
            STT(tsel[:, :], psf[3][:, 0:8], TOPK - 0.5, wdt[:, :], ALU.is_ge, ALU.mult, [pk(3), "wdt"], ["tsel"])
            TT(lo[:, :], lo[:, :], tsel[:, :], ALU.add, ["lo", "tsel"], ["lo"])
        TT(mskS[:, :, :], SC[:, :, :], bc(lo[:, :].unsqueeze(2), [128, 8, 129]), ALU.is_ge, ["SC", "lo"], ["mskS"])
        MEMSET(QZ[:, :, :], 0.0, ["QZ"])
        CP(QZ[:, 0:4, 0:64], rq[:rows, 0:8:2, :], ["rq"], ["QZ"])
        CP(QZ[:, 4:8, 64:128], rq[:rows, 1:8:2, :], ["rq"], ["QZ"])
        for h in range(8):
            TR(psb(5)[:, h * 16:(h + 1) * 16], QZ[:, h, :], ident[:rows, :rows], ["QZ", "ident"], [pk(5)])
        CP(Qblk[:].rearrange("p i h b -> p h i b"), psb(5)[:, 0:128].rearrange("p (h i b) -> p h i b", h=8, i=8), [pk(5)], ["Qblk"])
        TR(psb(6)[:, 0:16], rq[:rows, 8:10, :].rearrange("p h d -> p (h d)"), ident[:rows, :rows], ["rq", "ident"], [pk(6)])
        CP(kTs[:, :], psb(6)[:, 0:16], [pk(6)], ["kTs"])
        MEMSET(KTself[:, :, :], 0.0, ["KTself"])
        CP(KTself[:, :, 0:128:64], kTs[:, :].rearrange("p (i b) -> p i b", b=2), ["kTs"], ["KTself"])
        CP(vsbf[:, :], vf[:rows, :], ["vf"], ["vsbf"])
        for i in range(8):
            TS(vfm[:, :], vf[:rows, :], oh[:, i:i + 1], None, ALU.mult, None, ["vf", "oh"], ["vfm"])
            MM(psf[7][:, 0:128], aall[:, :], vfm[:, :], True, True, ["aall", "vfm"], [pk(7)])
            CP(Vself[:, i, :], psf[7][:, 0:128], [pk(7)], ["Vself"])
        MEMSET(Ps[:, :, :, :], 0.0, ["Ps"])
        for i in range(8):
            MEMSET(Pacc[:, :], 0.0, ["Pacc"])
            for c4 in range(9):
                ns = 16 if c4 < 8 else 1
                if c4 < 8:
                    gp = (i * 8 + c4) % 2
                    P.op("pool", lambda e, i=i, c4=c4, gp=gp: e.indirect_dma_start(
                        out=Kc2[gp].rearrange("p s d -> p (s d)"), out_offset=None,
                        in_=cache_k[:, :],
                        in_offset=bass.IndirectOffsetOnAxis(ap=idx8[:, i, c4:c4 + 1], axis=0)),
                        reads=["idx8"], writes=KCK[gp], dma=True)
                    P.op("pool", lambda e, i=i, c4=c4: e.indirect_dma_start(
                        out=Vc.rearrange("p s d -> p (s d)"), out_offset=None,
                        in_=cache_v[:, :],
                        in_offset=bass.IndirectOffsetOnAxis(ap=idx8[:, i, c4:c4 + 1], axis=0)),
                        reads=["idx8"], writes=["sc"], dma=True)
                    for s8 in range(2):
                        for s in range(8):
                            TR(psb(5)[:, s * 128:(s + 1) * 128], Kc2[gp][:, s8 * 8 + s, :], ident[:], KCK[gp] + ["ident"], [pk(5)])
                        CP(KTc.rearrange("p a b -> p (a b)"), psb(5)[:, :], [pk(5)], ["msk"], eng="act_copy")
                        for s in range(8):
                            sl = s8 * 8 + s
                            MM(psf[0][:, sl * 16:(sl + 1) * 16], KTc[:, s, :],
                               Qblk[:, i, :, :].rearrange("p h b -> p (h b)"), True, True,
                               ["msk", "Qblk"], [pk(0)])
                else:
                    MM(psf[0][:, 0:16], KTself[:, i, :], Qblk[:, i, :, :].rearrange("p h b -> p (h b)"),
                       True, True, ["KTself", "Qblk"], [pk(0)])
                for b2 in range(2):
                    ps_ = slice(64 * b2, 64 * b2 + 64)
                    src = psf[0][ps_, 0:ns * 16].rearrange("p (s h b) -> p s h b", h=8, b=2)[:, :, :, b2]
                    ACT(T4[ps_, 0:ns * 2, :].rearrange("p (s a) c -> p s (a c)", a=2), src, AF.Exp, [pk(0)], ["rl"], scale=0.125)
                    mk = mskS[ps_, i, c4 * 16:c4 * 16 + ns]
                    TT(Ps[ps_, 0:ns, b2, :], T4[ps_, 0:ns * 2, :].rearrange("p (s a) c -> p s (a c)", a=2),
                       bc(mk.unsqueeze(2), [64, ns, 8]), ALU.mult, ["rl", "mskS"], ["Ps"])
                RED(Ptmp[:, :], Ps[:, 0:ns, :, :].rearrange("p s b h -> p (b h) s"), ALU.add, ["Ps"], ["Ptmp"])
                TT(Pacc[:, :], Pacc[:, :], Ptmp[:, :], ALU.add, ["Pacc", "Ptmp"], ["Pacc"])
                for s in range(ns):
                    rhs = Vc[:, s, :] if c4 < 8 else Vself[:, i, :]
                    MM(psf[1][0:16, 0:128], Ps[:, s, :, :].rearrange("p b h -> p (b h)"), rhs,
                       (c4 == 0 and s == 0), (c4 == 8), ["Ps", "sc", "Vself"], [pk(1)])
            MM(psf[1][0:16, 128:129], Pacc[:, :], onesf[:, 0:1], True, True, ["Pacc", "onesf"], [pk(1)])
            P.op("dve", lambda e: e.reciprocal(out=den[:, :], in_=psf[1][0:16, 128:129]), reads=[pk(1)], writes=["den"])
            TS(Osel[:, :], psf[1][0:16, 0:64], hm[:, 0:1], None, ALU.mult, None, [pk(1), "hm"], ["Osel"])
            STT(Osel[:, :], psf[1][0:16, 64:128], hm[:, 1:2], Osel[:, :], ALU.mult, ALU.add, [pk(1), "hm", "Osel"], ["Osel"])
            TS(Osel[:, :], Osel[:, :], den[:, 0:1], None, ALU.mult, None, ["Osel", "den"], ["Osel"])
            TT(Oexp[:, :, :], bc(Osel[:, :].unsqueeze(1), [16, 8, 64]), bc(hm[:, 2:10].unsqueeze(2), [16, 8, 64]),
               ALU.mult, ["Osel", "hm"], ["Oexp"])
            MM(psf[2][0:16, 0:512], selp[:, i, :], Oexp[:, :, :].rearrange("p h d -> p (h d)"), i == 0, i == 7,
               ["selp", "Oexp"], [pk(2)])
        CP(mix[:rows, 512:1024], psf[2][0:16, 0:512], [pk(2)], ["mix"])
        MEMSET(bar[:, :], 0.0, PTK + ["pTa", "pTb", "bar"])
        out_proj_ln2(t, rows, c0)

    def load_w_in_out():
        wv = w_in_d.rearrange("(k p) n -> p k n", p=128)
        for (a0, a1) in ((0, 1024), (1024, 2048), (2048, INW)):
            DMA("pool", w_in[:, :, a0:a1], wv[:, :, a0:a1], [], ["w_in"] + RKEYS)
        DMA("pool", w_out, w_out_d.rearrange("(k p) n -> p k n", p=128), [], ["w_out"] + RKEYS)

    if debug:
        DMA("sp", dbg_tab[:, 0:512], cosT[:].rearrange("p a b -> p (a b)"), ["Tcos"], [])
        DMA("sp", dbg_tab[:, 512:1024], sinT[:].rearrange("p a b -> p (a b)"), ["Tsin"], [])
        DMA("sp", dbg_tab[:, 1024:1056], cosS[:].rearrange("p a b -> p (a b)"), ["Scos"], [])
        DMA("sp", dbg_tab[:, 1056:1088], sinS[:].rearrange("p a b -> p (a b)"), ["Ssin"], [])
    ngroups = NSEQ * (16 // GT)
    if ngroups_run is not None:
        ngroups = ngroups_run
    for gi in range(ngroups):
        seq = gi // (16 // GT)
        j0 = (gi % (16 // GT)) * GT
        tiles = [(t, 128, t * 128) for t in range(GT)]
        has_s = (gi == 0)
        if has_s:
            tiles.append((GT, NSMP, GT * 128))
        ntok = GT * 128 + (NSMP if has_s else 0)
        for (t, rows, c0) in tiles:
            if t < GT:
                tok0 = seq * SEQ + (j0 + t) * 128
                DMA("sp", x_res[:, t, :], x_prompt[tok0:tok0 + 128, :], [], ["xr%d" % t])
            else:
                DMA("sp", x_res[:NSMP, t, :], x_sample[:, :], [], ["xr%d" % t])
            to_xT(t, rows, c0, 4 if (t % 2 == 0) else 6)
        ffn(0, tiles, ntok, "ln1_g", "ln1_b", None)
        if debug and gi == 0:
            DMA("sp", dbg_ln1[:, :], x_res[:].rearrange("p a b -> p (a b)"), ["xr%d" % t for t in range(GT + 1)], [])
        load_w_in_out()
        load_ln("ln2_g", "ln2_b")
        for (t, rows, c0) in tiles:
            if t < GT:
                mixer_prompt(t, c0, seq, j0 + t)
            else:
                flush_xT()
                mixer_sample(t, c0)
            if debug and gi == 0:
                DMA("sp", dbg_mix[:, t * D:(t + 1) * D], mix[:, :], ["mix"], [])
                DMA("sp", dbg_sc[:, t * SEQ:(t + 1) * SEQ], sc[:, :], ["sc"], [])
                DMA("sp", dbg_msk[:, t * SEQ:(t + 1) * SEQ], msk[:, :], ["msk"], [])
        flush_xT()
        if debug and gi == 0:
            DMA("sp", dbg_ln2[:, :], x_res[:].rearrange("p a b -> p (a b)"), ["xr%d" % t for t in range(GT + 1)], [])

        def final(t, rows, seq=seq, j0=j0):
            if t < GT:
                tok0 = seq * SEQ + (j0 + t) * 128
                DMA("sp", y_prompt[tok0:tok0 + 128, :], x_res[:, t, :], ["xr%d" % t], [])
            else:
                DMA("sp", y_sample[:, :], x_res[:NSMP, t, :], ["xr%d" % t], [])
        ffn(1, tiles, ntok, "ln3_g", "ln3_b", final)

    P.emit(nc, es, None)
    es.close()
    return nc


_NC = None


def _consts():
    half = 32
    invf = (10000.0 ** (-np.arange(half, dtype=np.float32) / half)).astype(np.float32)
    c_invf = np.tile(invf[None, :], (128, 1)).astype(np.float32)
    p = np.arange(128)
    c_bo = (p[:, None] // 64 == p[None, :] // 64).astype(np.float32)
    b = np.arange(16)
    aall = np.zeros((16, 128), np.float32)
    rep = np.zeros((16, 128), np.float32)
    selp = np.zeros((16, 8, 16), np.float32)
    for bb in range(16):
        aall[bb, 64 * (bb % 2)] = 1.0
        rep[bb, 64 * (bb % 2):64 * (bb % 2) + 64] = 1.0
    for i in range(8):
        for r in range(16):
            selp[r, i, 2 * i + r // 8] = 1.0
    oh = (b[:, None] // 2 == np.arange(8)[None, :]).astype(np.float32)
    negfill = np.full((128, 1), -BIG, np.float32)
    negfill[0, 0] = 0.0
    negfill[64, 0] = 0.0
    hm = np.zeros((16, 10), np.float32)
    for r in range(16):
        h = r % 8
        hm[r, 0] = 1.0 if h < 4 else 0.0
        hm[r, 1] = 0.0 if h < 4 else 1.0
        hm[r, 2 + h] = 1.0
    return dict(c_invf=c_invf, c_bo=c_bo, c_aall=aall, c_rep=rep, c_oh=oh,
                c_negfill=negfill, c_selp=selp.reshape(16, -1), c_hm=hm)


def kernel(x_prompt, x_sample, cache_k, cache_v, cache_kidx, page_table, ln1_g, ln1_b, ffn1_w_up, ffn1_w_down,
           ln2_g, ln2_b, w_in, a_ln_g, a_ln_b, a_ws, a_bs, w_out, ln3_g, ln3_b, ffn2_w_up, ffn2_w_down):
    global _NC
    if _NC is None:
        _NC = build_nc()
    nc = _NC
    f = lambda a: np.ascontiguousarray(np.asarray(a))
    consts = _consts()
    ck = f(cache_k).reshape(10240 * 8, 2048)
    cv = f(cache_v).reshape(10240 * 8, 2048)
    cki = f(cache_kidx).reshape(10240 * 4, 2048)
    shared = dict(
        cache_k=ck, cache_v=cv, cache_kidx=cki,
        ln1_g=f(ln1_g).reshape(1, D), ln1_b=f(ln1_b).reshape(1, D), ln2_g=f(ln2_g).reshape(1, D),
        ln2_b=f(ln2_b).reshape(1, D), ln3_g=f(ln3_g).reshape(1, D), ln3_b=f(ln3_b).reshape(1, D),
        a_ln_g=f(a_ln_g).reshape(1, 512), a_ln_b=f(a_ln_b).reshape(1, 512),
        ffn1_w_up=f(ffn1_w_up).reshape(D, 2 * FF), ffn2_w_up=f(ffn2_w_up).reshape(D, 2 * FF),
        ffn1_w_down=f(ffn1_w_down).reshape(FF, D), ffn2_w_down=f(ffn2_w_down).reshape(FF, D),
        w_in=f(w_in).reshape(D, INW), w_out=f(w_out).reshape(D, D),
        a_ws=f(a_ws).reshape(4, 128, 128), a_bs=f(a_bs).reshape(4, 128), **consts)
    xp = f(x_prompt)
    xs = f(x_sample).reshape(128, D)
    pt = f(page_table).astype(np.int32)
    in_maps = []
    for c in range(8):
        m = dict(shared)
        m["x_prompt"] = xp[2 * c:2 * c + 2].reshape(NSEQ * SEQ, D)
        m["x_sample"] = xs[16 * c:16 * c + 16]
        m["page_table"] = pt[16 * c:16 * c + 16].reshape(NSMP * NPAGE, 1)
        in_maps.append(m)
    res = run_bass_kernel_spmd(nc, in_maps, core_ids=list(range(8)))
    R = res.results
    cat = lambda n: np.concatenate([r[n] for r in R], axis=0)
    y_p = cat("y_prompt").reshape(16, SEQ, D)
    y_s = cat("y_sample").reshape(128, 1, D)
    nk_p = cat("nk_p").reshape(1, 16, SEQ, 2, 64)
    nv_p = cat("nv_p").reshape(1, 16, SEQ, 2, 64)
    nki_p = cat("nki_p").reshape(1, 16, SEQ, 64)
    nk_s = cat("nk_s").reshape(1, 128, 1, 2, 64)
    nv_s = cat("nv_s").reshape(1, 128, 1, 2, 64)
    nki_s = cat("nki_s").reshape(1, 128, 1, 64)
    av_s = cat("av_s").reshape(1, 128, 1, 512)
    return (y_p, y_s, nk_p, nv_p, nki_p, nk_s, nv_s, nki_s, av_s)
```

```python
import numpy as np
from contextlib import ExitStack
import concourse.bass as bass
import concourse.mybir as mybir
from concourse.bass_utils import run_bass_kernel_spmd

F32 = mybir.dt.float32
BF16 = mybir.dt.bfloat16
I32 = mybir.dt.int32
AF = mybir.ActivationFunctionType
ALU = mybir.AluOpType
AX = mybir.AxisListType

D = 1024
FF = 2816
NFC = 22
SLICES = [(0, 6), (6, 6), (12, 5), (17, 5)]
CPS = 6
WG = 3
INW = 2116
SEQ = 2048
NSEQ = 2
NSMP = 16
ALPHA = 2.0 ** 0.25
EPS = 1e-5
IDX_W_SCALE = 256.0 ** -0.5
TOPK = 256
NBIS = 22
BIG = 1.0e30
GT = 8
NPAGE = 64


class Op:
    __slots__ = ("eng", "fn", "reads", "writes", "dma", "deps", "signal", "sem", "val", "idx", "prev_val", "raw")


class Prog:
    ENGS = ("pe", "act", "dve", "pool", "sp")

    def __init__(self):
        self.ops = []
        self.last_w = {}
        self.readers = {}

    def op(self, eng, fn, reads=(), writes=(), dma=False):
        o = Op()
        o.eng, o.fn, o.dma = eng, fn, dma
        o.reads, o.writes = tuple(reads), tuple(writes)
        o.deps = set()
        o.raw = set()
        o.signal = dma
        o.idx = len(self.ops)
        for k in o.reads:
            w = self.last_w.get(k)
            if w is not None:
                o.deps.add(w)
                o.raw.add(w)
        for k in o.writes:
            w = self.last_w.get(k)
            if w is not None:
                o.deps.add(w)
            for r in self.readers.get(k, ()):
                o.deps.add(r)
        for k in o.reads:
            self.readers.setdefault(k, []).append(o.idx)
        for k in o.writes:
            self.last_w[k] = o.idx
            self.readers[k] = []
        o.deps.discard(o.idx)
        self.ops.append(o)
        return o

    def emit(self, nc, es, out_dma_ops):
        ops = self.ops
        for o in ops:
            nd = set()
            for d in o.deps:
                p = ops[d]
                if (not p.dma) and (not o.dma) and p.eng == o.eng:
                    if o.eng == "pe" or d not in o.raw:
                        continue
                nd.add(d)
                p.signal = True
            o.deps = nd
        esem = {e: es.enter_context(nc.semaphore("S_" + e)) for e in self.ENGS}
        NDS = 12
        dsem = {e: [es.enter_context(nc.semaphore("D_%s_%d" % (e, i))) for i in range(NDS)]
                for e in ("sp", "pool", "act")}
        ecount = {e: 0 for e in self.ENGS}
        dcount = {e: [0] * NDS for e in dsem}
        dnext = {e: 0 for e in dsem}
        for o in ops:
            if o.dma:
                j = dnext[o.eng]
                dnext[o.eng] = (j + 1) % NDS
                o.prev_val = dcount[o.eng][j]
                dcount[o.eng][j] += 16
                o.sem, o.val = dsem[o.eng][j], dcount[o.eng][j]
            elif o.signal:
                ecount[o.eng] += 1
                o.sem, o.val = esem[o.eng], ecount[o.eng]
        by_eng = {e: [o for o in ops if o.eng == e] for e in self.ENGS}
        final_waits = [(o.sem, o.val) for o in ops if o.dma]
        block = es.enter_context(nc.Block())

        def run(engname, eng):
            waited = {}

            def wait(sem, val):
                key = id(sem)
                if waited.get(key, 0) >= val:
                    return
                waited[key] = val
                eng.wait_ge(sem, val)

            for o in by_eng[engname]:
                for d in sorted(o.deps):
                    p = ops[d]
                    wait(p.sem, p.val)
                if o.dma and o.prev_val > 0:
                    wait(o.sem, o.prev_val)
                ins = o.fn(eng)
                if o.dma:
                    ins.then_inc(o.sem, 16)
                elif o.signal:
                    ins.then_inc(o.sem, 1)
            if engname == "sp":
                fw = {}
                for sem, val in final_waits:
                    fw[id(sem)] = (sem, max(val, fw.get(id(sem), (None, 0))[1]))
                for sem, val in fw.values():
                    eng.wait_ge(sem, val)

        @block.tensor
        def _(e):
            run("pe", e)

        @block.scalar
        def _(e):
            run("act", e)

        @block.vector
        def _(e):
            run("dve", e)

        @block.gpsimd
        def _(e):
            run("pool", e)

        @block.sync
        def _(e):
            run("sp", e)


def build_nc(debug=False, npool=10240, ngroups_run=None):
    nc = bass.Bass("TRN2", target_bir_lowering=False)
    P = Prog()
    es = ExitStack()

    def din(name, shape, dt=F32):
        return nc.dram_tensor(name, list(shape), dt, kind="ExternalInput").ap()

    def dout(name, shape, dt=F32):
        return nc.dram_tensor(name, list(shape), dt, kind="ExternalOutput").ap()

    NTOK = NSEQ * SEQ
    x_prompt = din("x_prompt", [NTOK, D])
    x_sample = din("x_sample", [NSMP, D])
    cache_k = din("cache_k", [npool * 8, 2048])
    cache_v = din("cache_v", [npool * 8, 2048])
    cache_kidx = din("cache_kidx", [npool * 4, 2048])
    page_table = din("page_table", [NSMP * NPAGE, 1], I32)
    lnp = {n: din(n, [1, D]) for n in ("ln1_g", "ln1_b", "ln2_g", "ln2_b", "ln3_g", "ln3_b")}
    a_ln_g = din("a_ln_g", [1, 512])
    a_ln_b = din("a_ln_b", [1, 512])
    w_up_d = [din("ffn1_w_up", [D, 2 * FF]), din("ffn2_w_up", [D, 2 * FF])]
    w_dn_d = [din("ffn1_w_down", [FF, D]), din("ffn2_w_down", [FF, D])]
    w_in_d = din("w_in", [D, INW])
    w_out_d = din("w_out", [D, D])
    a_ws = din("a_ws", [4, 128, 128])
    a_bs = din("a_bs", [4, 128])
    c_invf = din("c_invf", [128, 32])
    c_bo = din("c_bo", [128, 128])
    c_aall = din("c_aall", [16, 128])
    c_rep = din("c_rep", [16, 128])
    c_oh = din("c_oh", [16, 8])
    c_negfill = din("c_negfill", [128, 1])
    c_selp = din("c_selp", [16, 8 * 16])
    c_hm = din("c_hm", [16, 10])

    y_prompt = dout("y_prompt", [NTOK, D])
    y_sample = dout("y_sample", [NSMP, D])
    nk_p = dout("nk_p", [NTOK, 128])
    nv_p = dout("nv_p", [NTOK, 128])
    nki_p = dout("nki_p", [NTOK, 64])
    nk_s = dout("nk_s", [NSMP, 128])
    nv_s = dout("nv_s", [NSMP, 128])
    nki_s = dout("nki_s", [NSMP, 64])
    av_s = dout("av_s", [NSMP, 512])
    if debug:
        dbg_tab = dout("dbg_tab", [128, 4 * 512 + 128])
        dbg_ln1 = dout("dbg_ln1", [128, (GT + 1) * D])
        dbg_ln2 = dout("dbg_ln2", [128, (GT + 1) * D])
        dbg_mix = dout("dbg_mix", [128, (GT + 1) * D], BF16)
        dbg_sc = dout("dbg_sc", [128, (GT + 1) * SEQ])
        dbg_msk = dout("dbg_msk", [128, (GT + 1) * SEQ], BF16)

    def sb(name, shape, dt=F32):
        return es.enter_context(nc.sbuf_tensor(name, list(shape), dt))

    psf = [es.enter_context(nc.psum_tensor("ps%d" % i, [128, 512], F32)) for i in range(8)]

    def psb(i):
        return psf[i][:].bitcast(BF16)

    def pk(i):
        return "ps%d" % i

    NT = GT + 1
    NTK = GT * 128 + NSMP
    x_res = sb("x_res", [128, NT, D])
    xT = sb("xT", [128, 8, NTK], BF16)
    R_bytes = max(CPS * NTK * 2 + CPS * D * 2 + 2 * 8 * 2 * WG * 128 * 2, 8 * INW * 2 + 8 * D * 2)
    Rt = sb("Rregion", [128, R_bytes // 4], F32)
    Rb = Rt[:].bitcast(BF16)
    o0 = 0
    hT = Rb[:, o0:o0 + CPS * NTK].rearrange("p (c n) -> p c n", c=CPS)
    o0 += CPS * NTK
    wd = Rb[:, o0:o0 + CPS * D].rearrange("p (c n) -> p c n", c=CPS)
    o0 += CPS * D
    WUE = 8 * 2 * WG * 128
    wu = [Rb[:, o0 + i * WUE:o0 + (i + 1) * WUE].rearrange("p (k t n) -> p k t n", k=8, t=2) for i in range(2)]
    w_in = Rb[:, 0:8 * INW].rearrange("p (k n) -> p k n", k=8)
    w_out = Rb[:, 8 * INW:8 * INW + 8 * D].rearrange("p (k n) -> p k n", k=8)
    RKEYS = ["hT", "wd", "wu0", "wu1"]

    gcur = sb("gcur", [128, D])
    bcur = sb("bcur", [128, D])
    agB = sb("agB", [128, 512])
    abB = sb("abB", [128, 512])
    ident = sb("ident", [128, 128], BF16)
    identf = sb("identf", [128, 128])
    wsT = sb("wsT", [128, 4, 128], BF16)
    wsf = sb("wsf", [128, 4, 128])
    wsb = sb("wsb", [128, 4, 128], BF16)
    bsT = sb("bsT", [128, 4])
    ws00 = sb("ws00", [16, 4])
    bs0 = sb("bs0", [16, 4])
    cosT = sb("cosT", [128, 16, 32])
    sinT = sb("sinT", [128, 16, 32])
    cosS = sb("cosS", [128, 1, 32])
    sinS = sb("sinS", [128, 1, 32])
    invf = sb("invf", [128, 32])
    kT = sb("kT", [128, SEQ], BF16)
    kidxT = sb("kidxT", [64, SEQ], BF16)
    vaug = sb("vaug", [128, 16, 2, 65], BF16)
    ybf = sb("ybf", [128, D], BF16)
    stats_ = [sb("stats%d" % i, [128, 12]) for i in range(4)]
    mv_ = [sb("mv%d" % i, [128, 2]) for i in range(4)]
    rstd_ = [sb("rstd%d" % i, [128, 1]) for i in range(4)]
    nmr_ = [sb("nmr%d" % i, [128, 1]) for i in range(4)]
    ln_ctr = [0]
    u_sb = sb("u_sb", [128, 512])
    vg = sb("vg", [128, 512])
    sg = [u_sb, vg]
    vln = sb("vln", [128, 512])
    vbf = sb("vbf", [128, 512], BF16)
    mix = sb("mix", [128, D], BF16)
    mixT = ybf[:, :].rearrange("p (k n) -> p k n", k=8)
    rq = sb("rq", [128, 10, 64], BF16)
    rqf = sb("rqf", [128, 10, 64])
    ri = sb("ri", [128, 5, 64], BF16)
    rif = sb("rif", [128, 5, 64])
    rt1 = sb("rt1", [128, 10, 32])
    rt2 = sb("rt2", [128, 10, 32])
    vf = sb("vf", [128, 128])
    wq = sb("wq", [128, 4])
    qT = sb("qT", [128, 4, 128], BF16)
    qiT = sb("qiT", [64, 4, 128], BF16)
    sc = sb("sc", [128, SEQ])
    rl = sb("rl", [128, 512])
    msk = sb("msk", [128, SEQ], BF16)
    junk = msk
    maskT = sb("maskT", [128, 16, 128], BF16)
    pT = sb("pT", [128, 16, 256], BF16)
    PTK = ["pT"] + ["pT%d" % i for i in range(16)]
    ebuf = [sb("ebuf%d" % i, [128, 256], BF16) for i in range(2)]
    lo = sb("lo", [128, 8])
    Wt = sb("Wt", [128, NBIS + 2])
    wdt = sb("wdt", [128, 8])
    w0 = sb("w0", [128, 8])
    mid = sb("mid", [128, 8])
    cnt = sb("cnt", [128, 8])
    tsel = sb("tsel", [128, 8])
    rinv = sb("rinv", [128, 8])
    bo = sb("bo", [128, 128])
    aall = sb("aall", [16, 128])
    rep = sb("rep", [16, 128])
    wqm = sb("wqm", [16, 4])
    vfm = sb("vfm", [16, 128])
    oh = sb("oh", [16, 8])
    negfill = sb("negfill", [128, 1])
    selp = sb("selp", [16, 8, 16], BF16)
    selpf = sb("selpf", [16, 8, 16])
    hm = sb("hm", [16, 10])
    onesf = sb("onesf", [128, 128])
    ptab = sb("ptab", [128, 8], I32)
    idx4 = sb("idx4", [128, 8, 4], I32)
    idx8 = sb("idx8", [128, 8, 8], I32)
    qiTs = sb("qiTs", [128, 8, 2, 4, 2], BF16)
    riZ = sb("riZ", [16, 2, 4, 128], BF16)
    wB = sb("wB", [128, 8, 4])
    ssd = sb("ssd", [16, 4, 64])
    ss4 = sb("ss4", [16, 4])
    ss1 = sb("ss1", [16, 1])
    ssb = sb("ssb", [16, 8])
    SC = sb("SC", [128, 8, 129])
    mskS = sb("mskS", [128, 8, 129], BF16)
    cmpS = sc[:, 0:8 * 129].rearrange("p (a b) -> p a b", a=8)
    Mx = sb("Mx", [128, 8])
    MxT = sb("MxT", [8, 1])
    Mdiag = sb("Mdiag", [8, 8])
    QZ = sb("QZ", [16, 8, 128], BF16)
    Qblk = sb("Qblk", [128, 8, 8, 2], BF16)
    kTs = sb("kTs", [128, 16], BF16)
    KTself = sb("KTself", [128, 8, 128], BF16)
    Vself = sb("Vself", [128, 8, 128], BF16)
    vsbf = sb("vsbf", [16, 128], BF16)
    Ps = sb("Ps", [128, 33, 2, 8], BF16)
    Pacc = sb("Pacc", [128, 16])
    Ptmp = sb("Ptmp", [128, 16])
    den = sb("den", [16, 1])
    Osel = sb("Osel", [16, 64])
    Oexp = sb("Oexp", [16, 8, 64], BF16)
    T4 = rl[:, :].rearrange("p (s c) -> p s c", c=4)
    KIc = pT[:, 0:8, :].rearrange("p a b -> p (a b)").rearrange("p (s d) -> p s d", d=64)
    Kc = pT[:, 8:16, :].rearrange("p a b -> p (a b)").rearrange("p (s d) -> p s d", d=128)
    KIc2 = [KIc, pT[:, 8:16, :].rearrange("p a b -> p (a b)").rearrange("p (s d) -> p s d", d=64)]
    Kc2 = [Kc, pT[:, 0:8, :].rearrange("p a b -> p (a b)").rearrange("p (s d) -> p s d", d=128)]
    KIK = [["pTa"], ["pTb"]]
    KCK = [["pTb"], ["pTa"]]
    bar = sb("bar", [128, 1])
    Vc2 = [sc[:, 0:1024].bitcast(BF16).rearrange("p (s d) -> p s d", d=128),
           sc[:, 1024:2048].bitcast(BF16).rearrange("p (s d) -> p s d", d=128)]
    VK = [["sca"], ["scb"]]
    Vc = sc[:, 0:1024].bitcast(BF16).rearrange("p (s d) -> p s d", d=128)
    kiTc = msk[:, 0:1024].rearrange("p (a b) -> p a b", a=8)
    KTc = msk[:, 1024:2048].rearrange("p (a b) -> p a b", a=8)

    def DMA(q, out, in_, reads, writes, **kw):
        P.op(q, lambda e, out=out, in_=in_, kw=kw: e.dma_start(out=out, in_=in_, **kw),
             reads=reads, writes=writes, dma=True)

    def MM(out, lhsT, rhs, start, stop, reads, writes):
        P.op("pe", lambda e: e.matmul(out, lhsT=lhsT, rhs=rhs, start=start, stop=stop),
             reads=reads, writes=writes)

    def TR(out, in_, idn, reads, writes):
        P.op("pe", lambda e: e.transpose(out=out, in_=in_, identity=idn), reads=reads, writes=writes)

    def ACT(out, in_, func, reads, writes, bias=None, scale=None):
        kw = {}
        if bias is not None:
            kw["bias"] = bias
        if scale is not None:
            kw["scale"] = scale
        P.op("act", lambda e: e.activation(out=out, in_=in_, func=func, **kw), reads=reads, writes=writes)

    def TS(out, in0, s1, s2, op0, op1, reads, writes, accum_out=None, eng="dve"):
        kw = {}
        if op1 is not None:
            kw["op1"] = op1
        if accum_out is not None:
            kw["accum_out"] = accum_out
        P.op(eng, lambda e: e.tensor_scalar(out=out, in0=in0, scalar1=s1, scalar2=s2, op0=op0, **kw),
             reads=reads, writes=writes)

    def TT(out, in0, in1, op, reads, writes, eng="dve"):
        P.op(eng, lambda e: e.tensor_tensor(out=out, in0=in0, in1=in1, op=op), reads=reads, writes=writes)

    def STT(out, in0, scalar, in1, op0, op1, reads, writes):
        P.op("dve", lambda e: e.scalar_tensor_tensor(out=out, in0=in0, scalar=scalar, in1=in1, op0=op0, op1=op1),
             reads=reads, writes=writes)

    def CP(out, in_, reads, writes, eng="dve"):
        P.op(eng, lambda e: e.tensor_copy(out=out, in_=in_), reads=reads, writes=writes)

    def RED(out, in_, op, reads, writes, axis=AX.X, absval=None):
        P.op("dve", lambda e: e.tensor_reduce(out=out, in_=in_, axis=axis, op=op, apply_absolute_value=absval),
             reads=reads, writes=writes)

    def MEMSET(ap, val, writes, eng="dve"):
        P.op(eng, lambda e: e.memset(ap, val), writes=writes)

    def bc(ap, shape):
        return ap.to_broadcast(list(shape))

    def load_ln(gn, bn):
        DMA("sp", gcur[:], lnp[gn].partition_broadcast(128), [], ["gcur"])
        DMA("sp", bcur[:], lnp[bn].partition_broadcast(128), [], ["bcur"])
    DMA("sp", agB[:], a_ln_g.partition_broadcast(128), [], ["agB"])
    DMA("sp", abB[:], a_ln_b.partition_broadcast(128), [], ["abB"])
    DMA("sp", invf[:], c_invf, [], ["invf"])
    DMA("sp", wsf[:], a_ws.rearrange("g t s -> t g s"), [], ["wsf"])
    DMA("sp", bsT[:], a_bs.rearrange("g t -> t g"), [], ["bsT"], allow_slow_non_contiguous=True)
    DMA("sp", ws00[:], a_ws[:, 0, 0:1].rearrange("g o -> o g").partition_broadcast(16), [], ["ws00"],
        allow_slow_non_contiguous=True)
    DMA("sp", bs0[:], a_bs[:, 0:1].rearrange("g o -> o g").partition_broadcast(16), [], ["bs0"],
        allow_slow_non_contiguous=True)
    DMA("sp", bo[:], c_bo, [], ["bo"])
    DMA("sp", aall[:], c_aall, [], ["aall"])
    DMA("sp", rep[:], c_rep, [], ["rep"])
    DMA("sp", oh[:], c_oh, [], ["oh"])
    DMA("sp", negfill[:], c_negfill, [], ["negfill"])
    DMA("sp", selpf[:].rearrange("p a b -> p (a b)"), c_selp, [], ["selpf"])
    DMA("sp", hm[:], c_hm, [], ["hm"])
    DMA("sp", ptab[:], page_table.rearrange("(i p) o -> p (i o)", p=128), [], ["ptab"],
        allow_slow_non_contiguous=True)
    CP(selp[:], selpf[:], ["selpf"], ["selp"])
    for c in range(4):
        TS(idx4[:, :, c], ptab[:, :], 4.0, float(c), ALU.mult, ALU.add, ["ptab"], ["idx4"])
    for c in range(8):
        TS(idx8[:, :, c], ptab[:, :], 8.0, float(c), ALU.mult, ALU.add, ["ptab"], ["idx8"])
    MEMSET(onesf[:], 1.0, ["onesf"])
    MEMSET(identf[:], 0.0, ["identf"], eng="pool")
    P.op("pool", lambda e: e.affine_select(out=identf[:], in_=identf[:], pattern=[[-1, 128]], compare_op=ALU.not_equal,
                                           fill=1.0, base=0, channel_multiplier=1), reads=["identf"], writes=["identf"])
    CP(ident[:], identf[:], ["identf"], ["ident"], eng="pool")
    for g in range(4):
        P.op("pool", lambda e, g=g: e.affine_select(out=wsf[:, g, :], in_=wsf[:, g, :], pattern=[[-1, 128]],
                                                    compare_op=ALU.is_ge, fill=0.0, base=0, channel_multiplier=1),
             reads=["wsf"], writes=["wsf"])
    CP(wsb[:], wsf[:], ["wsf"], ["wsb"], eng="pool")
    for g in range(4):
        TR(psb(7)[:, g * 128:(g + 1) * 128], wsb[:, g, :], ident[:], ["wsb", "ident"], [pk(7)])
    CP(wsT[:].rearrange("p g t -> p (g t)"), psb(7)[:, 0:512], [pk(7)], ["wsT"])

    posi = sb("posi", [128, 16], I32)
    posf = sb("posf", [128, 16])
    _pad_top = sb("pad_top", [128, 64])
    ang = sc[:, 0:512].rearrange("p (a b) -> p a b", a=16)
    kq = sc[:, 512:1024].rearrange("p (a b) -> p a b", a=16)
    rr = sc[:, 1024:1536].rearrange("p (a b) -> p a b", a=16)
    m1 = sc[:, 1536:2048].rearrange("p (a b) -> p a b", a=16)
    kqi = msk[:, 0:1024].bitcast(I32).rearrange("p (a b) -> p a b", a=16)
    TWO_PI = 2.0 * np.pi
    C1 = 6.28125
    C2 = TWO_PI - C1

    def sincos(posf_ap, nj, sin_out, cos_out, tag):
        a = ang[:, 0:nj, :]
        k_ = kq[:, 0:nj, :]
        ki_ = kqi[:, 0:nj, :]
        r_ = rr[:, 0:nj, :]
        m_ = m1[:, 0:nj, :]
        TT(a, bc(posf_ap.unsqueeze(2), [128, nj, 32]), bc(invf[:].unsqueeze(1), [128, nj, 32]), ALU.mult,
           ["posf", "invf"], ["sc"])
        if debug and nj == 16:
            dbg_ang = dout("dbg_ang", [128, 512])
            DMA("sp", dbg_ang[:, :], sc[:, 0:512], ["sc"], [])
        TS(k_, a, 1.0 / TWO_PI, None, ALU.mult, None, ["sc"], ["sc"])
        CP(ki_, k_, ["sc"], ["msk"])
        CP(k_, ki_, ["msk"], ["sc"])
        STT(r_, k_, -C1, a, ALU.mult, ALU.add, ["sc", "sc"], ["sc"])
        STT(r_, k_, -C2, r_, ALU.mult, ALU.add, ["sc", "sc"], ["sc"])

        def wrap():
            TS(m_, r_, np.pi, -TWO_PI, ALU.is_gt, ALU.mult, ["sc"], ["sc"])
            TT(r_, r_, m_, ALU.add, ["sc", "sc"], ["sc"])
            TS(m_, r_, -np.pi, TWO_PI, ALU.is_lt, ALU.mult, ["sc"], ["sc"])
            TT(r_, r_, m_, ALU.add, ["sc", "sc"], ["sc"])
            TS(r_, r_, np.pi, -np.pi, ALU.min, ALU.max, ["sc"], ["sc"])
        wrap()
        ACT(sin_out, r_, AF.Sin, ["sc"], [tag + "sin"])
        TS(r_, r_, np.pi / 2, None, ALU.add, None, ["sc"], ["sc"])
        wrap()
        ACT(cos_out, r_, AF.Sin, ["sc"], [tag + "cos"])

    P.op("pool", lambda e: e.iota(posi[:], pattern=[[128, 16]], base=0, channel_multiplier=1), writes=["posi"])
    CP(posf[:], posi[:], ["posi"], ["posf"])
    if debug:
        dbg_small = dout("dbg_small", [128, 16])
        DMA("sp", dbg_small[:, :], posf[:], ["posf"], [])
    sincos(posf[:], 16, sinT[:], cosT[:], "T")
    MEMSET(posf[:, 0:1], 8192.0, ["posf"])
    sincos(posf[:, 0:1], 1, sinS[:], cosS[:], "S")

    def layernorm(zin, rows, width, gtile, btile, yout, zkeys, ykeys, gk, bk):
        par = ln_ctr[0] % 4
        ln_ctr[0] += 1
        stats, mv, rstd, nmr = stats_[par], mv_[par], rstd_[par], nmr_[par]
        ks, km, kr, kn = "stats%d" % par, "mv%d" % par, "rstd%d" % par, "nmr%d" % par
        nchunk = width // 512
        for c in range(nchunk):
            P.op("dve", lambda e, c=c: e.bn_stats(out=stats[:rows, c * 6:(c + 1) * 6], in_=zin[:, c * 512:(c + 1) * 512]),
                 reads=zkeys, writes=[ks])
        P.op("dve", lambda e: e.bn_aggr(out=mv[:rows, :], in_=stats[:rows, 0:6 * nchunk]), reads=[ks], writes=[km])
        TS(rstd[:rows, :], mv[:rows, 1:2], EPS, None, ALU.add, None, [km], [kr])
        ACT(rstd[:rows, :], rstd[:rows, :], AF.Sqrt, [kr], [kr])
        P.op("dve", lambda e: e.reciprocal(out=rstd[:rows, :], in_=rstd[:rows, :]), reads=[kr], writes=[kr])
        STT(nmr[:rows, :], mv[:rows, 0:1], -1.0, rstd[:rows, :], ALU.mult, ALU.mult, [km, kr], [kn])
        ACT(yout, zin, AF.Identity, list(zkeys) + [kr, kn], ykeys, bias=nmr[:rows, :], scale=rstd[:rows, :])
        TT(yout, yout, gtile[:rows, 0:width], ALU.mult, list(ykeys) + [gk], ykeys)
        TT(yout, yout, btile[:rows, 0:width], ALU.add, list(ykeys) + [bk], ykeys)

    def ln_batch(items, gtile, btile, gk, bk, post):
        n = len(items)
        sl = []
        for i in range(n):
            par = ln_ctr[0] % 4
            ln_ctr[0] += 1
            sl.append(par)

        def A(i):
            z, rows, keys = items[i]
            par = sl[i]
            stats, mv, rstd = stats_[par], mv_[par], rstd_[par]
            for c in range(2):
                P.op("dve", lambda e, c=c: e.bn_stats(out=stats[:rows, c * 6:(c + 1) * 6], in_=z[:, c * 512:(c + 1) * 512]),
                     reads=keys, writes=["stats%d" % par])
            P.op("dve", lambda e: e.bn_aggr(out=mv[:rows, :], in_=stats[:rows, 0:12]), reads=["stats%d" % par],
                 writes=["mv%d" % par])
            TS(rstd[:rows, :], mv[:rows, 1:2], EPS, None, ALU.add, None, ["mv%d" % par], ["rstd%d" % par])
            ACT(rstd[:rows, :], rstd[:rows, :], AF.Sqrt, ["rstd%d" % par], ["rstd%d" % par])

        def B(i):
            z, rows, keys = items[i]
            par = sl[i]
            mv, rstd, nmr = mv_[par], rstd_[par], nmr_[par]
            P.op("dve", lambda e: e.reciprocal(out=rstd[:rows, :], in_=rstd[:rows, :]), reads=["rstd%d" % par],
                 writes=["rstd%d" % par])
            STT(nmr[:rows, :], mv[:rows, 0:1], -1.0, rstd[:rows, :], ALU.mult, ALU.mult, ["mv%d" % par, "rstd%d" % par],
                ["nmr%d" % par])
            ACT(z, z, AF.Identity, list(keys) + ["rstd%d" % par, "nmr%d" % par], keys, bias=nmr[:rows, :], scale=rstd[:rows, :])

        def C(i):
            z, rows, keys = items[i]
            TT(z, z, gtile[:rows, 0:D], ALU.mult, list(keys) + [gk], keys)
            TT(z, z, btile[:rows, 0:D], ALU.add, list(keys) + [bk], keys)
            post(i)

        for step in range(n + 2):
            if step < n:
                A(step)
            if 0 <= step - 1 < n:
                B(step - 1)
            if 0 <= step - 2 < n:
                C(step - 2)

    def to_xT(t, rows, c0, pbank):
        CP(ybf[:rows, :], x_res[:rows, t, :], ["xr%d" % t], ["ybf"], eng="act_copy")
        for kc in range(8):
            TR(psb(pbank)[:, kc * 128:kc * 128 + rows], ybf[:rows, kc * 128:(kc + 1) * 128], ident[:rows, :rows],
               ["ybf", "ident"], [pk(pbank)])
        CP(xT[:, :, c0:c0 + rows], psb(pbank).rearrange("p (k n) -> p k n", k=8)[:, :, 0:rows], [pk(pbank)], ["xT%d" % t])

    _CP = CP

    def CP(out, in_, reads, writes, eng="dve"):
        if eng == "act_copy":
            P.op("act", lambda e: e.copy(out=out, in_=in_), reads=reads, writes=writes)
        else:
            _CP(out, in_, reads, writes, eng=eng)

    def ffn(fi, tiles, ntok, g_name, b_name, final):
        segs = []
        c = 0
        while c < ntok:
            n = min(512, ntok - c)
            segs.append((c, n))
            c += n
        groups = []
        for s, (cs0, ncs) in enumerate(SLICES):
            c = 0
            while c < ncs:
                ng = min(WG, ncs - c)
                groups.append((s, c, ng, cs0 + c))
                c += ng
        issued = [0]

        def issue_group(extra):
            k = issued[0]
            if k >= len(groups):
                return
            issued[0] += 1
            _, _, ng, cc0 = groups[k]
            b = k % 2
            for half in range(2):
                col0 = half * FF + cc0 * 128
                DMA("pool", wu[b][:, :, half, 0:ng * 128],
                    w_up_d[fi][:, col0:col0 + ng * 128].rearrange("(k p) n -> p k n", p=128),
                    [], ["wu%d" % b] + extra)

        gk = 0
        for s, (cs0, ncs) in enumerate(SLICES):
            if s == 0:
                DMA("pool", wd[:, 0:ncs, :], w_dn_d[fi][cs0 * 128:(cs0 + ncs) * 128, :].rearrange("(c p) n -> p c n", p=128),
                    [], ["wd", "w_in", "w_out"])
                issue_group(["w_in", "w_out"])
                issue_group(["w_in", "w_out"])
            else:
                DMA("pool", wd[:, 0:ncs, :], w_dn_d[fi][cs0 * 128:(cs0 + ncs) * 128, :].rearrange("(c p) n -> p c n", p=128),
                    [], ["wd"])
            while gk < len(groups) and groups[gk][0] == s:
                _, c, ng, cc0 = groups[gk]
                b = gk % 2
                wk = "wu%d" % b
                for cl in range(ng):
                    for si, (c0, n) in enumerate(segs):
                        pg, pu = (0, 1) if (si % 2 == 0) else (2, 3)
                        for kc in range(8):
                            MM(psf[pg][:, 0:n], wu[b][:, kc, 0, cl * 128:(cl + 1) * 128], xT[:, kc, c0:c0 + n], kc == 0, kc == 7,
                               [wk, "xTall"], [pk(pg)])
                        for kc in range(8):
                            MM(psf[pu][:, 0:n], wu[b][:, kc, 1, cl * 128:(cl + 1) * 128], xT[:, kc, c0:c0 + n], kc == 0, kc == 7,
                               [wk, "xTall"], [pk(pu)])
                        sgb = sg[si % 2]
                        ACT(sgb[:, 0:n], psf[pg][:, 0:n], AF.Silu, [pk(pg)], [("u_sb", "vg")[si % 2]])
                        STT(hT[:, c + cl, c0:c0 + n], sgb[:, 0:n], 0.5, psf[pu][:, 0:n], ALU.mult, ALU.mult,
                            [("u_sb", "vg")[si % 2], pk(pu)], ["hT"])
                gk += 1
                issue_group([])
            for ti, (t, rows, c0) in enumerate(tiles):
                pb = (4, 5) if ti % 2 == 0 else (6, 7)
                for half in range(2):
                    for c in range(ncs):
                        MM(psf[pb[half]][:rows, :], hT[:, c, c0:c0 + rows], wd[:, c, half * 512:(half + 1) * 512],
                           c == 0, c == ncs - 1, ["hT", "wd"], [pk(pb[half])])
                    xs = x_res[:rows, t, half * 512:(half + 1) * 512]
                    if s == 0:
                        STT(xs, xs, ALPHA, psf[pb[half]][:rows, :], ALU.mult, ALU.add,
                            ["xr%d" % t, pk(pb[half])], ["xr%d" % t])
                    else:
                        TT(xs, xs, psf[pb[half]][:rows, :], ALU.add, ["xr%d" % t, pk(pb[half])], ["xr%d" % t])
        load_ln(g_name, b_name)
        def post(i):
            t, rows, c0 = tiles[i]
            if final is None:
                to_xT(t, rows, c0, 4 if (t % 2 == 0) else 6)
            else:
                final(t, rows)
        ln_batch([(x_res[:rows, t, :], rows, ["xr%d" % t]) for (t, rows, c0) in tiles], gcur, bcur, "gcur", "bcur", post)

    _op = P.op

    def op_bridge(eng, fn, reads=(), writes=(), dma=False):
        writes = list(writes)
        if any(k.startswith("xT") and k != "xTall" for k in writes):
            writes.append("xTall")
        return _op(eng, fn, reads=reads, writes=writes, dma=dma)
    P.op = op_bridge

    def rope(src, nh, rows, cos_t, sin_t, outb, outf, skeys, okeys):
        cb = bc(cos_t.unsqueeze(1), [rows, nh, 32])
        sbb = bc(sin_t.unsqueeze(1), [rows, nh, 32])
        x1 = src[:, :, 0:32]
        x2 = src[:, :, 32:64]
        a = rt1[:rows, 0:nh, :]
        b = rt2[:rows, 0:nh, :]
        TT(a, x1, cb, ALU.mult, list(skeys) + ["Tcos", "Scos"], ["rt1"])
        TT(b, x2, sbb, ALU.mult, list(skeys) + ["Tsin", "Ssin"], ["rt2"])
        TT(outf[:, :, 0:32], a, b, ALU.subtract, ["rt1", "rt2"], okeys)
        TT(a, x2, cb, ALU.mult, list(skeys) + ["Tcos", "Scos"], ["rt1"])
        TT(b, x1, sbb, ALU.mult, list(skeys) + ["Tsin", "Ssin"], ["rt2"])
        TT(outf[:, :, 32:64], a, b, ALU.add, ["rt1", "rt2"], okeys)
        CP(outb, outf, okeys, okeys)

    def mixer_common(t, rows, c0, cos_t, sin_t):
        for nb in range(5):
            w0c = nb * 512
            wn = min(512, INW - w0c)
            for kc in range(8):
                MM(psf[nb][:rows, 0:wn], xT[:, kc, c0:c0 + rows], w_in[:, kc, w0c:w0c + wn], kc == 0, kc == 7,
                   ["xT%d" % t, "w_in"], [pk(nb)])
        ACT(u_sb[:rows, :], psf[0][:rows, :], AF.Gelu_apprx_tanh, [pk(0)], ["u_sb"])
        ACT(vg[:rows, :], psf[1][:rows, :], AF.Gelu_apprx_tanh, [pk(1)], ["vg"])
        layernorm(vg[:rows, :], rows, 512, agB, abB, vln[:rows, :], ["vg"], ["vln"], "agB", "abB")
        CP(vbf[:rows, :], vln[:rows, :], ["vln"], ["vbf"])
        for g in range(2):
            rope(psf[2][:rows, g * 256:(g + 1) * 256].rearrange("p (h d) -> p h d", d=64), 4, rows, cos_t, sin_t,
                 rq[:rows, g:8:2, :], rqf[:rows, g:8:2, :], [pk(2)], ["rq"])
        rope(psf[3][:rows, 0:128].rearrange("p (h d) -> p h d", d=64), 2, rows, cos_t, sin_t,
             rq[:rows, 8:10, :], rqf[:rows, 8:10, :], [pk(3)], ["rq"])
        rope(psf[3][:rows, 256:512].rearrange("p (h d) -> p h d", d=64), 4, rows, cos_t, sin_t,
             ri[:rows, 0:4, :], rif[:rows, 0:4, :], [pk(3)], ["ri"])
        rope(psf[4][:rows, 0:64].rearrange("p (h d) -> p h d", d=64), 1, rows, cos_t, sin_t,
             ri[:rows, 4:5, :], rif[:rows, 4:5, :], [pk(4)], ["ri"])
        CP(vf[:rows, :], psf[3][:rows, 128:256], [pk(3)], ["vf"], eng="act_copy")
        TS(wq[:rows, :], psf[4][:rows, 64:68], IDX_W_SCALE, None, ALU.mult, None, [pk(4)], ["wq"])

    def out_proj_ln2(t, rows, c0):
        for kc in range(8):
            TR(psb(5)[:, kc * 128:kc * 128 + rows], mix[:rows, kc * 128:(kc + 1) * 128], ident[:rows, :rows],
               ["mix", "ident"], [pk(5)])
        CP(mixT[:, :, 0:rows], psb(5).rearrange("p (k n) -> p k n", k=8)[:, :, 0:rows], [pk(5)], ["ybf"])
        for half in range(2):
            for kc in range(8):
                MM(psf[half][:rows, :], mixT[:, kc, 0:rows], w_out[:, kc, half * 512:(half + 1) * 512], kc == 0, kc == 7,
                   ["ybf", "w_out"], [pk(half)])
            xs = x_res[:rows, t, half * 512:(half + 1) * 512]
            STT(xs, xs, ALPHA, psf[half][:rows, :], ALU.mult, ALU.add, ["xr%d" % t, pk(half)], ["xr%d" % t])
        layernorm(x_res[:rows, t, :], rows, D, gcur, bcur, x_res[:rows, t, :],
                  ["xr%d" % t], ["xr%d" % t], "gcur", "bcur")
        pending_xT.append((t, rows, c0))

    pending_xT = []

    def flush_xT():
        while pending_xT:
            t_, rows_, c0_ = pending_xT.pop(0)
            to_xT(t_, rows_, c0_, 6)

    def rope_parts(src, nh, rows, cos_t, sin_t, outb, outf, skeys, okeys):
        cb = bc(cos_t.unsqueeze(1), [rows, nh, 32])
        sbb = bc(sin_t.unsqueeze(1), [rows, nh, 32])
        x1 = src[:, :, 0:32]
        x2 = src[:, :, 32:64]
        a_ = rt1[:rows, 0:nh, :]
        b_ = rt2[:rows, 0:nh, :]

        def p1():
            TT(a_, x1, cb, ALU.mult, list(skeys) + ["Tcos", "Scos"], ["rt1"])
            TT(b_, x2, sbb, ALU.mult, list(skeys) + ["Tsin", "Ssin"], ["rt2"])
            TT(outf[:, :, 0:32], a_, b_, ALU.subtract, ["rt1", "rt2"], okeys)

        def p2():
            TT(a_, x2, cb, ALU.mult, list(skeys) + ["Tcos", "Scos"], ["rt1"])
            TT(b_, x1, sbb, ALU.mult, list(skeys) + ["Tsin", "Ssin"], ["rt2"])
            TT(outf[:, :, 32:64], a_, b_, ALU.add, ["rt1", "rt2"], okeys)
            CP(outb, outf, okeys, okeys)
        return [p1, p2]

    def mixer_prompt(t, c0, seq, j):
        rows = 128
        tok0 = seq * SEQ + j * 128
        cos_t = cosT[:, j, :]
        sin_t = sinT[:, j, :]
        for nb in range(5):
            w0c = nb * 512
            wn = min(512, INW - w0c)
            for kc in range(8):
                MM(psf[nb][:rows, 0:wn], xT[:, kc, c0:c0 + rows], w_in[:, kc, w0c:w0c + wn], kc == 0, kc == 7,
                   ["xT%d" % t, "w_in"], [pk(nb)])
        flush_xT()
        rope(psf[3][:rows, 256:512].rearrange("p (h d) -> p h d", d=64), 4, rows, cos_t, sin_t,
             ri[:rows, 0:4, :], rif[:rows, 0:4, :], [pk(3)], ["ri"])
        rope(psf[4][:rows, 0:64].rearrange("p (h d) -> p h d", d=64), 1, rows, cos_t, sin_t,
             ri[:rows, 4:5, :], rif[:rows, 4:5, :], [pk(4)], ["ri"])
        TS(wq[:rows, :], psf[4][:rows, 64:68], IDX_W_SCALE, None, ALU.mult, None, [pk(4)], ["wq"])
        DMA("sp", nki_p[tok0:tok0 + 128, :], rif[:, 4, :], ["ri"], [])
        for h in range(5):
            TR(psb(6)[0:64, (1 + h) * 128:(2 + h) * 128], ri[:, h, :], ident[:], ["ri", "ident"], [pk(6)])
        CP(qiT[:].rearrange("p h n -> p (h n)"), psb(6)[0:64, 128:640], [pk(6)], ["qiT"])
        CP(kidxT[:, j * 128:(j + 1) * 128], psb(6)[0:64, 640:768], [pk(6)], ["kidxT"])
        nk = (j + 1) * 128
        wdiag = vbf[:, :].rearrange("p (h n) -> p h n", h=4)
        rlb = msk[:, :].rearrange("p (h n) -> p h n", h=4)
        for h in range(4):
            TS(wdiag[:, h, :], identf[:, :], wq[:, h:h + 1], None, ALU.mult, None, ["identf", "wq"], ["vbf"])
        for k0 in range(0, nk, 512):
            n = min(512, nk - k0)
            for hp in range(2):
                for hh in range(2):
                    h = 2 * hp + hh
                    MM(psf[5 + hh][:, 0:n], qiT[:, h, :], kidxT[:, k0:k0 + n], True, True, ["qiT", "kidxT"], [pk(5 + hh)])
                for hh in range(2):
                    h = 2 * hp + hh
                    ACT(rlb[:, h, 0:n], psf[5 + hh][:, 0:n], AF.Relu, [pk(5 + hh)], ["msk"])
                for hh in range(2):
                    h = 2 * hp + hh
                    MM(psf[7][:, 0:n], wdiag[:, h, :], rlb[:, h, 0:n], h == 0, h == 3, ["vbf", "msk"], [pk(7)])
            CP(sc[:, k0:k0 + n], psf[7][:, 0:n], [pk(7)], ["sc"], eng="act_copy")
        fillers = []
        fillers.append(lambda: ACT(u_sb[:rows, :], psf[0][:rows, :], AF.Gelu_apprx_tanh, [pk(0)], ["u_sb"]))
        fillers.append(lambda: ACT(vg[:rows, :], psf[1][:rows, :], AF.Gelu_apprx_tanh, [pk(1)], ["vg"]))
        par = ln_ctr[0] % 4
        ln_ctr[0] += 1
        st_, mv2, rs_, nm_ = stats_[par], mv_[par], rstd_[par], nmr_[par]
        ks, km, kr, kn = "stats%d" % par, "mv%d" % par, "rstd%d" % par, "nmr%d" % par

        def lnA():
            P.op("dve", lambda e: e.bn_stats(out=st_[:rows, 0:6], in_=vg[:rows, :]), reads=["vg"], writes=[ks])
            P.op("dve", lambda e: e.bn_aggr(out=mv2[:rows, :], in_=st_[:rows, 0:6]), reads=[ks], writes=[km])
            TS(rs_[:rows, :], mv2[:rows, 1:2], EPS, None, ALU.add, None, [km], [kr])
            ACT(rs_[:rows, :], rs_[:rows, :], AF.Sqrt, [kr], [kr])

        def lnB():
            P.op("dve", lambda e: e.reciprocal(out=rs_[:rows, :], in_=rs_[:rows, :]), reads=[kr], writes=[kr])
            STT(nm_[:rows, :], mv2[:rows, 0:1], -1.0, rs_[:rows, :], ALU.mult, ALU.mult, [km, kr], [kn])
            ACT(vln[:rows, :], vg[:rows, :], AF.Identity, ["vg", kr, kn], ["vln"], bias=nm_[:rows, :], scale=rs_[:rows, :])

        def lnC():
            TT(vln[:rows, :], vln[:rows, :], agB[:rows, :], ALU.mult, ["vln", "agB"], ["vln"])
            TT(vln[:rows, :], vln[:rows, :], abB[:rows, :], ALU.add, ["vln", "abB"], ["vln"])
            CP(vbf[:rows, :], vln[:rows, :], ["vln"], ["vbf"])

        def gm_mm():
            for g in range(4):
                MM(psf[7][:, g * 128:(g + 1) * 128], wsT[:, g, :], vbf[:, g * 128:(g + 1) * 128], True, True,
                   ["wsT", "vbf"], [pk(7)])

        def gm_ev(g):
            STT(mix[:, g * 128:(g + 1) * 128], psf[7][:, g * 128:(g + 1) * 128], bsT[:, g:g + 1],
                u_sb[:, g * 128:(g + 1) * 128], ALU.add, ALU.mult, [pk(7), "bsT", "u_sb"], ["mix"])
        fillers += [lnA]
        rq0 = rope_parts(psf[2][:rows, 0:256].rearrange("p (h d) -> p h d", d=64), 4, rows, cos_t, sin_t,
                         rq[:rows, 0:8:2, :], rqf[:rows, 0:8:2, :], [pk(2)], ["rq"])
        rq1 = rope_parts(psf[2][:rows, 256:512].rearrange("p (h d) -> p h d", d=64), 4, rows, cos_t, sin_t,
                         rq[:rows, 1:8:2, :], rqf[:rows, 1:8:2, :], [pk(2)], ["rq"])
        rk = rope_parts(psf[3][:rows, 0:128].rearrange("p (h d) -> p h d", d=64), 2, rows, cos_t, sin_t,
                        rq[:rows, 8:10, :], rqf[:rows, 8:10, :], [pk(3)], ["rq"])
        fillers += [rq0[0], lnB, rq0[1], lnC, rq1[0], gm_mm, rq1[1]]
        fillers += [lambda: gm_ev(0), rk[0], lambda: gm_ev(1), rk[1], lambda: gm_ev(2), lambda: gm_ev(3)]

        def vstuff():
            CP(vf[:rows, :], psf[3][:rows, 128:256], [pk(3)], ["vf"], eng="act_copy")
            DMA("sp", nk_p[tok0:tok0 + 128, :], rqf[:, 8:10, :].rearrange("p h d -> p (h d)"), ["rq"], [])
            DMA("sp", nv_p[tok0:tok0 + 128, :], vf[:, :], ["vf"], [])

        def vaug_f():
            CP(vaug[:, j, :, 0:64], vf[:, :].rearrange("p (g d) -> p g d", g=2), ["vf"], ["vaug"])
            MEMSET(vaug[:, j, :, 64:65], 1.0, ["vaug"])

        def qtr():
            for hl in range(4):
                TR(psb(5)[:, hl * 128:(hl + 1) * 128], rq[:, 2 * hl:2 * hl + 2, :].rearrange("p h d -> p (h d)"), ident[:],
                   ["rq", "ident"], [pk(5)])
            CP(qT[:].rearrange("p h n -> p (h n)"), psb(5)[:, 0:512], [pk(5)], ["qT"], eng="act_copy")

        def ktr():
            TR(psb(6)[:, 0:128], rq[:, 8:10, :].rearrange("p h d -> p (h d)"), ident[:], ["rq", "ident"], [pk(6)])
            CP(kT[:, j * 128:(j + 1) * 128], psb(6)[:, 0:128], [pk(6)], ["kT"])
        fillers += [vstuff, vaug_f, qtr, ktr]

        def fill(nf=1):
            for _ in range(nf):
                if fillers:
                    fillers.pop(0)()
        P.op("pool", lambda e: e.affine_select(out=sc[:, j * 128:(j + 1) * 128], in_=sc[:, j * 128:(j + 1) * 128],
                                               pattern=[[-1, 128]], compare_op=ALU.is_ge, fill=-BIG, base=0,
                                               channel_multiplier=1), reads=["sc"], writes=["sc"])
        if j >= 2:
            nf = j * 128
            RED(lo[:, 0:1], sc[:, 0:nf], ALU.min, ["sc"], ["lo"])
            RED(w0[:, 0:1], sc[:, 0:nk], ALU.max, ["sc"], ["w0"])
            STT(w0[:, 0:1], w0[:, 0:1], 1.0, lo[:, 0:1], ALU.add, ALU.subtract, ["w0", "lo"], ["w0"])
            for it in range(NBIS + 2):
                TS(Wt[:, it:it + 1], w0[:, 0:1], 0.5 ** it, None, ALU.mult, None, ["w0"], ["Wt"])
            TT(mid[:, 0:1], lo[:, 0:1], Wt[:, 1:2], ALU.add, ["lo", "Wt"], ["mid"])
            for it in range(NBIS):
                TS(junk[:, 0:nk], sc[:, 0:nk], mid[:, 0:1], None, ALU.is_ge, ALU.add, ["sc", "mid"], ["msk", "cnt"],
                   accum_out=cnt[:, 0:1])
                fill(1)
                if it < NBIS - 1:
                    STT(tsel[:, 0:1], cnt[:, 0:1], TOPK - 0.5, Wt[:, it + 1:it + 2], ALU.is_ge, ALU.mult,
                        ["cnt", "Wt"], ["tsel"])
                    STT(mid[:, 0:1], mid[:, 0:1], Wt[:, it + 2:it + 3], tsel[:, 0:1], ALU.subtract, ALU.add,
                        ["mid", "Wt", "tsel"], ["mid"])
                else:
                    STT(tsel[:, 0:1], cnt[:, 0:1], TOPK - 0.5, Wt[:, it + 1:it + 2], ALU.is_lt, ALU.mult,
                        ["cnt", "Wt"], ["tsel"])
                    TT(lo[:, 0:1], mid[:, 0:1], tsel[:, 0:1], ALU.subtract, ["mid", "tsel"], ["lo"])
            fill(100)
            TS(msk[:, 0:nk], sc[:, 0:nk], lo[:, 0:1], None, ALU.is_ge, None, ["sc", "lo"], ["msk"])
        else:
            fill(100)
            TS(msk[:, 0:nk], sc[:, 0:nk], -1.0e29, None, ALU.is_ge, None, ["sc"], ["msk"])
        for kt in range(j + 1):
            pbk = 5 if kt < 8 else 6
            TR(psb(pbk)[:, (kt % 8) * 128:(kt % 8 + 1) * 128], msk[:, kt * 128:(kt + 1) * 128], ident[:],
               ["msk", "ident"], [pk(pbk)])
        n1 = min(j + 1, 8)
        CP(maskT[:, 0:n1, :].rearrange("p a b -> p (a b)"), psb(5)[:, 0:n1 * 128], [pk(5)], ["maskT"])
        if j + 1 > 8:
            n2 = j + 1 - 8
            CP(maskT[:, 8:8 + n2, :].rearrange("p a b -> p (a b)"), psb(6)[:, 0:n2 * 128], [pk(6)], ["maskT"])
        for g in range(2):
            pob = 4 if g == 0 else 7
            gs = slice(64 * g, 64 * g + 64)
            for hf in range(2):
                for kt in range(j + 1):
                    pl = kt % 4
                    MM(psf[pl][:, 0:256], kT[gs, kt * 128:(kt + 1) * 128],
                       qT[gs, 2 * hf:2 * hf + 2, :].rearrange("p h n -> p (h n)"),
                       True, True, ["kT", "qT"], [pk(pl)])
                    ACT(pT[:, kt, :], psf[pl][:, 0:256], AF.Exp, [pk(pl)], ["pT%d" % kt], scale=0.125)
                    TT(pT[:, kt, :].rearrange("p (h n) -> p h n", h=2), pT[:, kt, :].rearrange("p (h n) -> p h n", h=2),
                       bc(maskT[:, kt, :].unsqueeze(1), [128, 2, 128]), ALU.mult, ["pT%d" % kt, "maskT"], ["pT%d" % kt],
                       eng=("pool" if kt % 3 == 0 else "dve"))
                for hh in range(2):
                    hl = 2 * hf + hh
                    for kt in range(j + 1):
                        MM(psf[pob][:, hl * 65:(hl + 1) * 65], pT[:, kt, hh * 128:(hh + 1) * 128], vaug[:, kt, g, :],
                           kt == 0, kt == j, ["pT%d" % kt, "vaug"], [pk(pob)])
            pov = psf[pob][:, 0:260].rearrange("p (h c) -> p h c", c=65)
            P.op("dve", lambda e, pov=pov: e.reciprocal(out=rinv[:, 0:4], in_=pov[:, :, 64]),
                 reads=[pk(pob)], writes=["rinv"])
            TT(mix[:, 512 + g * 256:512 + (g + 1) * 256].rearrange("p (h d) -> p h d", d=64), pov[:, :, 0:64],
               bc(rinv[:, 0:4].unsqueeze(2), [128, 4, 64]), ALU.mult, [pk(pob), "rinv"], ["mix"])
        out_proj_ln2(t, rows, c0)

    def mixer_sample(t, c0):
        rows = NSMP
        mixer_common(t, rows, c0, cosS[:rows, 0, :], sinS[:rows, 0, :])
        DMA("sp", nk_s[:, :], rqf[:rows, 8:10, :].rearrange("p h d -> p (h d)"), ["rq"], [])
        DMA("sp", nv_s[:, :], vf[:rows, :], ["vf"], [])
        DMA("sp", nki_s[:, :], rif[:rows, 4, :], ["ri"], [])
        DMA("sp", av_s[:, :], vln[:rows, :], ["vln"], [])
        for g in range(4):
            TS(vg[:rows, g * 128:(g + 1) * 128], vln[:rows, g * 128:(g + 1) * 128], ws00[:, g:g + 1], bs0[:, g:g + 1],
               ALU.mult, ALU.add, ["vln", "ws00", "bs0"], ["vg"])
        TT(mix[:rows, 0:512], vg[:rows, :], u_sb[:rows, :], ALU.mult, ["vg", "u_sb"], ["mix"])
        MEMSET(riZ[:, :, :, :], 0.0, ["riZ"])
        CP(riZ[:, 0, :, 0:64], ri[:rows, 0:4, :], ["ri"], ["riZ"])
        CP(riZ[:, 1, :, 64:128], ri[:rows, 0:4, :], ["ri"], ["riZ"])
        for s2 in range(2):
            for h in range(4):
                TR(psb(5)[:, (s2 * 4 + h) * 16:(s2 * 4 + h + 1) * 16], riZ[:, s2, h, :], ident[:rows, :rows],
                   ["riZ", "ident"], [pk(5)])
        for s2 in range(2):
            CP(qiTs[:, :, s2, :, :].rearrange("p i h b -> p h i b"),
               psb(5)[:, s2 * 64:(s2 + 1) * 64].rearrange("p (h i b) -> p h i b", h=4, i=8), [pk(5)], ["qiTs"])
        for i in range(8):
            TS(wqm[:, :], wq[:rows, :], oh[:, i:i + 1], None, ALU.mult, None, ["wq", "oh"], ["wqm"])
            MM(psf[6][:, i * 4:(i + 1) * 4], rep[:, :], wqm[:, :], True, True, ["rep", "wqm"], [pk(6)])
        CP(wB[:].rearrange("p a b -> p (a b)"), psf[6][:, 0:32], [pk(6)], ["wB"])
        TT(ssd[:, :, :], rif[:rows, 0:4, :], bc(rif[:rows, 4:5, :], [rows, 4, 64]), ALU.mult, ["ri"], ["ssd"])
        RED(ss4[:, :], ssd[:, :, :], ALU.add, ["ssd"], ["ss4"])
        TS(ss4[:, :], ss4[:, :], 0.0, None, ALU.max, None, ["ss4"], ["ss4"])
        TT(ss4[:, :], ss4[:, :], wq[:rows, :], ALU.mult, ["ss4", "wq"], ["ss4"])
        RED(ss1[:, :], ss4[:, :], ALU.add, ["ss4"], ["ss1"])
        TS(ssb[:, :], oh[:, :], ss1[:, 0:1], None, ALU.mult, None, ["oh", "ss1"], ["ssb"])
        MM(psf[7][:, 0:8], aall[:, :], ssb[:, :], True, True, ["aall", "ssb"], [pk(7)])
        TS(SC[:, :, 128], psf[7][:, 0:8], negfill[:, 0:1], None, ALU.add, None, [pk(7), "negfill"], ["SC"])
        MEMSET(bar[:, :], 0.0, PTK + ["pTa", "pTb", "bar"])
        for i in range(8):
            for c4 in range(4):
                gp = (i * 4 + c4) % 2
                P.op("pool", lambda e, i=i, c4=c4, gp=gp: e.indirect_dma_start(
                    out=KIc2[gp].rearrange("p s d -> p (s d)"), out_offset=None,
                    in_=cache_kidx[:, :],
                    in_offset=bass.IndirectOffsetOnAxis(ap=idx4[:, i, c4:c4 + 1], axis=0)),
                    reads=["idx4"], writes=KIK[gp], dma=True)
                for s8 in range(2):
                    for m in range(8):
                        TR(psb(5)[:, m * 128:(m + 1) * 128],
                           KIc2[gp][:, s8 * 16 + 2 * m:s8 * 16 + 2 * m + 2, :].rearrange("p s d -> p (s d)"), ident[:],
                           KIK[gp] + ["ident"], [pk(5)])
                    CP(kiTc.rearrange("p a b -> p (a b)"), psb(5)[:, :], [pk(5)], ["msk"], eng="act_copy")
                    for m in range(8):
                        sl = c4 * 32 + s8 * 16 + 2 * m
                        pbk = 0 if sl < 64 else 1
                        MM(psf[pbk][:, (sl % 64) * 8:(sl % 64) * 8 + 16], kiTc[:, m, :],
                           qiTs[:, i, :, :, :].rearrange("p s h b -> p (s h b)"), True, True,
                           ["msk", "qiTs"], [pk(pbk)])
            for b2 in range(2):
                ps_ = slice(64 * b2, 64 * b2 + 64)
                for hb in range(2):
                    src = psf[hb][ps_, :].rearrange("p (s h b) -> p s h b", h=4, b=2)[:, :, :, b2]
                    STT(T4[ps_, hb * 64:(hb + 1) * 64, :], src, 0.0, bc(wB[ps_, i, :].unsqueeze(1), [64, 64, 4]),
                        ALU.max, ALU.mult, [pk(hb), "wB"], ["rl"])
                RED(SC[ps_, i, 0:128], T4[ps_, :, :], ALU.add, ["rl"], ["SC"])
        RED(Mx[:, :], SC[:, :, 0:128], ALU.max, ["SC"], ["Mx"], absval=True)
        P.op("pe", lambda e: e.transpose(out=psf[2][0:8, 0:128], in_=Mx[:, :], identity=identf[:]),
             reads=["Mx", "identf"], writes=[pk(2)])
        RED(MxT[:, :], psf[2][0:8, 0:128], ALU.max, [pk(2)], ["MxT"])
        TS(MxT[:, :], MxT[:, :], 1.0, None, ALU.add, None, ["MxT"], ["MxT"])
        TS(Mdiag[:, :], identf[0:8, 0:8], MxT[:, 0:1], None, ALU.mult, None, ["identf", "MxT"], ["Mdiag"])
        MM(psf[2][:, 256:264], onesf[0:8, :], Mdiag[:, :], True, True, ["onesf", "Mdiag"], [pk(2)])
        TS(lo[:, :], psf[2][:, 256:264], -1.0, None, ALU.mult, None, [pk(2)], ["lo"])
        TS(w0[:, :], psf[2][:, 256:264], 2.0, None, ALU.mult, None, [pk(2)], ["w0"])
        for it in range(NBIS):
            TS(wdt[:, :], w0[:, :], 0.5 ** (it + 1), None, ALU.mult, None, ["w0"], ["wdt"])
            TT(mid[:, :], lo[:, :], wdt[:, :], ALU.add, ["lo", "wdt"], ["mid"])
            TT(cmpS[:, :, :], SC[:, :, :], bc(mid[:, :].unsqueeze(2), [128, 8, 129]), ALU.is_ge, ["SC", "mid"], ["sc"])
            RED(cnt[:, :], cmpS[:, :, :], ALU.add, ["sc"], ["cnt"])
            MM(psf[3][:, 0:8], bo[:, :], cnt[:, :], True, True, ["bo", "cnt"], [pk(3)])
            STT(tsel[:, :], psf[3][:, 0:8], TOPK - 0.5, wdt[:, :], ALU.is_ge, ALU.mult, [pk(3), "wdt"], ["tsel"])
            TT(lo[:, :], lo[:, :], tsel[:, :], ALU.add, ["lo", "tsel"], ["lo"])
        TT(mskS[:, :, :], SC[:, :, :], bc(lo[:, :].unsqueeze(2), [128, 8, 129]), ALU.is_ge, ["SC", "lo"], ["mskS"])
        MEMSET(bar[:, :], 0.0, ["sc", "sca", "scb", "bar"])
        MEMSET(QZ[:, :, :], 0.0, ["QZ"])
        CP(QZ[:, 0:4, 0:64], rq[:rows, 0:8:2, :], ["rq"], ["QZ"])
        CP(QZ[:, 4:8, 64:128], rq[:rows, 1:8:2, :], ["rq"], ["QZ"])
        for h in range(8):
            TR(psb(5)[:, h * 16:(h + 1) * 16], QZ[:, h, :], ident[:rows, :rows], ["QZ", "ident"], [pk(5)])
        CP(Qblk[:].rearrange("p i h b -> p h i b"), psb(5)[:, 0:128].rearrange("p (h i b) -> p h i b", h=8, i=8), [pk(5)], ["Qblk"])
        TR(psb(6)[:, 0:16], rq[:rows, 8:10, :].rearrange("p h d -> p (h d)"), ident[:rows, :rows], ["rq", "ident"], [pk(6)])
        CP(kTs[:, :], psb(6)[:, 0:16], [pk(6)], ["kTs"])
        MEMSET(KTself[:, :, :], 0.0, ["KTself"])
        CP(KTself[:, :, 0:128:64], kTs[:, :].rearrange("p (i b) -> p i b", b=2), ["kTs"], ["KTself"])
        CP(vsbf[:, :], vf[:rows, :], ["vf"], ["vsbf"])
        for i in range(8):
            TS(vfm[:, :], vf[:rows, :], oh[:, i:i + 1], None, ALU.mult, None, ["vf", "oh"], ["vfm"])
            MM(psf[7][:, 0:128], aall[:, :], vfm[:, :], True, True, ["aall", "vfm"], [pk(7)])
            CP(Vself[:, i, :], psf[7][:, 0:128], [pk(7)], ["Vself"])
        MEMSET(Ps[:, :, :, :], 0.0, ["Ps"])
        for i in range(8):
            MEMSET(Pacc[:, :], 0.0, ["Pacc"])
            for c4 in range(9):
                ns = 16 if c4 < 8 else 1
                if c4 < 8:
                    gp = (i * 8 + c4) % 2
                    P.op("pool", lambda e, i=i, c4=c4, gp=gp: e.indirect_dma_start(
                        out=Kc2[gp].rearrange("p s d -> p (s d)"), out_offset=None,
                        in_=cache_k[:, :],
                        in_offset=bass.IndirectOffsetOnAxis(ap=idx8[:, i, c4:c4 + 1], axis=0)),
                        reads=["idx8"], writes=KCK[gp], dma=True)
                    P.op("pool", lambda e, i=i, c4=c4, gp=gp: e.indirect_dma_start(
                        out=Vc2[gp].rearrange("p s d -> p (s d)"), out_offset=None,
                        in_=cache_v[:, :],
                        in_offset=bass.IndirectOffsetOnAxis(ap=idx8[:, i, c4:c4 + 1], axis=0)),
                        reads=["idx8"], writes=VK[gp], dma=True)
                    for s8 in range(2):
                        for s in range(8):
                            TR(psb(5)[:, s * 128:(s + 1) * 128], Kc2[gp][:, s8 * 8 + s, :], ident[:], KCK[gp] + ["ident"], [pk(5)])
                        CP(KTc.rearrange("p a b -> p (a b)"), psb(5)[:, :], [pk(5)], ["msk"], eng="act_copy")
                        for s in range(8):
                            sl = s8 * 8 + s
                            MM(psf[0][:, sl * 16:(sl + 1) * 16], KTc[:, s, :],
                               Qblk[:, i, :, :].rearrange("p h b -> p (h b)"), True, True,
                               ["msk", "Qblk"], [pk(0)])
                else:
                    MM(psf[0][:, 0:16], KTself[:, i, :], Qblk[:, i, :, :].rearrange("p h b -> p (h b)"),
                       True, True, ["KTself", "Qblk"], [pk(0)])
                for b2 in range(2):
                    ps_ = slice(64 * b2, 64 * b2 + 64)
                    src = psf[0][ps_, 0:ns * 16].rearrange("p (s h b) -> p s h b", h=8, b=2)[:, :, :, b2]
                    ACT(T4[ps_, 0:ns * 2, :].rearrange("p (s a) c -> p s (a c)", a=2), src, AF.Exp, [pk(0)], ["rl"], scale=0.125)
                    mk = mskS[ps_, i, c4 * 16:c4 * 16 + ns]
                    TT(Ps[ps_, 0:ns, b2, :], T4[ps_, 0:ns * 2, :].rearrange("p (s a) c -> p s (a c)", a=2),
                       bc(mk.unsqueeze(2), [64, ns, 8]), ALU.mult, ["rl", "mskS"], ["Ps"])
                RED(Ptmp[:, :], Ps[:, 0:ns, :, :].rearrange("p s b h -> p (b h) s"), ALU.add, ["Ps"], ["Ptmp"])
                TT(Pacc[:, :], Pacc[:, :], Ptmp[:, :], ALU.add, ["Pacc", "Ptmp"], ["Pacc"])
                for s in range(ns):
                    rhs = Vc2[gp][:, s, :] if c4 < 8 else Vself[:, i, :]
                    MM(psf[1][0:16, 0:128], Ps[:, s, :, :].rearrange("p b h -> p (b h)"), rhs,
                       (c4 == 0 and s == 0), (c4 == 8), ["Ps", "Vself"] + (VK[gp] if c4 < 8 else []), [pk(1)])
            MM(psf[1][0:16, 128:129], Pacc[:, :], onesf[:, 0:1], True, True, ["Pacc", "onesf"], [pk(1)])
            P.op("dve", lambda e: e.reciprocal(out=den[:, :], in_=psf[1][0:16, 128:129]), reads=[pk(1)], writes=["den"])
            TS(Osel[:, :], psf[1][0:16, 0:64], hm[:, 0:1], None, ALU.mult, None, [pk(1), "hm"], ["Osel"])
            STT(Osel[:, :], psf[1][0:16, 64:128], hm[:, 1:2], Osel[:, :], ALU.mult, ALU.add, [pk(1), "hm", "Osel"], ["Osel"])
            TS(Osel[:, :], Osel[:, :], den[:, 0:1], None, ALU.mult, None, ["Osel", "den"], ["Osel"])
            TT(Oexp[:, :, :], bc(Osel[:, :].unsqueeze(1), [16, 8, 64]), bc(hm[:, 2:10].unsqueeze(2), [16, 8, 64]),
               ALU.mult, ["Osel", "hm"], ["Oexp"])
            MM(psf[2][0:16, 0:512], selp[:, i, :], Oexp[:, :, :].rearrange("p h d -> p (h d)"), i == 0, i == 7,
               ["selp", "Oexp"], [pk(2)])
        CP(mix[:rows, 512:1024], psf[2][0:16, 0:512], [pk(2)], ["mix"])
        MEMSET(bar[:, :], 0.0, PTK + ["pTa", "pTb", "bar", "sc", "sca", "scb"])
        out_proj_ln2(t, rows, c0)

    def load_w_in_out():
        wv = w_in_d.rearrange("(k p) n -> p k n", p=128)
        for (a0, a1) in ((0, 1024), (1024, 2048), (2048, INW)):
            DMA("pool", w_in[:, :, a0:a1], wv[:, :, a0:a1], [], ["w_in"] + RKEYS)
        DMA("pool", w_out, w_out_d.rearrange("(k p) n -> p k n", p=128), [], ["w_out"] + RKEYS)

    if debug:
        DMA("sp", dbg_tab[:, 0:512], cosT[:].rearrange("p a b -> p (a b)"), ["Tcos"], [])
        DMA("sp", dbg_tab[:, 512:1024], sinT[:].rearrange("p a b -> p (a b)"), ["Tsin"], [])
        DMA("sp", dbg_tab[:, 1024:1056], cosS[:].rearrange("p a b -> p (a b)"), ["Scos"], [])
        DMA("sp", dbg_tab[:, 1056:1088], sinS[:].rearrange("p a b -> p (a b)"), ["Ssin"], [])
    ngroups = NSEQ * (16 // GT)
    if ngroups_run is not None:
        ngroups = ngroups_run
    for gi in range(ngroups):
        seq = gi // (16 // GT)
        j0 = (gi % (16 // GT)) * GT
        tiles = [(t, 128, t * 128) for t in range(GT)]
        has_s = (gi == 0)
        if has_s:
            tiles.append((GT, NSMP, GT * 128))
        ntok = GT * 128 + (NSMP if has_s else 0)
        for (t, rows, c0) in tiles:
            if t < GT:
                tok0 = seq * SEQ + (j0 + t) * 128
                DMA("sp", x_res[:, t, :], x_prompt[tok0:tok0 + 128, :], [], ["xr%d" % t])
            else:
                DMA("sp", x_res[:NSMP, t, :], x_sample[:, :], [], ["xr%d" % t])
            to_xT(t, rows, c0, 4 if (t % 2 == 0) else 6)
        ffn(0, tiles, ntok, "ln1_g", "ln1_b", None)
        if debug and gi == 0:
            DMA("sp", dbg_ln1[:, :], x_res[:].rearrange("p a b -> p (a b)"), ["xr%d" % t for t in range(GT + 1)], [])
        load_w_in_out()
        load_ln("ln2_g", "ln2_b")
        for (t, rows, c0) in tiles:
            if t < GT:
                mixer_prompt(t, c0, seq, j0 + t)
            else:
                flush_xT()
                mixer_sample(t, c0)
            if debug and gi == 0:
                DMA("sp", dbg_mix[:, t * D:(t + 1) * D], mix[:, :], ["mix"], [])
                DMA("sp", dbg_sc[:, t * SEQ:(t + 1) * SEQ], sc[:, :], ["sc"], [])
                DMA("sp", dbg_msk[:, t * SEQ:(t + 1) * SEQ], msk[:, :], ["msk"], [])
        flush_xT()
        if debug and gi == 0:
            DMA("sp", dbg_ln2[:, :], x_res[:].rearrange("p a b -> p (a b)"), ["xr%d" % t for t in range(GT + 1)], [])

        def final(t, rows, seq=seq, j0=j0):
            if t < GT:
                tok0 = seq * SEQ + (j0 + t) * 128
                DMA("sp", y_prompt[tok0:tok0 + 128, :], x_res[:, t, :], ["xr%d" % t], [])
            else:
                DMA("sp", y_sample[:, :], x_res[:NSMP, t, :], ["xr%d" % t], [])
        ffn(1, tiles, ntok, "ln3_g", "ln3_b", final)

    P.emit(nc, es, None)
    es.close()
    return nc


_NC = None


def _consts():
    half = 32
    invf = (10000.0 ** (-np.arange(half, dtype=np.float32) / half)).astype(np.float32)
    c_invf = np.tile(invf[None, :], (128, 1)).astype(np.float32)
    p = np.arange(128)
    c_bo = (p[:, None] // 64 == p[None, :] // 64).astype(np.float32)
    b = np.arange(16)
    aall = np.zeros((16, 128), np.float32)
    rep = np.zeros((16, 128), np.float32)
    selp = np.zeros((16, 8, 16), np.float32)
    for bb in range(16):
        aall[bb, 64 * (bb % 2)] = 1.0
        rep[bb, 64 * (bb % 2):64 * (bb % 2) + 64] = 1.0
    for i in range(8):
        for r in range(16):
            selp[r, i, 2 * i + r // 8] = 1.0
    oh = (b[:, None] // 2 == np.arange(8)[None, :]).astype(np.float32)
    negfill = np.full((128, 1), -BIG, np.float32)
    negfill[0, 0] = 0.0
    negfill[64, 0] = 0.0
    hm = np.zeros((16, 10), np.float32)
    for r in range(16):
        h = r % 8
        hm[r, 0] = 1.0 if h < 4 else 0.0
        hm[r, 1] = 0.0 if h < 4 else 1.0
        hm[r, 2 + h] = 1.0
    return dict(c_invf=c_invf, c_bo=c_bo, c_aall=aall, c_rep=rep, c_oh=oh,
                c_negfill=negfill, c_selp=selp.reshape(16, -1), c_hm=hm)


def kernel(x_prompt, x_sample, cache_k, cache_v, cache_kidx, page_table, ln1_g, ln1_b, ffn1_w_up, ffn1_w_down,
           ln2_g, ln2_b, w_in, a_ln_g, a_ln_b, a_ws, a_bs, w_out, ln3_g, ln3_b, ffn2_w_up, ffn2_w_down):
    global _NC
    if _NC is None:
        _NC = build_nc()
    nc = _NC
    f = lambda a: np.ascontiguousarray(np.asarray(a))
    consts = _consts()
    ck = f(cache_k).reshape(10240 * 8, 2048)
    cv = f(cache_v).reshape(10240 * 8, 2048)
    cki = f(cache_kidx).reshape(10240 * 4, 2048)
    shared = dict(
        cache_k=ck, cache_v=cv, cache_kidx=cki,
        ln1_g=f(ln1_g).reshape(1, D), ln1_b=f(ln1_b).reshape(1, D), ln2_g=f(ln2_g).reshape(1, D),
        ln2_b=f(ln2_b).reshape(1, D), ln3_g=f(ln3_g).reshape(1, D), ln3_b=f(ln3_b).reshape(1, D),
        a_ln_g=f(a_ln_g).reshape(1, 512), a_ln_b=f(a_ln_b).reshape(1, 512),
        ffn1_w_up=f(ffn1_w_up).reshape(D, 2 * FF), ffn2_w_up=f(ffn2_w_up).reshape(D, 2 * FF),
        ffn1_w_down=f(ffn1_w_down).reshape(FF, D), ffn2_w_down=f(ffn2_w_down).reshape(FF, D),
        w_in=f(w_in).reshape(D, INW), w_out=f(w_out).reshape(D, D),
        a_ws=f(a_ws).reshape(4, 128, 128), a_bs=f(a_bs).reshape(4, 128), **consts)
    xp = f(x_prompt)
    xs = f(x_sample).reshape(128, D)
    pt = f(page_table).astype(np.int32)
    in_maps = []
    for c in range(8):
        m = dict(shared)
        m["x_prompt"] = xp[2 * c:2 * c + 2].reshape(NSEQ * SEQ, D)
        m["x_sample"] = xs[16 * c:16 * c + 16]
        m["page_table"] = pt[16 * c:16 * c + 16].reshape(NSMP * NPAGE, 1)
        in_maps.append(m)
    res = run_bass_kernel_spmd(nc, in_maps, core_ids=list(range(8)))
    R = res.results
    cat = lambda n: np.concatenate([r[n] for r in R], axis=0)
    y_p = cat("y_prompt").reshape(16, SEQ, D)
    y_s = cat("y_sample").reshape(128, 1, D)
    nk_p = cat("nk_p").reshape(1, 16, SEQ, 2, 64)
    nv_p = cat("nv_p").reshape(1, 16, SEQ, 2, 64)
    nki_p = cat("nki_p").reshape(1, 16, SEQ, 64)
    nk_s = cat("nk_s").reshape(1, 128, 1, 2, 64)
    nv_s = cat("nv_s").reshape(1, 128, 1, 2, 64)
    nki_s = cat("nki_s").reshape(1, 128, 1, 64)
    av_s = cat("av_s").reshape(1, 128, 1, 512)
    return (y_p, y_s, nk_p, nv_p, nki_p, nk_s, nv_s, nki_s, av_s)
```
